# Optimizing a Trainium2 kernel written in Bass

```python
import jax, jax.numpy as jnp
from jax import lax
import numpy as np

D_MODEL = 1024
BATCH = 8
SEQ = 2048
DEPTH = 1
DEC_BATCH = 128
DEC_SEQ = 1
PAST_LEN = 16384
PAGE_SIZE = 128

RET_HEADS = 4
RET_DK = 128
RET_DV = 128
GDN_HEADS = 4
GDN_DK = 128
GDN_DV = 128
CONV_W = 4
D_FF = 2816
CHUNK = 64
ROPE_BASE = 10000.0
EPS = 1e-6

RET_QK = RET_HEADS * RET_DK
RET_V = RET_HEADS * RET_DV
GDN_QK = GDN_HEADS * GDN_DK
GDN_V = GDN_HEADS * GDN_DV
CONV_DIM = 2 * GDN_QK + GDN_V
SPLITS = (RET_QK, RET_QK, RET_V, RET_V, CONV_DIM, GDN_V, GDN_HEADS, GDN_HEADS, D_MODEL, D_MODEL)
D_IN = sum(SPLITS)

kernel_name = "hybrid_retention_gdn_macaron_step"


def rmsnorm(x, g):
    xf = x.astype(jnp.float32)
    y = xf * lax.rsqrt(jnp.mean(xf * xf, axis=-1, keepdims=True) + EPS) * g.astype(jnp.float32)
    return y.astype(x.dtype)


def swiglu(x, w_gate, w_up, w_down):
    return (jax.nn.silu(x @ w_gate) * (x @ w_up)) @ w_down


def rope(x, pos):
    d = x.shape[-1]
    inv = ROPE_BASE ** (-jnp.arange(0, d, 2, dtype=jnp.float32) / d)
    ang = pos[:, None] * inv[None, :]
    cos = jnp.cos(ang)[None, :, None, :]
    sin = jnp.sin(ang)[None, :, None, :]
    x1, x2 = x[..., : d // 2], x[..., d // 2:]
    return jnp.concatenate([x1 * cos - x2 * sin, x1 * sin + x2 * cos], axis=-1)


def chunk_len(t):
    return CHUNK if t % CHUNK == 0 else t


def to_chunks(x, c):
    b, t, h = x.shape[:3]
    x = x.reshape((b, t // c, c, h) + x.shape[3:])
    return jnp.moveaxis(x, (1, 3), (0, 2))


def from_chunks(x):
    x = jnp.moveaxis(x, (0, 2), (1, 3))
    b, n, c, h, d = x.shape
    return x.reshape(b, n * c, h, d)


def decay_masks(G):
    c = G.shape[-1]
    causal = jnp.tril(jnp.ones((c, c), dtype=bool))
    diff = G[..., :, None] - G[..., None, :]
    return jnp.where(causal, jnp.exp(jnp.where(causal, diff, 0.0)), 0.0)


def retention_chunked(q, k, v, g, s0):
    c = chunk_len(q.shape[1])
    qc, kc, vc, gc = to_chunks(q, c), to_chunks(k, c), to_chunks(v, c), to_chunks(g, c)
    G = jnp.cumsum(gc, axis=-1)
    dm = decay_masks(G)
    scores = jnp.einsum('nbhid,nbhjd->nbhij', qc, kc) * dm
    o_intra = jnp.einsum('nbhij,nbhje->nbhie', scores, vc)

    def step(s, inp):
        q_, k_, v_, G_ = inp
        o = jnp.exp(G_)[..., None] * jnp.einsum('bhid,bhde->bhie', q_, s)
        gl = G_[..., -1:]
        s = jnp.exp(gl)[..., None] * s + jnp.einsum('bhjd,bhje->bhde', k_ * jnp.exp(gl - G_)[..., None], v_)
        return s, o

    s_final, o_cross = lax.scan(step, s0, (qc, kc, vc, G))
    return from_chunks(o_intra + o_cross), s_final


def gated_delta_chunked(q, k, v, g, beta, s0):
    c = chunk_len(q.shape[1])
    qc, kc, vc = to_chunks(q, c), to_chunks(k, c), to_chunks(v, c)
    gc, bc = to_chunks(g, c), to_chunks(beta, c)
    G = jnp.cumsum(gc, axis=-1)
    dm = decay_masks(G)
    strict = jnp.tril(jnp.ones((c, c), dtype=bool), k=-1)
    kk = jnp.einsum('nbhid,nbhjd->nbhij', kc, kc)
    a_mat = jnp.eye(c, dtype=jnp.float32) + jnp.where(strict, bc[..., :, None] * dm * kk, 0.0)
    u_v = lax.linalg.triangular_solve(a_mat, bc[..., None] * vc, left_side=True, lower=True, unit_diagonal=True)
    w_k = lax.linalg.triangular_solve(a_mat, (bc * jnp.exp(G))[..., None] * kc,
                                      left_side=True, lower=True, unit_diagonal=True)
    qk = jnp.einsum('nbhid,nbhjd->nbhij', qc, kc) * dm

    def step(s, inp):
        q_, k_, uv_, wk_, qk_, G_ = inp
        u = uv_ - jnp.einsum('bhid,bhde->bhie', wk_, s)
        o = jnp.exp(G_)[..., None] * jnp.einsum('bhid,bhde->bhie', q_, s) + jnp.einsum('bhij,bhje->bhie', qk_, u)
        gl = G_[..., -1:]
        s = jnp.exp(gl)[..., None] * s + jnp.einsum('bhjd,bhje->bhde', k_ * jnp.exp(gl - G_)[..., None], u)
        return s, o

    s_final, o = lax.scan(step, s0, (qc, kc, u_v, w_k, qk, G))
    return from_chunks(o), s_final


def short_conv(u, buf, w):
    t = u.shape[1]
    full = jnp.concatenate([buf, u], axis=1)
    out = full[:, 0:t] * w[0]
    for i in range(1, CONV_W):
        out = out + full[:, i:i + t] * w[i]
    return jax.nn.silu(out), full[:, -(CONV_W - 1):]


def token_mixer(h, pos, s_ret, s_gdn, s_conv, w_in, ret_norm_g, gdn_conv_w, gdn_a_log, gdn_dt_bias,
                gdn_norm_g, w_ret_branch, w_gdn_branch, w_out):
    f32 = jnp.float32
    b, t, _ = h.shape
    proj = (h @ w_in).astype(f32)
    idx = tuple(int(i) for i in np.cumsum(SPLITS)[:-1])
    rq, rk, rv, rg, conv_in, gz, ga, gb, gate_r, gate_g = jnp.split(proj, idx, axis=-1)

    q = rope(rq.reshape(b, t, RET_HEADS, RET_DK), pos)
    k = rope(rk.reshape(b, t, RET_HEADS, RET_DK), pos) * (RET_DK ** -0.5)
    v = rv.reshape(b, t, RET_HEADS, RET_DV)
    log_gamma = jnp.log(1.0 - 2.0 ** (-5.0 - jnp.arange(RET_HEADS, dtype=f32)))
    g_ret = jnp.broadcast_to(log_gamma, (b, t, RET_HEADS))
    o_r, s_ret_new = retention_chunked(q, k, v, g_ret, s_ret.astype(f32))
    mu = jnp.mean(o_r, axis=-1, keepdims=True)
    var = jnp.mean(jnp.square(o_r - mu), axis=-1, keepdims=True)
    o_r = (o_r - mu) * lax.rsqrt(var + EPS)
    o_r = o_r.reshape(b, t, RET_V) * ret_norm_g.astype(f32)
    o_r = jax.nn.silu(rg) * o_r

    cq, s_conv_new = short_conv(conv_in, s_conv.astype(f32), gdn_conv_w.astype(f32))
    gq, gk, gv = jnp.split(cq, (GDN_QK, 2 * GDN_QK), axis=-1)
    gq = gq.reshape(b, t, GDN_HEADS, GDN_DK)
    gk = gk.reshape(b, t, GDN_HEADS, GDN_DK)
    gv = gv.reshape(b, t, GDN_HEADS, GDN_DV)
    gq = gq * lax.rsqrt(jnp.sum(gq * gq, axis=-1, keepdims=True) + EPS) * (GDN_DK ** -0.5)
    gk = gk * lax.rsqrt(jnp.sum(gk * gk, axis=-1, keepdims=True) + EPS)
    beta = jax.nn.sigmoid(gb)
    g_gdn = -jnp.exp(gdn_a_log.astype(f32)) * jax.nn.softplus(ga + gdn_dt_bias.astype(f32))
    o_g, s_gdn_new = gated_delta_chunked(gq, gk, gv, g_gdn, beta, s_gdn.astype(f32))
    o_g = o_g * lax.rsqrt(jnp.mean(o_g * o_g, axis=-1, keepdims=True) + EPS) * gdn_norm_g.astype(f32)
    o_g = o_g.reshape(b, t, GDN_V) * jax.nn.silu(gz)

    y = (jax.nn.sigmoid(gate_r) * (o_r @ w_ret_branch) + jax.nn.sigmoid(gate_g) * (o_g @ w_gdn_branch))
    y = (y @ w_out).astype(h.dtype)
    return y, s_ret_new, s_gdn_new, s_conv_new


def layer(x, pos, s_ret, s_gdn, s_conv, ffn1_pre_g, ffn1_post_g, ffn1_w_gate, ffn1_w_up, ffn1_w_down,
          mix_pre_g, mix_post_g, w_in, ret_norm_g, gdn_conv_w, gdn_a_log, gdn_dt_bias, gdn_norm_g,
          w_ret_branch, w_gdn_branch, w_out, ffn2_pre_g, ffn2_post_g, ffn2_w_gate, ffn2_w_up, ffn2_w_down):
    x = x + 0.5 * rmsnorm(swiglu(rmsnorm(x, ffn1_pre_g), ffn1_w_gate, ffn1_w_up, ffn1_w_down), ffn1_post_g)
    m, s_ret_new, s_gdn_new, s_conv_new = token_mixer(
        rmsnorm(x, mix_pre_g), pos, s_ret, s_gdn, s_conv, w_in, ret_norm_g, gdn_conv_w, gdn_a_log,
        gdn_dt_bias, gdn_norm_g, w_ret_branch, w_gdn_branch, w_out)
    x = x + rmsnorm(m, mix_post_g)
    x = x + 0.5 * rmsnorm(swiglu(rmsnorm(x, ffn2_pre_g), ffn2_w_gate, ffn2_w_up, ffn2_w_down), ffn2_post_g)
    return x, s_ret_new, s_gdn_new, s_conv_new


def setup_inputs(seed: int = 0) -> dict:
    key = jax.random.key(seed)
    ks = iter(jax.random.split(key, 40))
    f32 = jnp.float32
    L = DEPTH

    def nrm(shape, scale):
        return scale * jax.random.normal(next(ks), shape, f32)

    def gain(n):
        return 1.0 + nrm((L, n), 0.02)

    return {
        "x_prompt": nrm((BATCH, SEQ, D_MODEL), 1.0),
        "x_sample": nrm((DEC_BATCH, DEC_SEQ, D_MODEL), 1.0),
        "state_ret": nrm((L, DEC_BATCH, RET_HEADS, RET_DK, RET_DV), 0.1),
        "state_gdn": nrm((L, DEC_BATCH, GDN_HEADS, GDN_DK, GDN_DV), 0.1),
        "state_conv": nrm((L, DEC_BATCH, CONV_W - 1, CONV_DIM), 1.0),
        "ffn1_pre_g": gain(D_MODEL),
        "ffn1_post_g": gain(D_MODEL),
        "ffn1_w_gate": nrm((L, D_MODEL, D_FF), D_MODEL ** -0.5),
        "ffn1_w_up": nrm((L, D_MODEL, D_FF), D_MODEL ** -0.5),
        "ffn1_w_down": nrm((L, D_FF, D_MODEL), D_FF ** -0.5),
        "mix_pre_g": gain(D_MODEL),
        "mix_post_g": gain(D_MODEL),
        "w_in": nrm((L, D_MODEL, D_IN), D_MODEL ** -0.5),
        "ret_norm_g": gain(RET_V),
        "gdn_conv_w": nrm((L, CONV_W, CONV_DIM), CONV_W ** -0.5),
        "gdn_a_log": jnp.log(jax.random.uniform(next(ks), (L, GDN_HEADS), f32, 1.0, 16.0)),
        "gdn_dt_bias": nrm((L, GDN_HEADS), 0.5),
        "gdn_norm_g": gain(GDN_DV),
        "w_ret_branch": nrm((L, RET_V, D_MODEL), RET_V ** -0.5),
        "w_gdn_branch": nrm((L, GDN_V, D_MODEL), GDN_V ** -0.5),
        "w_out": nrm((L, D_MODEL, D_MODEL), D_MODEL ** -0.5),
        "ffn2_pre_g": gain(D_MODEL),
        "ffn2_post_g": gain(D_MODEL),
        "ffn2_w_gate": nrm((L, D_MODEL, D_FF), D_MODEL ** -0.5),
        "ffn2_w_up": nrm((L, D_MODEL, D_FF), D_MODEL ** -0.5),
        "ffn2_w_down": nrm((L, D_FF, D_MODEL), D_FF ** -0.5),
    }


def reference(x_prompt, x_sample, state_ret, state_gdn, state_conv, ffn1_pre_g, ffn1_post_g, ffn1_w_gate,
              ffn1_w_up, ffn1_w_down, mix_pre_g, mix_post_g, w_in, ret_norm_g, gdn_conv_w, gdn_a_log,
              gdn_dt_bias, gdn_norm_g, w_ret_branch, w_gdn_branch, w_out, ffn2_pre_g, ffn2_post_g,
              ffn2_w_gate, ffn2_w_up, ffn2_w_down):
    f32 = jnp.float32
    pos_p = jnp.arange(SEQ, dtype=f32)
    pos_s = PAST_LEN + jnp.arange(DEC_SEQ, dtype=f32)
    yp, ys = x_prompt, x_sample
    rp, gp, cp, rs, gs, cs = [], [], [], [], [], []
    for l in range(DEPTH):
        w = (ffn1_pre_g[l], ffn1_post_g[l], ffn1_w_gate[l], ffn1_w_up[l], ffn1_w_down[l],
             mix_pre_g[l], mix_post_g[l], w_in[l], ret_norm_g[l], gdn_conv_w[l], gdn_a_log[l],
             gdn_dt_bias[l], gdn_norm_g[l], w_ret_branch[l], w_gdn_branch[l], w_out[l],
             ffn2_pre_g[l], ffn2_post_g[l], ffn2_w_gate[l], ffn2_w_up[l], ffn2_w_down[l])
        z_ret = jnp.zeros((BATCH, RET_HEADS, RET_DK, RET_DV), f32)
        z_gdn = jnp.zeros((BATCH, GDN_HEADS, GDN_DK, GDN_DV), f32)
        z_conv = jnp.zeros((BATCH, CONV_W - 1, CONV_DIM), f32)
        yp, r1, g1, c1 = layer(yp, pos_p, z_ret, z_gdn, z_conv, *w)
        ys, r2, g2, c2 = layer(ys, pos_s, state_ret[l], state_gdn[l], state_conv[l], *w)
        rp.append(r1); gp.append(g1); cp.append(c1)
        rs.append(r2); gs.append(g2); cs.append(c2)
    new_ret_p = jnp.stack(rp)
    new_gdn_p = jnp.stack(gp)
    new_conv_p = jnp.stack(cp)
    new_ret_s = jnp.stack(rs)
    new_gdn_s = jnp.stack(gs)
    new_conv_s = jnp.stack(cs)
    return (yp, ys, new_ret_p, new_gdn_p, new_conv_p, new_ret_s, new_gdn_s, new_conv_s)
```

```python
import numpy as np
from contextlib import ExitStack
import concourse.bass as bass
import concourse.mybir as mybir
from concourse.bass_utils import run_bass_kernel_spmd

F32 = mybir.dt.float32
BF16 = mybir.dt.bfloat16
AF = mybir.ActivationFunctionType
ALU = mybir.AluOpType
AX = mybir.AxisListType

ENGS = ("pe", "act", "dve", "pool", "sp")

D = 1024
DFF = 2816
NJ = 22
DIN = 6152
SEQ = 2048
NS = 16
EPS = 1e-6
GAMMA = [1.0 - 2.0 ** (-5.0 - h) for h in range(4)]
DK = 128
NCF = 7


class H:
    __slots__ = ("t", "key", "rot", "idx", "gen")

    def __init__(self, t, key, rot, idx, gen):
        self.t, self.key, self.rot, self.idx, self.gen = t, key, rot, idx, gen


def _res(keys):
    out = []
    for k in keys:
        if isinstance(k, H):
            assert k.rot.gen[k.idx] == k.gen, "stale scratch tile %s" % (k.key,)
            out.append(k.key)
        else:
            out.append(k)
    return out


class _Rec:
    def __init__(self):
        self.call = None

    def __getattr__(self, name):
        if name.startswith("__"):
            raise AttributeError(name)

        def f(*a, **kw):
            self.call = (name, a, kw)
            return self
        return f

    def then_inc(self, *a, **k):
        return self


def _free_size(ap):
    try:
        n = 1
        for d in ap.shape[1:]:
            n *= int(d)
        return n
    except Exception:
        return 256


SCHED = True
PSUM_EXCL = True
HYBRID_K = None
TABLE_AWARE = False
REGION_FLUSH = False
KEEP_ORDER = set()
SCHED_REGIONS = {"ffn1", "ffn2", "G2", "M", "R"}


class Ctx:
    NDSEM = 28

    def __init__(self, nc, es, same_eng_sync=("act", "dve", "pool")):
        self.nc = nc
        self.es = es
        self.same_eng_sync = same_eng_sync
        self.q = {e: [] for e in ENGS}
        self.cnt = {e: 0 for e in ENGS}
        self.waited = {e: {} for e in ENGS}
        self.sems = {}
        self.EPOCH = 2000
        for i in range(self.NDSEM):
            self.sems[("d", i)] = es.enter_context(nc.semaphore("s_d%d" % i))
        self.ndma = 0
        self.res = {}
        self.out_events = []
        self.rec = SCHED
        self.nodes = []

    def sb(self, name, shape, dt):
        return self.es.enter_context(self.nc.sbuf_tensor(name, list(shape), dt))

    def ps(self, name, shape, dt):
        return self.es.enter_context(self.nc.psum_tensor(name, list(shape), dt))

    def _deps(self, reads, writes, eng=None):
        deps = []
        for r in reads:
            st = self.res.get(r)
            if st and st["w"]:
                deps.append(st["w"])
            if st and PSUM_EXCL and isinstance(r, tuple) and r[0] == "pb":
                for ev in st["r"].items():
                    if ev[0][0] != eng:
                        deps.append(ev)
        for w in writes:
            st = self.res.get(w)
            if st:
                if st["w"]:
                    deps.append(st["w"])
                deps.extend(st["r"].items())
        return deps

    def _emit_waits(self, eng, deps):
        wd = self.waited[eng]
        best = {}
        for (k, v) in deps:
            if k[0] == eng and (eng == "pe" or eng not in self.same_eng_sync):
                continue
            if wd.get(k, 0) < v and best.get(k, 0) < v:
                best[k] = v
        for k, v in best.items():
            wd[k] = v
            self._eo(eng).wait_ge(self.sems[k], v)

    def _update(self, ev, reads, writes):
        for r in reads:
            st = self.res.setdefault(r, {"w": None, "r": {}})
            if st["r"].get(ev[0], 0) < ev[1]:
                st["r"][ev[0]] = ev[1]
        for w in writes:
            self.res[w] = {"w": ev, "r": {}}

    def fence(self, new_keys, old_keys):
        if self.rec:
            self.nodes.append(("fence", list(new_keys), list(old_keys)))
            return
        self._fence(new_keys, old_keys)

    def _fence(self, new_keys, old_keys):
        acc = {}
        for ok in old_keys:
            st = self.res.get(ok)
            if not st:
                continue
            evs = list(st["r"].items())
            if st["w"]:
                evs.append(st["w"])
            for k, v in evs:
                if acc.get(k, 0) < v:
                    acc[k] = v
        for nk in new_keys:
            st = self.res.setdefault(nk, {"w": None, "r": {}})
            for k, v in acc.items():
                if st["r"].get(k, 0) < v:
                    st["r"][k] = v

    def op(self, eng, fn, reads=(), writes=()):
        reads, writes = _res(reads), _res(writes)
        if self.rec:
            r = _Rec()
            fn(r)
            self.nodes.append(("op", eng, r.call, reads, writes))
            return None
        return self._op(eng, fn, reads, writes)

    def _op(self, eng, fn, reads, writes):
        deps = self._deps(reads, writes, eng)
        self._emit_waits(eng, deps)
        self.cnt[eng] += 1
        n = self.cnt[eng] - 1
        sk = (eng, n // self.EPOCH)
        if sk not in self.sems:
            self.sems[sk] = self.es.enter_context(self.nc.semaphore("s_%s_%d" % sk))
        ev = (sk, n % self.EPOCH + 1)
        fn(self._eo(eng)).then_inc(self.sems[sk], 1)
        self._update(ev, reads, writes)
        return ev

    def dma(self, queue, fn, reads=(), writes=(), is_output=False):
        reads, writes = _res(reads), _res(writes)
        if self.rec:
            r = _Rec()
            fn(r)
            self.nodes.append(("dma", queue, r.call, reads, writes, is_output))
            return None
        return self._dma(queue, fn, reads, writes, is_output)

    def _dma(self, queue, fn, reads, writes, is_output):
        n = self.ndma
        self.ndma += 1
        k = ("d", n % self.NDSEM)
        rnd = n // self.NDSEM
        deps = self._deps(reads, writes)
        if rnd > 0:
            deps.append((k, 16 * rnd))
        self._emit_waits(queue, deps)
        ev = (k, 16 * (rnd + 1))
        fn(self._eo(queue)).then_inc(self.sems[k], 16)
        self._update(ev, reads, writes)
        if is_output:
            self.out_events.append(ev)
        return ev

    def _eo(self, e):
        nc = self.nc
        return {"pe": nc.tensor, "act": nc.scalar, "dve": nc.vector, "pool": nc.gpsimd, "sp": nc.sync}[e]

    def region(self, name):
        if not REGION_FLUSH:
            return
        self.flush()
        self.rec = SCHED and (SCHED_REGIONS is None or name in SCHED_REGIONS)

    def finish(self, eng="sp"):
        self.flush()
        self._emit_waits(eng, self.out_events)

    def _est(self, nd):
        kind, eng, call = nd[0], nd[1], nd[2]
        name, a, kw = call
        if kind == "dma":
            out = kw.get("out", a[0] if a else None)
            nbytes = _free_size(out) * int(out.shape[0]) * 4 if out is not None else 65536
            lat = 2.0 + nbytes / 250e3
            return (1.0 if eng == "pool" else 0.06), lat
        if eng == "pe":
            if name == "transpose":
                return 0.1, 0.1
            rhs = kw.get("rhs")
            n = _free_size(rhs) if rhs is not None else 128
            d = 0.04 + n / 2600.0
            if rhs is not None and rhs.dtype == F32:
                d *= 4
            return d, d
        out = kw.get("out", a[0] if a else None)
        n = _free_size(out) if out is not None else 256
        if eng == "act":
            d = 0.30 + n / 1200.0
        elif eng == "dve":
            d = 0.16 + n / 1200.0
        else:
            d = 0.25 + n / 550.0
        return d, d

    def flush(self):
        import heapq
        nodes, self.nodes = self.nodes, []
        if not nodes:
            return
        self.rec = False
        N = len(nodes)
        lastw, readers = {}, {}
        preds = [None] * N
        succs = [[] for _ in range(N)]
        for idx, nd in enumerate(nodes):
            if nd[0] == "fence":
                rd, wr = (), list(nd[1]) + list(nd[2])
            else:
                rd, wr = nd[3], nd[4]
            p = set()
            for r in rd:
                w = lastw.get(r)
                if w is not None:
                    p.add(w)
                if PSUM_EXCL and isinstance(r, tuple) and r[0] == "pb":
                    for q in readers.get(r, ()):
                        if nodes[q][1] != nd[1]:
                            p.add(q)
            for w_ in wr:
                w = lastw.get(w_)
                if w is not None:
                    p.add(w)
                rs = readers.get(w_)
                if rs:
                    p.update(rs)
            p.discard(idx)
            preds[idx] = p
            for q in p:
                succs[q].append(idx)
            for r in rd:
                readers.setdefault(r, set()).add(idx)
            for w_ in wr:
                lastw[w_] = idx
                readers[w_] = set()
        if KEEP_ORDER:
            last_e = {}
            for idx, nd in enumerate(nodes):
                if nd[0] == "fence":
                    continue
                e = nd[1] if nd[0] == "op" else "dma_" + nd[1]
                if e in KEEP_ORDER:
                    q = last_e.get(e)
                    if q is not None and q not in preds[idx]:
                        preds[idx].add(q)
                        succs[q].append(idx)
                    last_e[e] = idx
        occ = [0.0] * N
        lat = [0.0] * N
        engs = [None] * N
        aset = [0] * N
        for i, nd in enumerate(nodes):
            if nd[0] != "fence":
                occ[i], lat[i] = self._est(nd)
                engs[i] = nd[1]
                if nd[0] == "op" and nd[1] == "act":
                    f_ = nd[2][2].get("func")
                    if f_ == AF.Silu:
                        aset[i] = 1
                    elif f_ == AF.Exp or f_ == AF.Ln:
                        aset[i] = 2
                    elif f_ == AF.Sigmoid:
                        aset[i] = 3
        cur_set = [0]
        TBL = 1.3
        blev = [0.0] * N
        for i in range(N - 1, -1, -1):
            b = 0.0
            for q in succs[i]:
                if blev[q] > b:
                    b = blev[q]
            blev[i] = b + lat[i]
        LAT = 0.15
        indeg = [len(p) for p in preds]
        finish = [0.0] * N
        eng_free = {e: 0.0 for e in ENGS}
        pending = {e: [] for e in ENGS}
        avail = {e: [] for e in ENGS}
        order = []

        def ready_time(i):
            t = 0.0
            for q in preds[i]:
                l = 0.0 if (engs[q] == "pe" and engs[i] == "pe") or engs[q] is None else LAT
                if finish[q] + l > t:
                    t = finish[q] + l
            return t

        def release(i):
            stack = [i]
            while stack:
                j = stack.pop()
                for q in succs[j]:
                    indeg[q] -= 1
                    if indeg[q] == 0:
                        rt = ready_time(q)
                        if engs[q] is None:
                            finish[q] = rt
                            order.append(q)
                            stack.append(q)
                        else:
                            heapq.heappush(pending[engs[q]], (rt, q))

        for i in range(N):
            if indeg[i] == 0:
                if engs[i] is None:
                    finish[i] = 0.0
                    order.append(i)
                    release(i)
                else:
                    heapq.heappush(pending[engs[i]], (0.0, i))
        nsched = sum(1 for e in engs if e is not None)
        done = 0
        while done < nsched:
            best = None
            for e in ENGS:
                pe_, av = pending[e], avail[e]
                while pe_ and pe_[0][0] <= eng_free[e]:
                    rt, q = heapq.heappop(pe_)
                    heapq.heappush(av, (-blev[q], q))
                if av:
                    if e == "act" and TABLE_AWARE:
                        pick = None
                        for it in heapq.nsmallest(8, av):
                            if aset[it[1]] == 0 or aset[it[1]] == cur_set[0]:
                                pick = it
                                break
                        if pick is None:
                            pick = av[0]
                            cand = (eng_free[e] + TBL, pick[0], pick[1], e, pick)
                        else:
                            cand = (eng_free[e], pick[0], pick[1], e, pick)
                    else:
                        cand = (eng_free[e], av[0][0], av[0][1], e, True)
                elif pe_:
                    cand = (pe_[0][0], -blev[pe_[0][1]], pe_[0][1], e, False)
                else:
                    continue
                if best is None or cand[:3] < best[:3]:
                    best = cand
            st, _, i, e, from_av = best
            if from_av is True:
                heapq.heappop(avail[e])
            elif from_av is False:
                heapq.heappop(pending[e])
            else:
                avail[e].remove(from_av)
                heapq.heapify(avail[e])
            if e == "act" and aset[i] != 0:
                if not from_av and aset[i] != cur_set[0] and TABLE_AWARE:
                    st += TBL
                cur_set[0] = aset[i]
            finish[i] = st + lat[i]
            eng_free[e] = st + occ[i]
            order.append(i)
            done += 1
            release(i)
        assert len(order) == N, (len(order), N)
        if HYBRID_K is not None:
            pre = order[:HYBRID_K]
            ps_ = set(pre)
            order = pre + [i for i in range(N) if i not in ps_]
            self.hyb_info = (N, [nodes[i][:2] + (nodes[i][2][0],) if nodes[i][0] != "fence" else ("fence",) for i in order[max(0, HYBRID_K - 3):HYBRID_K + 1]])
        self.sched_makespan = getattr(self, "sched_makespan", 0.0) + (max(finish) if finish else 0.0)
        for i in order:
            nd = nodes[i]
            if nd[0] == "fence":
                self._fence(nd[1], nd[2])
            elif nd[0] == "op":
                name, a, kw = nd[2]
                self._op(nd[1], lambda e, name=name, a=a, kw=kw: getattr(e, name)(*a, **kw), nd[3], nd[4])
            else:
                name, a, kw = nd[2]
                self._dma(nd[1], lambda e, name=name, a=a, kw=kw: getattr(e, name)(*a, **kw), nd[3], nd[4], nd[5])
        self.rec = True

    def replay(self, e, engobj):
        for item in self.q[e]:
            if item[0] == "wait":
                engobj.wait_ge(self.sems[item[1]], item[2])
            elif item[0] == "op":
                item[1](engobj).then_inc(self.sems[e], 1)
            else:
                item[1](engobj).then_inc(self.sems[item[2]], 16)

    def run_block(self):
        return
        nc = self.nc
        with nc.Block() as block:
            @block.sync
            def _(e):
                self.replay("sp", e)

            @block.tensor
            def _(e):
                self.replay("pe", e)

            @block.scalar
            def _(e):
                self.replay("act", e)

            @block.vector
            def _(e):
                self.replay("dve", e)

            @block.gpsimd
            def _(e):
                self.replay("pool", e)


class Rot:
    def __init__(self, c, name, n, shape, dt):
        self.t = [c.sb("%s%d" % (name, i), shape, dt) for i in range(n)]
        self.name = name
        self.gen = [0] * n
        self.i = 0

    def get(self):
        i = self.i
        self.i = (i + 1) % len(self.t)
        self.gen[i] += 1
        return H(self.t[i], (self.name, i), self, i, self.gen[i])


DBG_SKIP = set()


def build(stage=99, debug=False):
    nc = bass.Bass("TRN2", target_bir_lowering=False)
    SK = DBG_SKIP
    G2CUT = 99
    for f_ in SK:
        if f_.startswith('cut'):
            G2CUT = int(f_[3:])

    def din(name, shape):
        return nc.dram_tensor(name, list(shape), F32, kind="ExternalInput").ap()

    def dout(name, shape):
        return nc.dram_tensor(name, list(shape), F32, kind="ExternalOutput").ap()

    xin = din("xin", [SEQ + NS, D])
    sret_in = din("sret", [NS, 4, 128, 128])
    sgdn_in = din("sgdn", [NS, 4, 128, 128])
    sconv_in = din("sconv", [NS, 3, 1536])
    W = {}
    for nm, shp in [("ffn1_w_gate", [D, DFF]), ("ffn1_w_up", [D, DFF]), ("ffn1_w_down", [DFF, D]),
                    ("w_in", [D, DIN]), ("w_ret_branch", [512, D]), ("w_gdn_branch", [512, D]),
                    ("w_out", [D, D]),
                    ("ffn2_w_gate", [D, DFF]), ("ffn2_w_up", [D, DFF]), ("ffn2_w_down", [DFF, D])]:
        W[nm] = din(nm, shp)
    w_in = W["w_in"]
    gpre_in = din("gpre", [128, 24])
    gpost_in = din("gpost", [3, D])
    rng_in = din("ret_norm_g", [1, 512])
    gng_in = din("gdn_norm_g", [1, 128])
    convw_in = din("convw", [128, 12, 4])
    alog_in = din("a_log", [1, 4])
    dtb_in = din("dt_bias", [1, 4])
    cs_in = din("cossin", [128, 17, 3, 64])
    cf_in = din("constf", [128, NCF, 128])
    rt_in = din("rettab", [128, 2, 4, 128])
    kdec_in = din("kdec", [128, 4])

    y_out = dout("y", [SEQ + NS, D])
    nsr_p = dout("nsr_p", [4, 128, 128])
    nsg_p = dout("nsg_p", [4, 128, 128])
    nsc_p = dout("nsc_p", [3, 1536])
    nsr_s = dout("nsr_s", [NS, 4, 128, 128])
    nsg_s = dout("nsg_s", [NS, 4, 128, 128])
    nsc_s = dout("nsc_s", [NS, 3, 1536])

    with ExitStack() as es:
        c = Ctx(nc, es)
        es.enter_context(nc.allow_non_contiguous_dma(reason="tiny strided state outputs"))
        NTB = 7
        CB = 784
        SC0 = 768
        SI = 6
        x1 = c.sb("x1", [128, NTB, D], F32)
        hT = c.sb("hT", [128, 8, CB], BF16)
        S0 = c.sb("S0", [128, NJ * CB], BF16)
        aT = S0[:, :].rearrange("p (j t) -> p j t", j=NJ)
        o_rT = S0[:, 0:4 * CB].rearrange("p (h t) -> p h t", h=4)
        o_gT = S0[:, 4 * CB:8 * CB].rearrange("p (h t) -> p h t", h=4)
        cqT = S0[:, 8 * CB:20 * CB].rearrange("p (s t) -> p s t", s=12)
        yT = S0[:, 8 * CB:16 * CB].rearrange("p (k t) -> p k t", k=8)
        NSLOT = 7
        ring = [c.sb("ring%d" % i, [128, 8, 512], BF16) for i in range(NSLOT)]
        wsmall = c.sb("wsmall", [128, 8, 8], BF16)
        gp = c.sb("gp", [128, D], F32)
        gpre = c.sb("gpre_sb", [128, 24], F32)
        cs_t = c.sb("cs_t", [128, NTB, 3, 64], F32)
        cf = c.sb("cf", [128, NCF, 128], F32)
        cb = c.sb("cb", [128, NCF, 128], BF16)
        rettab = c.sb("rettab_sb", [128, 2, 4, 128], F32)
        kdec = c.sb("kdec_sb", [128, 4], F32)
        rng_t = c.sb("rng_t", [128, 512], F32)
        gng_t = c.sb("gng_t", [128, 128], F32)
        convw = c.sb("convw_sb", [128, 12, 4], F32)
        alog_t = c.sb("alog_t", [128, 4], F32)
        dtb_t = c.sb("dtb_t", [128, 4], F32)
        nega_t = c.sb("nega_t", [128, 4], F32)
        nhalf = c.sb("nhalf", [128, 4], F32)
        epst = c.sb("epst", [128, 4], F32)
        Sret = c.sb("Sret", [128, 4, 128], F32)
        Sretb = c.sb("Sretb", [128, 4, 128], BF16)
        Sgdn = c.sb("Sgdn", [128, 4, 128], F32)
        Sgdnb = c.sb("Sgdnb", [128, 4, 128], BF16)
        car = c.sb("car", [128, 12, 3], F32)
        scb = c.sb("scb", [128, 12, 3, NS], F32)
        gsc = c.sb("gsc", [128, NTB, 12], F32)
        s_qm = c.sb("s_qm", [128, 4, NS, NS], BF16)
        s_km = c.sb("s_km", [128, 4, NS, NS], BF16)
        s_os = c.sb("s_os", [128, 512], F32)
        s_vs = c.sb("s_vs", [128, 512], F32)
        s_ks = c.sb("s_ks", [128, 512], BF16)
        s_abc = c.sb("s_abc", [128, NS, 4], F32)
        s_gate = c.sb("s_gate", [128, 512], BF16)
        rF = Rot(c, "rF", 9, [128, 520], F32)
        rB = Rot(c, "rB", 12, [128, 512], BF16)
        rBB = Rot(c, "rBB", 2, [128, 1024], BF16)
        rS = Rot(c, "rS", 24, [128, 16], F32)
        pb = [c.ps("pb%d" % i, [128, 512], F32) for i in range(8)]
        pbb = [p.bitcast(BF16) for p in pb]
        bank_i = [0]
        bank_gen = [0] * 8

        class BK(int):
            pass

        def bank():
            b = bank_i[0]
            bank_i[0] = (b + 1) % 8
            bank_gen[b] += 1
            r = BK(b)
            r.gen = bank_gen[b]
            return r

        IDF, TRI, SU, MASKS, ONES, DI0, DI1 = 0, 1, 2, 3, 4, 5, 6
        def PB(b):
            assert bank_gen[int(b)] == b.gen, "stale psum bank %d" % int(b)
            return ("pb", int(b))

        def v4(ap):
            return ap.rearrange("p (h e) -> p h e", h=4)

        def hb(h):
            return slice(h * 128, (h + 1) * 128)

        def ld(dst, src, key):
            c.dma("sp", lambda e: e.dma_start(out=dst, in_=src), writes=[key])

        ld(gpre[:], gpre_in, "gpre")
        ld(cf[:], cf_in, "cf")
        ld(rettab[:], rt_in, "rettab")
        ld(kdec[:], kdec_in, "kdec")
        ld(convw[:], convw_in, "convw")
        ld(rng_t[:], rng_in.partition_broadcast(128), "rng")
        ld(gng_t[:], gng_in.partition_broadcast(128), "gng")
        ld(alog_t[:], alog_in.partition_broadcast(128), "alog")
        ld(dtb_t[:], dtb_in.partition_broadcast(128), "dtb")
        c.op("dve", lambda e: e.tensor_copy(out=cb[:], in_=cf[:]), reads=["cf"], writes=["cb"])
        c.op("pool", lambda e: e.memset(nhalf[:], -0.5), writes=["nhalf"])
        c.op("pool", lambda e: e.memset(epst[:], EPS), writes=["epst"])
        c.op("act", lambda e: e.activation(out=nega_t[:], in_=alog_t[:], func=AF.Exp), reads=["alog"], writes=["nega"])
        c.op("dve", lambda e: e.tensor_scalar(out=nega_t[:], in0=nega_t[:], scalar1=-1.0, scalar2=None, op0=ALU.mult),
             reads=["nega"], writes=["nega"])
        for t_, k_ in ((Sret, "Sret"), (Sgdn, "Sgdn"), (Sretb, "Sretb"), (Sgdnb, "Sgdnb"), (car, "car")):
            c.op("pool", lambda e, t_=t_: e.memset(t_[:], 0.0), writes=[k_])
        identb = cb[:, IDF, :]

        wstate = {"n": 0}

        wload_after = [()]

        def wload(src, r0, kc, c0, ncols):
            s = wstate["n"] % NSLOT
            wstate["n"] += 1
            t, key = ring[s], ("ring", s)
            v = src[r0 * 128:(r0 + kc) * 128, c0:c0 + ncols].rearrange("(c p) n -> p c n", p=128)
            c.dma("pool", lambda e: e.dma_start(out=t[:, 0:kc, 0:ncols], in_=v), reads=list(wload_after[0]), writes=[key])
            return t, key

        def mm8(b, nrows_out, ncols, lhsT_fn, rhs_fn, reads, nk=8):
            for k in range(nk):
                lt, rt_ = lhsT_fn(k), rhs_fn(k)
                c.op("pe", lambda e, k=k, lt=lt, rt_=rt_: e.matmul(pb[b][:nrows_out, 0:ncols], lhsT=lt, rhs=rt_,
                                                                  start=(k == 0), stop=(k == nk - 1)),
                     reads=reads, writes=[PB(b)])

        def prenormA(i, nrows, c0, gidx):
            kx = ("x1", i)
            jt = rBB.get()
            ss = rS.get()
            c.op("act", lambda e: e.activation(out=jt.t[:nrows, :], in_=x1[:nrows, i, :], func=AF.Square,
                                               accum_out=ss.t[:nrows, 0:1]), reads=[kx], writes=[jt, ss])
            c.op("dve", lambda e: e.tensor_scalar(out=ss.t[:nrows, 1:2], in0=ss.t[:nrows, 0:1], scalar1=1.0 / D,
                                                  scalar2=EPS, op0=ALU.mult, op1=ALU.add), reads=[ss], writes=[ss])
            c.op("pool", lambda e: e.tensor_tensor(out=ss.t[:nrows, 2:3], in0=ss.t[:nrows, 1:2], in1=nhalf[:nrows, 0:1],
                                                   op=ALU.pow), reads=[ss, "nhalf"], writes=[ss])
            hn = rBB.get()
            c.op("dve", lambda e: e.tensor_scalar(out=hn.t[:nrows, :], in0=x1[:nrows, i, :], scalar1=ss.t[:nrows, 2:3],
                                                  scalar2=None, op0=ALU.mult), reads=[kx, ss], writes=[hn])

            def partB():
                b = bank()
                for k in range(8):
                    c.op("pe", lambda e, k=k: e.transpose(out=pbb[b][:, k * 128:k * 128 + nrows],
                                                          in_=hn.t[:nrows, k * 128:(k + 1) * 128],
                                                          identity=cb[:nrows, IDF, :nrows]),
                         reads=[hn, "cb"], writes=[PB(b)])
                src = pbb[b][:, :].rearrange("p (k t) -> p k t", k=8)[:, :, 0:nrows]
                gb = gpre[:, gidx * 8:(gidx + 1) * 8].unsqueeze(2).broadcast_to([128, 8, nrows])
                c.op("dve", lambda e: e.tensor_tensor(out=hT[:, :, c0:c0 + nrows], in0=src, in1=gb, op=ALU.mult),
                     reads=[PB(b), "gpre"], writes=[("hT", i)])
            return partB

        def prenorm(i, nrows, c0, gidx):
            prenormA(i, nrows, c0, gidx)()

        def load_gp(row):
            c.dma("sp", lambda e: e.dma_start(out=gp[:], in_=gpost_in[row:row + 1, :].partition_broadcast(128)),
                  writes=["gp"])

        def post(i, nrows, b0, b1, scale, final_row0=None):
            ss = rS.get()
            jt = rBB.get()
            c.op("act", lambda e: e.activation(out=jt.t[:nrows, 0:512], in_=pb[b0][:nrows, :], func=AF.Square,
                                               accum_out=ss.t[:nrows, 0:1]), reads=[PB(b0)], writes=[jt, ss])
            c.op("act", lambda e: e.activation(out=jt.t[:nrows, 512:1024], in_=pb[b1][:nrows, :], func=AF.Square,
                                               accum_out=ss.t[:nrows, 1:2]), reads=[PB(b1)], writes=[jt, ss])
            c.op("dve", lambda e: e.tensor_tensor(out=ss.t[:nrows, 2:3], in0=ss.t[:nrows, 0:1], in1=ss.t[:nrows, 1:2],
                                                  op=ALU.add), reads=[ss], writes=[ss])
            m = 1.0 / (scale * scale)
            c.op("dve", lambda e: e.tensor_scalar(out=ss.t[:nrows, 3:4], in0=ss.t[:nrows, 2:3], scalar1=m / D,
                                                  scalar2=EPS * m, op0=ALU.mult, op1=ALU.add), reads=[ss], writes=[ss])
            c.op("pool", lambda e: e.tensor_tensor(out=ss.t[:nrows, 4:5], in0=ss.t[:nrows, 3:4], in1=nhalf[:nrows, 0:1],
                                                   op=ALU.pow), reads=[ss, "nhalf"], writes=[ss])
            for hf, bb in ((0, b0), (1, b1)):
                t = rF.get()
                c.op("dve", lambda e, t=t, bb=bb, hf=hf: e.scalar_tensor_tensor(
                    out=t.t[:nrows, 0:512], in0=pb[bb][:nrows, :], scalar=ss.t[:nrows, 4:5], op0=ALU.mult,
                    in1=gp[:nrows, hf * 512:(hf + 1) * 512], op1=ALU.mult),
                    reads=[PB(bb), ss, "gp"], writes=[t])
                c.op("pool", lambda e, t=t, hf=hf: e.tensor_tensor(
                    out=x1[:nrows, i, hf * 512:(hf + 1) * 512], in0=x1[:nrows, i, hf * 512:(hf + 1) * 512],
                    in1=t.t[:nrows, 0:512], op=ALU.add), reads=[t, ("x1", i)], writes=[("x1", i)])
            if final_row0 is not None:
                c.dma("sp", lambda e: e.dma_start(out=y_out[final_row0:final_row0 + nrows, :], in_=x1[:nrows, i, :]),
                      reads=[("x1", i)], is_output=True)

        s0_ffn_keys = set()
        s0_mix_keys = set()

        def ffn(tiles, cgs, wg, wu, wd, gpost_row, final, after_tile=None):
            load_gp(gpost_row)
            ntl = (NJ + 3) // 4
            loaded = {}

            def load_gu(g):
                nj = min(4, NJ - g * 4)
                loaded[g] = (wload(wg, 0, 8, g * 512, nj * 128), wload(wu, 0, 8, g * 512, nj * 128), nj)

            load_gu(0)
            dts = []
            dspecs = [(hf, r0, kc) for hf in range(2) for (r0, kc) in ((0, 8), (8, 8), (16, 6))]
            for g in range(ntl):
                if g + 1 < ntl:
                    load_gu(g + 1)
                else:
                    for (hf, r0, kc) in dspecs[:NSLOT - 2]:
                        dts.append(wload(wd, r0, kc, hf * 512, 512))
                (gt, gk), (ut, uk), nj = loaded.pop(g)
                for jj in range(nj):
                    j = g * 4 + jj
                    bgs = [bank() for _ in cgs]
                    bus = [bank() for _ in cgs]
                    for (wt, wk, bks) in ((gt, gk, bgs), (ut, uk, bus)):
                        for k in range(8):
                            for ci, (c0, n, tl) in enumerate(cgs):
                                b = bks[ci]
                                c.op("pe", lambda e, k=k, b=b, c0=c0, n=n, wt=wt: e.matmul(
                                    pb[b][:, 0:n], lhsT=wt[:, k, jj * 128:(jj + 1) * 128], rhs=hT[:, k, c0:c0 + n],
                                    start=(k == 0), stop=(k == 7)),
                                    reads=[wk] + [("hT", t) for t in tl], writes=[PB(b)])
                    for ci, (c0, n, tl) in enumerate(cgs):
                        bg, bu = bgs[ci], bus[ci]
                        sg = rF.get()
                        c.op("act", lambda e, sg=sg, bg=bg, n=n: e.activation(out=sg.t[:, 0:n], in_=pb[bg][:, 0:n],
                                                                          func=AF.Silu),
                             reads=[PB(bg)], writes=[sg])
                        s0_ffn_keys.add(("aT", j, c0))
                        c.op("dve", lambda e, sg=sg, bu=bu, n=n, c0=c0, j=j: e.tensor_tensor(
                            out=aT[:, j, c0:c0 + n], in0=sg.t[:, 0:n], in1=pb[bu][:, 0:n], op=ALU.mult),
                            reads=[sg, PB(bu)], writes=[("aT", j, c0)])
            for (hf, r0, kc) in dspecs[NSLOT - 2:]:
                dts.append(wload(wd, r0, kc, hf * 512, 512))
            pend = [None]
            for (i, nrows, c0, row0) in tiles:
                bs = []
                cgc0 = [cc for (cc, n, tl) in cgs if cc <= c0 < cc + n][0]
                for hf in range(2):
                    b = bank()
                    bs.append(b)
                    for j in range(NJ):
                        dt_, dk_ = dts[hf * 3 + j // 8]
                        c.op("pe", lambda e, j=j, b=b, dt_=dt_: e.matmul(
                            pb[b][:nrows, :], lhsT=aT[:, j, c0:c0 + nrows], rhs=dt_[:, j % 8, :],
                            start=(j == 0), stop=(j == NJ - 1)),
                            reads=[dk_, ("aT", j, cgc0)], writes=[PB(b)])
                if pend[0] is not None:
                    pend[0]()
                    pend[0] = None
                post(i, nrows, bs[0], bs[1], 0.5, final_row0=(row0 if final else None))
                if after_tile is not None:
                    pend[0] = after_tile(i, nrows, c0)
            if pend[0] is not None:
                pend[0]()

        def rope(b, i, nrows, out_ap, out_key):
            q4 = pb[b][:nrows, :].rearrange("p (h two d) -> p h two d", h=4, two=2)
            t1 = rF.get()
            t1v = t1.t[:nrows, 0:512].rearrange("p (h two d) -> p h two d", h=4, two=2)
            cosb = cs_t[:nrows, i, 0, :].unsqueeze(1).unsqueeze(1).broadcast_to([nrows, 4, 2, 64])
            c.op("dve", lambda e: e.tensor_tensor(out=t1v, in0=q4, in1=cosb, op=ALU.mult),
                 reads=[PB(b), "cs_t"], writes=[t1])
            u = rF.get()
            uv = u.t[:nrows, 0:512].rearrange("p (h two d) -> p h two d", h=4, two=2)
            nsb = cs_t[:nrows, i, 2, :].unsqueeze(1).broadcast_to([nrows, 4, 64])
            sb_ = cs_t[:nrows, i, 1, :].unsqueeze(1).broadcast_to([nrows, 4, 64])
            c.op("dve", lambda e: e.tensor_tensor(out=uv[:, :, 0, :], in0=q4[:, :, 1, :], in1=nsb, op=ALU.mult),
                 reads=[PB(b), "cs_t"], writes=[u])
            c.op("dve", lambda e: e.tensor_tensor(out=uv[:, :, 1, :], in0=q4[:, :, 0, :], in1=sb_, op=ALU.mult),
                 reads=[PB(b), "cs_t", u], writes=[u])
            c.op("pool", lambda e: e.tensor_tensor(out=out_ap, in0=t1.t[:nrows, 0:512], in1=u.t[:nrows, 0:512],
                                                   op=ALU.add), reads=[t1, u], writes=[out_key])

        def ret_proj(i, nrows, c0, Ts):
            bs = []
            for (T, kT) in Ts:
                b = bank()
                bs.append(b)
                mm8(b, nrows, 512, lambda k: hT[:, k, c0:c0 + nrows], lambda k, T=T: T[:, k, :], [kT, ("hT", i)])
            return bs

        def onorm_tail(i, nrows, c0, on, gate, dstT, dkey, gtab, gkey, gbc, defer=False):
            if gbc:
                gi = gtab[:nrows, :].unsqueeze(1).broadcast_to([nrows, 4, 128])
                c.op("pool", lambda e: e.tensor_tensor(out=v4(on.t[:nrows, 0:512]), in0=v4(on.t[:nrows, 0:512]), in1=gi,
                                                       op=ALU.mult), reads=[on, gkey], writes=[on])
            else:
                c.op("pool", lambda e: e.tensor_tensor(out=on.t[:nrows, 0:512], in0=on.t[:nrows, 0:512],
                                                       in1=gtab[:nrows, :], op=ALU.mult), reads=[on, gkey], writes=[on])
            orn = rB.get()
            g_ap = s_gate[:nrows, :] if gate is None else gate.t[:nrows, :]
            g_k = "s_gate" if gate is None else gate
            c.op("pool", lambda e: e.tensor_tensor(out=orn.t[:nrows, :], in0=on.t[:nrows, 0:512], in1=g_ap,
                                                   op=ALU.mult), reads=[on, g_k], writes=[orn])
            s0_mix_keys.add((dkey, i))

            def partB():
                bt = bank()
                for h in range(4):
                    c.op("pe", lambda e, h=h: e.transpose(out=pbb[bt][:, h * 128:h * 128 + nrows], in_=orn.t[:nrows, hb(h)],
                                                          identity=cb[:nrows, IDF, :nrows]),
                         reads=[orn, "cb"], writes=[PB(bt)])
                src = pbb[bt][:, 0:512].rearrange("p (h t) -> p h t", h=4)[:, :, 0:nrows]
                c.op("act", lambda e: e.activation(out=dstT[:, :, c0:c0 + nrows], in_=src, func=AF.Copy),
                     reads=[PB(bt)], writes=[(dkey, i)])
            if defer:
                return partB
            partB()
            return None

        def groupnorm_ret(bo, nrows, src=None, skey=None):
            sm = rS.get()
            sm2 = rS.get()
            if src is None:
                src, skey = pb[bo][:nrows, :], PB(bo)
            c.op("dve", lambda e: e.tensor_reduce(out=sm.t[:nrows, 0:4], in_=v4(src), axis=AX.X, op=ALU.add),
                 reads=[skey], writes=[sm])
            sq = rF.get()
            c.op("act", lambda e: e.activation(out=sq.t[:nrows, 0:512], in_=src, func=AF.Square),
                 reads=[skey], writes=[sq])
            c.op("dve", lambda e: e.tensor_reduce(out=sm.t[:nrows, 4:8], in_=v4(sq.t[:nrows, 0:512]), axis=AX.X, op=ALU.add),
                 reads=[sq, sm], writes=[sm])
            c.op("dve", lambda e: e.tensor_scalar(out=sm.t[:nrows, 8:12], in0=sm.t[:nrows, 0:4], scalar1=1.0 / 128,
                                                  scalar2=None, op0=ALU.mult), reads=[sm], writes=[sm])
            c.op("dve", lambda e: e.tensor_tensor(out=sm.t[:nrows, 12:16], in0=sm.t[:nrows, 8:12], in1=sm.t[:nrows, 8:12],
                                                  op=ALU.mult), reads=[sm], writes=[sm])
            c.op("dve", lambda e: e.scalar_tensor_tensor(out=sm2.t[:nrows, 0:4], in0=sm.t[:nrows, 4:8], scalar=1.0 / 128,
                                                         op0=ALU.mult, in1=sm.t[:nrows, 12:16], op1=ALU.subtract),
                 reads=[sm], writes=[sm2])
            c.op("dve", lambda e: e.tensor_scalar(out=sm2.t[:nrows, 4:8], in0=sm2.t[:nrows, 0:4], scalar1=EPS, scalar2=None,
                                                  op0=ALU.add), reads=[sm2], writes=[sm2])
            c.op("pool", lambda e: e.tensor_tensor(out=sm2.t[:nrows, 8:12], in0=sm2.t[:nrows, 4:8], in1=nhalf[:nrows, 0:4],
                                                   op=ALU.pow), reads=[sm2, "nhalf"], writes=[sm2])
            on = rF.get()
            for h in range(4):
                c.op("dve", lambda e, h=h: e.tensor_scalar(out=on.t[:nrows, hb(h)], in0=src[:, hb(h)],
                                                           scalar1=sm.t[:nrows, 8 + h:9 + h], scalar2=sm2.t[:nrows, 8 + h:9 + h],
                                                           op0=ALU.subtract, op1=ALU.mult),
                     reads=[skey, sm, sm2, on] if h else [skey, sm, sm2], writes=[on])
            return on

        def rmsnorm_gdn(bo, nrows, src=None, skey=None):
            sq = rF.get()
            sm = rS.get()
            if src is None:
                src, skey = pb[bo][:nrows, :], PB(bo)
            c.op("act", lambda e: e.activation(out=sq.t[:nrows, 0:512], in_=src, func=AF.Square),
                 reads=[skey], writes=[sq])
            c.op("dve", lambda e: e.tensor_reduce(out=sm.t[:nrows, 0:4], in_=v4(sq.t[:nrows, 0:512]), axis=AX.X, op=ALU.add),
                 reads=[sq], writes=[sm])
            c.op("dve", lambda e: e.tensor_scalar(out=sm.t[:nrows, 4:8], in0=sm.t[:nrows, 0:4], scalar1=1.0 / 128,
                                                  scalar2=EPS, op0=ALU.mult, op1=ALU.add), reads=[sm], writes=[sm])
            c.op("pool", lambda e: e.tensor_tensor(out=sm.t[:nrows, 8:12], in0=sm.t[:nrows, 4:8], in1=nhalf[:nrows, 0:4],
                                                   op=ALU.pow), reads=[sm, "nhalf"], writes=[sm])
            on = rF.get()
            c.op("dve", lambda e: e.tensor_tensor(out=v4(on.t[:nrows, 0:512]), in0=v4(src),
                                                  in1=sm.t[:nrows, 8:12].unsqueeze(2).broadcast_to([nrows, 4, 128]),
                                                  op=ALU.mult), reads=[skey, sm], writes=[on])
            return on

        def phase_R(ptiles, with_sample, g1=None):
            Ts = [wload(w_in, 0, 8, cc, 512) for cc in (0, 512, 1024, 1536)]

            def pump(n, site="x"):
                if ("np_" + site) in SK:
                    return
                if g1 is not None:
                    for _ in range(n):
                        next(g1, None)
            pump(1)
            pendR = None
            for (i, nrows, c0, row0) in ptiles:
                bq, bk, bv, bg = ret_proj(i, 128, c0, Ts)
                if pendR is not None:
                    pendR()
                    pendR = None
                pump(1, "a")
                qr = rB.get()
                rope(bq, i, 128, qr.t[:, :], qr)
                kf = rF.get()
                rope(bk, i, 128, kf.t[:, 0:512], kf)
                krb = rB.get()
                c.op("act", lambda e: e.activation(out=krb.t[:, :], in_=kf.t[:, 0:512], func=AF.Copy), reads=[kf], writes=[krb])
                kd = rB.get()
                c.op("pool", lambda e: e.tensor_tensor(out=v4(kd.t[:, :]), in0=v4(kf.t[:, 0:512]),
                                                       in1=kdec[:, :].unsqueeze(2).broadcast_to([128, 4, 128]), op=ALU.mult),
                     reads=[kf, "kdec"], writes=[kd])
                vb = rB.get()
                c.op("act", lambda e: e.activation(out=vb.t[:, :], in_=pb[bv][:, :], func=AF.Copy), reads=[PB(bv)], writes=[vb])
                rgs = rB.get()
                c.op("act", lambda e: e.activation(out=rgs.t[:, :], in_=pb[bg][:, :], func=AF.Silu), reads=[PB(bg)], writes=[rgs])
                bt = bank()
                for h in range(4):
                    c.op("pe", lambda e, h=h: e.transpose(out=pbb[bt][:, hb(h)], in_=qr.t[:, hb(h)], identity=identb),
                         reads=[qr, "cb"], writes=[PB(bt)])
                for h in range(4):
                    c.op("pe", lambda e, h=h: e.transpose(out=pbb[bt][:, hb(4 + h)], in_=krb.t[:, hb(h)], identity=identb),
                         reads=[krb, "cb"], writes=[PB(bt)])
                qkT = rBB.get()
                c.op("act", lambda e: e.activation(out=qkT.t[:, :], in_=pbb[bt][:, 0:1024], func=AF.Copy),
                     reads=[PB(bt)], writes=[qkT])
                qg = rB.get()
                c.op("pool", lambda e: e.tensor_tensor(out=qg.t[:, :], in0=qkT.t[:, 0:512],
                                                       in1=rettab[:, 1, :, :].rearrange("p h i -> p (h i)"), op=ALU.mult),
                     reads=[qkT, "rettab"], writes=[qg])
                pump(5, "b")
                bsc = bank()
                for h in range(4):
                    c.op("pe", lambda e, h=h: e.matmul(pb[bsc][:, hb(h)], lhsT=qkT.t[:, hb(4 + h)], rhs=qkT.t[:, hb(h)],
                                                       start=True, stop=True), reads=[qkT], writes=[PB(bsc)])
                sT = rB.get()
                c.op("dve", lambda e: e.tensor_tensor(out=sT.t[:, :], in0=pb[bsc][:, :],
                                                      in1=rettab[:, 0, :, :].rearrange("p h i -> p (h i)"), op=ALU.mult),
                     reads=[PB(bsc), "rettab"], writes=[sT])
                bo = bank()
                for h in range(4):
                    c.op("pe", lambda e, h=h: e.matmul(pb[bo][:, hb(h)], lhsT=sT.t[:, hb(h)], rhs=vb.t[:, hb(h)],
                                                       start=True, stop=False), reads=[sT, vb], writes=[PB(bo)])
                    c.op("pe", lambda e, h=h: e.matmul(pb[bo][:, hb(h)], lhsT=qg.t[:, hb(h)], rhs=Sretb[:, h, :],
                                                       start=False, stop=True), reads=[qg, "Sretb"], writes=[PB(bo)])
                bS = bank()
                for h in range(4):
                    c.op("pe", lambda e, h=h: e.matmul(pb[bS][:, hb(h)], lhsT=kd.t[:, hb(h)], rhs=vb.t[:, hb(h)],
                                                       start=True, stop=True), reads=[kd, vb], writes=[PB(bS)])
                for h in range(4):
                    c.op("dve", lambda e, h=h: e.scalar_tensor_tensor(out=Sret[:, h, :], in0=Sret[:, h, :],
                                                                      scalar=float(GAMMA[h] ** 128), op0=ALU.mult,
                                                                      in1=pb[bS][:, hb(h)], op1=ALU.add),
                         reads=[PB(bS), "Sret"], writes=["Sret"])
                c.op("act", lambda e: e.activation(out=Sretb[:], in_=Sret[:], func=AF.Copy), reads=["Sret"], writes=["Sretb"])
                on = groupnorm_ret(bo, 128)
                pendR = onorm_tail(i, 128, c0, on, rgs, o_rT, "orT", rng_t, "rng", False, defer=True)
            if pendR is not None:
                pendR()
            if with_sample:
                c.region("Rs")
                sample_ret(Ts)
                c.region("R2")

        def build_masked(dst, dkey, srcT_fn, src_reads):
            di = cf[:, DI0:DI0 + 2, :].rearrange("p a (b m) -> p (a b) m", m=NS)
            for h in range(4):
                c.op("dve", lambda e, h=h: e.tensor_tensor(out=dst[:, h, :, :],
                                                           in0=srcT_fn(h).unsqueeze(1).broadcast_to([128, NS, NS]),
                                                           in1=di, op=ALU.mult),
                     reads=src_reads + ["cf"] + ([dkey] if h else []), writes=[dkey])

        def sample_state_update(h, tg, S0g, lhs_tok, ublk, a_scalar, a_bc, out_dram):
            bO = bank()
            c.op("pe", lambda e: e.matmul(pb[bO][:, :], lhsT=lhs_tok, rhs=ublk.t[:NS, :], start=True, stop=True),
                 reads=[ublk, "s_ks"], writes=[PB(bO)])
            if a_scalar is not None:
                c.op("dve", lambda e: e.scalar_tensor_tensor(out=S0g.t[:, 0:512], in0=S0g.t[:, 0:512], scalar=a_scalar,
                                                             op0=ALU.mult, in1=pb[bO][:, :], op1=ALU.add),
                     reads=[S0g, PB(bO)], writes=[S0g])
            else:
                c.op("dve", lambda e: e.tensor_tensor(out=v4(S0g.t[:, 0:512]), in0=v4(S0g.t[:, 0:512]), in1=a_bc,
                                                      op=ALU.mult), reads=[S0g, "s_abc"], writes=[S0g])
                c.op("dve", lambda e: e.tensor_tensor(out=S0g.t[:, 0:512], in0=S0g.t[:, 0:512], in1=pb[bO][:, :],
                                                      op=ALU.add), reads=[S0g, PB(bO)], writes=[S0g])
            c.dma("sp", lambda e: e.dma_start(out=out_dram[tg * 4:(tg + 1) * 4, h].rearrange("t d e -> d t e"),
                                              in_=v4(S0g.t[:, 0:512])), reads=[S0g], is_output=True)
            Snb = rB.get()
            c.op("act", lambda e: e.activation(out=Snb.t[:, :], in_=S0g.t[:, 0:512], func=AF.Copy), reads=[S0g], writes=[Snb])
            return Snb

        def make_ublk(u_ap, u_reads, tg):
            ub = rB.get()
            c.op("dve", lambda e: e.tensor_tensor(out=v4(ub.t[:NS, :]), in0=u_ap.unsqueeze(1).broadcast_to([NS, 4, 128]),
                                                  in1=cf[:NS, IDF, tg * 4:(tg + 1) * 4].unsqueeze(2).broadcast_to([NS, 4, 128]),
                                                  op=ALU.mult), reads=u_reads + ["cf"], writes=[ub])
            return ub

        def sample_ret(Ts):
            i, c0 = SI, SC0
            bq, bk, bv, bg = ret_proj(i, NS, c0, Ts)
            qr = rB.get()
            rope(bq, i, NS, qr.t[:NS, :], qr)
            kf = rF.get()
            rope(bk, i, NS, kf.t[:NS, 0:512], kf)
            c.op("act", lambda e: e.activation(out=s_ks[:NS, :], in_=kf.t[:NS, 0:512], func=AF.Copy, scale=float(DK ** -0.5)),
                 reads=[kf], writes=["s_ks"])
            c.op("act", lambda e: e.activation(out=s_vs[:NS, :], in_=pb[bv][:NS, :], func=AF.Copy), reads=[PB(bv)], writes=["s_vs"])
            c.op("act", lambda e: e.activation(out=s_gate[:NS, :], in_=pb[bg][:NS, :], func=AF.Silu), reads=[PB(bg)], writes=["s_gate"])
            bt = bank()
            for h in range(4):
                c.op("pe", lambda e, h=h: e.transpose(out=pbb[bt][:, h * NS:(h + 1) * NS], in_=qr.t[:NS, hb(h)],
                                                      identity=cb[:NS, IDF, :NS]), reads=[qr, "cb"], writes=[PB(bt)])
            qTs = rB.get()
            c.op("act", lambda e: e.activation(out=qTs.t[:, 0:4 * NS], in_=pbb[bt][:, 0:4 * NS], func=AF.Copy),
                 reads=[PB(bt)], writes=[qTs])
            build_masked(s_qm, "s_qm", lambda h: qTs.t[:, h * NS:(h + 1) * NS], [qTs])
            for h in range(4):
                snbs = []
                for tg in range(4):
                    S0g = rF.get()
                    c.dma("sp", lambda e, S0g=S0g, tg=tg: e.dma_start(
                        out=v4(S0g.t[:, 0:512]), in_=sret_in[tg * 4:(tg + 1) * 4, h].rearrange("t d e -> d t e")),
                        writes=[S0g])
                    ub = make_ublk(s_vs[:NS, hb(h)], ["s_vs"], tg)
                    snbs.append(sample_state_update(h, tg, S0g, s_ks[:NS, hb(h)], ub, float(GAMMA[h]), None, nsr_s))
                bQ = bank()
                for t in range(NS):
                    c.op("pe", lambda e, t=t: e.matmul(pb[bQ][:NS, 0:128], lhsT=s_qm[:, h, t, :],
                                                       rhs=snbs[t // 4].t[:, hb(t % 4)], start=(t == 0), stop=(t == NS - 1)),
                         reads=["s_qm", snbs[t // 4]], writes=[PB(bQ)])
                c.op("act", lambda e, h=h: e.activation(out=s_os[:NS, hb(h)], in_=pb[bQ][:NS, 0:128], func=AF.Copy),
                     reads=[PB(bQ), "s_os"], writes=["s_os"])
            on = groupnorm_ret(None, NS, s_os[:NS, :], "s_os")
            onorm_tail(i, NS, c0, on, None, o_rT, "orT", rng_t, "rng", False)

        def phase_G1(cgs_p, with_sample, last_block):
            g1w = [wload(w_in, 0, 8, 2048 + T * 512, 512) for T in range(3)]
            yield
            for T in range(3):
                Tt, kT = g1w[T]
                for cc in range(4):
                    s = T * 4 + cc
                    groups = [(c0, n, tl, False) for (c0, n, tl) in cgs_p]
                    if with_sample:
                        groups.append((SC0, NS, [SI], True))
                    for (c0, n, tl, is_s) in groups:
                        b = bank()
                        mm8(b, 128, n, lambda k: Tt[:, k, cc * 128:(cc + 1) * 128], lambda k: hT[:, k, c0:c0 + n],
                            [kT] + [("hT", t) for t in tl])
                        acc = rF.get()
                        if not is_s:
                            ub = rF.get()
                            c.op("pool", lambda e: e.tensor_copy(out=ub.t[:, 0:3], in_=car[:, s, :]), reads=["car"], writes=[ub])
                            c.op("act", lambda e: e.activation(out=ub.t[:, 3:3 + n], in_=pb[b][:, 0:n], func=AF.Copy),
                                 reads=[PB(b), ub], writes=[ub])
                            c.op("pool", lambda e: e.tensor_copy(out=car[:, s, :], in_=ub.t[:, n:n + 3]), reads=[ub, "car"],
                                 writes=["car"])
                            c.op("dve", lambda e: e.tensor_scalar(out=acc.t[:, 0:n], in0=ub.t[:, 3:3 + n], scalar1=convw[:, s, 3:4],
                                                                  scalar2=None, op0=ALU.mult), reads=[ub, "convw"], writes=[acc])
                            for tap in (2, 1, 0):
                                c.op("dve", lambda e, tap=tap: e.scalar_tensor_tensor(
                                    out=acc.t[:, 0:n], in0=ub.t[:, tap:tap + n], scalar=convw[:, s, tap:tap + 1], op0=ALU.mult,
                                    in1=acc.t[:, 0:n], op1=ALU.add), reads=[ub, "convw", acc], writes=[acc])
                        else:
                            us = rF.get()
                            c.op("pool", lambda e: e.memset(us.t[:, 0:128], 0.0), writes=[us])
                            c.op("act", lambda e: e.activation(out=us.t[:, 0:n], in_=pb[b][:, 0:n], func=AF.Copy),
                                 reads=[PB(b), us], writes=[us])
                            c.op("dve", lambda e: e.tensor_scalar(out=acc.t[:, 0:n], in0=us.t[:, 0:n], scalar1=convw[:, s, 3:4],
                                                                  scalar2=None, op0=ALU.mult), reads=[us, "convw"], writes=[acc])
                            for tap in (2, 1, 0):
                                c.op("dve", lambda e, tap=tap: e.scalar_tensor_tensor(
                                    out=acc.t[:, 0:n], in0=scb[:, s, tap, :], scalar=convw[:, s, tap:tap + 1], op0=ALU.mult,
                                    in1=acc.t[:, 0:n], op1=ALU.add), reads=["scb", "convw", acc], writes=[acc])
                            bt = bank()
                            c.op("pe", lambda e: e.transpose(out=pb[bt][:, 0:128], in_=us.t[:, 0:128], identity=cf[:, IDF, :]),
                                 reads=[us, "cf"], writes=[PB(bt)])
                            ut = rF.get()
                            c.op("act", lambda e: e.activation(out=ut.t[:NS, 0:128], in_=pb[bt][:NS, 0:128], func=AF.Copy),
                                 reads=[PB(bt)], writes=[ut])
                            c.dma("sp", lambda e: e.dma_start(out=nsc_s[:, 2, s * 128:(s + 1) * 128], in_=ut.t[:NS, 0:128]),
                                  reads=[ut], is_output=True)
                        ck = ("cqT", s, c0)
                        s0_mix_keys.add(ck)
                        if s >= 8:
                            c.op("act", lambda e: e.activation(out=cqT[:, s, c0:c0 + n], in_=acc.t[:, 0:n], func=AF.Silu),
                                 reads=[acc], writes=[ck])
                        else:
                            cs = rF.get()
                            c.op("act", lambda e: e.activation(out=cs.t[:, 0:n], in_=acc.t[:, 0:n], func=AF.Silu),
                                 reads=[acc], writes=[cs])
                            sqb = rB.get()
                            c.op("pool", lambda e: e.tensor_tensor(out=sqb.t[:, 0:n], in0=cs.t[:, 0:n], in1=cs.t[:, 0:n],
                                                                   op=ALU.mult), reads=[cs], writes=[sqb])
                            b2 = bank()
                            c.op("pe", lambda e: e.matmul(pb[b2][:, 0:n], lhsT=cb[:, ONES, :], rhs=sqb.t[:, 0:n],
                                                          start=True, stop=True), reads=[sqb, "cb"], writes=[PB(b2)])
                            rr = rF.get()
                            c.op("act", lambda e: e.activation(out=rr.t[:, 0:n], in_=pb[b2][:, 0:n], func=AF.Ln, bias=epst[:, 0:1]),
                                 reads=[PB(b2), "epst"], writes=[rr])
                            c.op("act", lambda e: e.activation(out=rr.t[:, 0:n], in_=rr.t[:, 0:n], func=AF.Exp, scale=-0.5),
                                 reads=[rr], writes=[rr])
                            sc_ = float(DK ** -0.5) if s < 4 else 1.0
                            c.op("dve", lambda e: e.scalar_tensor_tensor(out=cqT[:, s, c0:c0 + n], in0=cs.t[:, 0:n], scalar=sc_,
                                                                         op0=ALU.mult, in1=rr.t[:, 0:n], op1=ALU.mult),
                                 reads=[cs, rr], writes=[ck])
                        yield
            if last_block:
                for r in range(3):
                    cc_ = rF.get()
                    c.op("dve", lambda e, r=r: e.tensor_copy(out=cc_.t[:, 0:12], in_=car[:, :, r]), reads=["car"], writes=[cc_])
                    bt = bank()
                    c.op("pe", lambda e: e.transpose(out=pb[bt][:12, 0:128], in_=cc_.t[:, 0:12], identity=cf[:, IDF, :]),
                         reads=[cc_, "cf"], writes=[PB(bt)])
                    co = rF.get()
                    c.op("act", lambda e: e.activation(out=co.t[:12, 0:128], in_=pb[bt][:12, 0:128], func=AF.Copy),
                         reads=[PB(bt)], writes=[co])
                    c.dma("sp", lambda e, r=r: e.dma_start(out=nsc_p[r, :].rearrange("(s p) -> s p", p=128), in_=co.t[:12, 0:128]),
                          reads=[co], is_output=True)

        def gdn_scalars(i, nrows, c0):
            ba = bank()
            mm8(ba, nrows, 8, lambda k: hT[:, k, c0:c0 + nrows], lambda k: wsmall[:, k, :], ["wsmall", ("hT", i)])
            sA = rS.get()
            sB = rS.get()
            c.op("dve", lambda e: e.tensor_tensor(out=sA.t[:nrows, 0:4], in0=pb[ba][:nrows, 0:4], in1=dtb_t[:nrows, :],
                                                  op=ALU.add), reads=[PB(ba), "dtb"], writes=[sA])
            c.op("act", lambda e: e.activation(out=sA.t[:nrows, 4:8], in_=sA.t[:nrows, 0:4], func=AF.Abs), reads=[sA], writes=[sA])
            c.op("act", lambda e: e.activation(out=sA.t[:nrows, 8:12], in_=sA.t[:nrows, 4:8], func=AF.Exp, scale=-1.0),
                 reads=[sA], writes=[sA])
            c.op("dve", lambda e: e.tensor_scalar(out=sA.t[:nrows, 8:12], in0=sA.t[:nrows, 8:12], scalar1=1.0, scalar2=None,
                                                  op0=ALU.add), reads=[sA], writes=[sA])
            c.op("act", lambda e: e.activation(out=sA.t[:nrows, 8:12], in_=sA.t[:nrows, 8:12], func=AF.Ln),
                 reads=[sA], writes=[sA])
            c.op("dve", lambda e: e.tensor_scalar(out=sA.t[:nrows, 12:16], in0=sA.t[:nrows, 0:4], scalar1=0.0, scalar2=None,
                                                  op0=ALU.max), reads=[sA], writes=[sA])
            c.op("dve", lambda e: e.tensor_tensor(out=sB.t[:nrows, 0:4], in0=sA.t[:nrows, 12:16], in1=sA.t[:nrows, 8:12],
                                                  op=ALU.add), reads=[sA], writes=[sB])
            gk = ("gsc", i)
            c.op("dve", lambda e: e.tensor_tensor(out=gsc[:nrows, i, 0:4], in0=sB.t[:nrows, 0:4], in1=nega_t[:nrows, :],
                                                  op=ALU.mult), reads=[sB, "nega"], writes=[gk])
            c.op("act", lambda e: e.activation(out=sB.t[:nrows, 4:8], in_=pb[ba][:nrows, 4:8], func=AF.Exp, scale=-1.0),
                 reads=[PB(ba), sB], writes=[sB])
            c.op("dve", lambda e: e.tensor_scalar(out=sB.t[:nrows, 8:12], in0=sB.t[:nrows, 4:8], scalar1=1.0, scalar2=None,
                                                  op0=ALU.add), reads=[sB], writes=[sB])
            c.op("dve", lambda e: e.reciprocal(out=gsc[:nrows, i, 4:8], in_=sB.t[:nrows, 8:12]), reads=[sB, gk], writes=[gk])
            c.op("dve", lambda e: e.tensor_scalar(out=gsc[:nrows, i, 8:12], in0=gsc[:nrows, i, 4:8], scalar1=-1.0, scalar2=None,
                                                  op0=ALU.mult), reads=[gk], writes=[gk])
            return gk

        def phase_G2(ptiles, with_sample):
            wsf = rF.get()
            c.dma("sp", lambda e: e.dma_start(out=wsf.t[:, 0:64].rearrange("p (c n) -> p c n", n=8),
                                              in_=w_in[:, 4096:4104].rearrange("(c p) n -> p c n", p=128)), writes=[wsf])
            c.op("dve", lambda e: e.tensor_copy(out=wsmall[:, :, :], in_=wsf.t[:, 0:64].rearrange("p (c n) -> p c n", n=8)),
                 reads=[wsf], writes=["wsmall"])
            T7, k7 = wload(w_in, 0, 8, 3584, 512)
            pendG = None
            for (i, nrows, c0, row0) in ptiles:
                gk = gdn_scalars(i, 128, c0)
                if G2CUT <= 1:
                    continue
                bG = bank()
                for col, m in ((0, TRI), (4, SU), (8, ONES)):
                    c.op("pe", lambda e, col=col, m=m: e.matmul(pb[bG][:, col:col + 4], lhsT=cf[:, m, :], rhs=gsc[:, i, 0:4],
                                                                start=True, stop=True), reads=[gk, "cf"], writes=[PB(bG)])
                ex = rS.get()
                c.op("act", lambda e: e.activation(out=ex.t[:, 0:12], in_=pb[bG][:, 0:12], func=AF.Exp), reads=[PB(bG)], writes=[ex])
                gsu = rF.get()
                c.op("pool", lambda e: e.tensor_tensor(out=v4(gsu.t[:, 0:512]),
                                                       in0=cf[:, SU, :].unsqueeze(1).broadcast_to([128, 4, 128]),
                                                       in1=gsc[:, i, 0:4].unsqueeze(2).broadcast_to([128, 4, 128]), op=ALU.mult),
                     reads=[gk, "cf"], writes=[gsu])
                bD = bank()
                for h in range(4):
                    c.op("pe", lambda e, h=h: e.matmul(pb[bD][:, hb(h)], lhsT=gsu.t[:, hb(h)], rhs=cf[:, TRI, :],
                                                       start=True, stop=True), reads=[gsu, "cf"], writes=[PB(bD)])
                E = rF.get()
                c.op("act", lambda e: e.activation(out=E.t[:, 0:512], in_=pb[bD][:, :], func=AF.Exp), reads=[PB(bD)], writes=[E])
                EMS = rF.get()
                c.op("pool", lambda e: e.tensor_tensor(out=v4(EMS.t[:, 0:512]), in0=v4(E.t[:, 0:512]),
                                                       in1=cf[:, MASKS, :].unsqueeze(1).broadcast_to([128, 4, 128]), op=ALU.mult),
                     reads=[E, "cf"], writes=[EMS])
                c.op("pool", lambda e: e.tensor_tensor(out=v4(E.t[:, 0:512]), in0=v4(E.t[:, 0:512]),
                                                       in1=cf[:, TRI, :].unsqueeze(1).broadcast_to([128, 4, 128]), op=ALU.mult),
                     reads=[E, "cf"], writes=[E])
                kq_reads = [("cqT", s, cc) for s in range(8) for cc in [cg0 for cg0 in cq_cg0(c0)]]
                if G2CUT <= 2:
                    continue
                if pendG is not None:
                    pendG()
                    pendG = None
                bK = bank()
                for h in range(4):
                    c.op("pe", lambda e, h=h: e.matmul(pb[bK][:, hb(h)], lhsT=cqT[:, 4 + h, c0:c0 + 128], rhs=cqT[:, 4 + h, c0:c0 + 128],
                                                       start=True, stop=True), reads=kq_reads, writes=[PB(bK)])
                Y = rB.get()
                for h in range(4):
                    c.op("dve", lambda e, h=h: e.scalar_tensor_tensor(out=Y.t[:, hb(h)], in0=pb[bK][:, hb(h)],
                                                                      scalar=gsc[:, i, 8 + h:9 + h], op0=ALU.mult,
                                                                      in1=EMS.t[:, hb(h)], op1=ALU.mult),
                         reads=[PB(bK), gk, EMS] + ([Y] if h else []), writes=[Y])
                if G2CUT <= 3:
                    continue
                bX = bank()
                for h in range(4):
                    c.op("pe", lambda e, h=h: e.transpose(out=pbb[bX][:, hb(h)], in_=Y.t[:, hb(h)], identity=identb),
                         reads=[Y, "cb"], writes=[PB(bX)])
                X = rB.get()
                c.op("act", lambda e: e.activation(out=X.t[:, :], in_=pbb[bX][:, 0:512], func=AF.Copy), reads=[PB(bX)], writes=[X])
                PT = rB.get()
                c.op("pool", lambda e: e.tensor_tensor(out=v4(PT.t[:, :]), in0=v4(Y.t[:, :]),
                                                       in1=cb[:, IDF, :].unsqueeze(1).broadcast_to([128, 4, 128]), op=ALU.add),
                     reads=[Y, "cb"], writes=[PT])
                if G2CUT <= 4:
                    continue
                for step in range(6):
                    bXn = bank()
                    for h in range(4):
                        c.op("pe", lambda e, h=h, X=X, Y=Y: e.matmul(pb[bXn][:, hb(h)], lhsT=Y.t[:, hb(h)], rhs=X.t[:, hb(h)],
                                                                    start=True, stop=True), reads=[X, Y], writes=[PB(bXn)])
                    if step < 5:
                        bYn = bank()
                        for h in range(4):
                            c.op("pe", lambda e, h=h, X=X, Y=Y: e.matmul(pb[bYn][:, hb(h)], lhsT=X.t[:, hb(h)], rhs=Y.t[:, hb(h)],
                                                                        start=True, stop=True), reads=[X, Y], writes=[PB(bYn)])
                    Xn = rB.get()
                    c.op("act", lambda e, Xn=Xn: e.activation(out=Xn.t[:, :], in_=pb[bXn][:, :], func=AF.Copy),
                         reads=[PB(bXn)], writes=[Xn])
                    if step < 5:
                        Yn = rB.get()
                        c.op("dve", lambda e, Yn=Yn: e.tensor_copy(out=Yn.t[:, :], in_=pb[bYn][:, :]), reads=[PB(bYn)], writes=[Yn])
                    bP = bank()
                    for h in range(4):
                        c.op("pe", lambda e, h=h, Xn=Xn, PT=PT: e.matmul(pb[bP][:, hb(h)], lhsT=Xn.t[:, hb(h)], rhs=PT.t[:, hb(h)],
                                                                        start=True, stop=True), reads=[Xn, PT], writes=[PB(bP)])
                    PTn = rB.get()
                    c.op("dve", lambda e, PTn=PTn, PT=PT: e.tensor_tensor(out=PTn.t[:, :], in0=pb[bP][:, :], in1=PT.t[:, :],
                                                                         op=ALU.add), reads=[PB(bP), PT], writes=[PTn])
                    X, PT = Xn, PTn
                    if step < 5:
                        Y = Yn
                if G2CUT <= 5:
                    continue
                bQ = bank()
                for h in range(4):
                    c.op("pe", lambda e, h=h: e.matmul(pb[bQ][:, hb(h)], lhsT=cqT[:, 4 + h, c0:c0 + 128], rhs=cqT[:, h, c0:c0 + 128],
                                                       start=True, stop=True), reads=kq_reads, writes=[PB(bQ)])
                qkm = rB.get()
                c.op("dve", lambda e: e.tensor_tensor(out=qkm.t[:, :], in0=pb[bQ][:, :], in1=E.t[:, 0:512], op=ALU.mult),
                     reads=[PB(bQ), E], writes=[qkm])
                if 'suba' in SK:
                    continue
                bT = bank()
                kv_reads = [("cqT", s, cg0) for s in range(4, 12) for cg0 in cq_cg0(c0)]
                for h in range(4):
                    c.op("pe", lambda e, h=h: e.transpose(out=pbb[bT][:, hb(h)], in_=cqT[:, 4 + h, c0:c0 + 128], identity=identb),
                         reads=kv_reads + ["cb"], writes=[PB(bT)])
                bT2 = bank()
                for h in range(4):
                    c.op("pe", lambda e, h=h: e.transpose(out=pbb[bT2][:, hb(h)], in_=cqT[:, 8 + h, c0:c0 + 128], identity=identb),
                         reads=kv_reads + ["cb"], writes=[PB(bT2)])
                if 'subb' in SK:
                    continue
                kg = rB.get()
                c.op("dve", lambda e: e.tensor_tensor(out=v4(kg.t[:, :]), in0=v4(pbb[bT][:, 0:512]),
                                                      in1=ex.t[:, 0:4].unsqueeze(2).broadcast_to([128, 4, 128]), op=ALU.mult),
                     reads=[PB(bT), ex], writes=[kg])
                if 'subc' in SK:
                    continue
                kd = rB.get()
                c.op("dve", lambda e: e.tensor_tensor(out=v4(kd.t[:, :]), in0=v4(pbb[bT][:, 0:512]),
                                                      in1=ex.t[:, 4:8].unsqueeze(2).broadcast_to([128, 4, 128]), op=ALU.mult),
                     reads=[PB(bT), ex], writes=[kd])
                if 'subd' in SK:
                    continue
                vt = rB.get()
                c.op("act", lambda e: e.activation(out=vt.t[:, :], in_=pbb[bT2][:, 0:512], func=AF.Copy), reads=[PB(bT2)], writes=[vt])
                if G2CUT <= 6:
                    continue
                bW = bank()
                for h in range(4):
                    c.op("pe", lambda e, h=h: e.matmul(pb[bW][:, hb(h)], lhsT=kg.t[:, hb(h)], rhs=PT.t[:, hb(h)],
                                                       start=True, stop=True), reads=[kg, PT], writes=[PB(bW)])
                NW = rB.get()
                c.op("act", lambda e: e.activation(out=NW.t[:, :], in_=pb[bW][:, :], func=AF.Copy, scale=-1.0),
                     reads=[PB(bW)], writes=[NW])
                if G2CUT <= 7:
                    continue
                bU = bank()
                for h in range(4):
                    c.op("pe", lambda e, h=h: e.matmul(pb[bU][:, hb(h)], lhsT=PT.t[:, hb(h)], rhs=vt.t[:, hb(h)],
                                                       start=True, stop=False), reads=[PT, vt], writes=[PB(bU)])
                    c.op("pe", lambda e, h=h: e.matmul(pb[bU][:, hb(h)], lhsT=NW.t[:, hb(h)], rhs=Sgdnb[:, h, :],
                                                       start=False, stop=True), reads=[NW, "Sgdnb"], writes=[PB(bU)])
                U = rB.get()
                c.op("dve", lambda e: e.tensor_tensor(out=v4(U.t[:, :]), in0=v4(pb[bU][:, :]),
                                                      in1=gsc[:, i, 4:8].unsqueeze(2).broadcast_to([128, 4, 128]), op=ALU.mult),
                     reads=[PB(bU), gk], writes=[U])
                if G2CUT <= 8:
                    continue
                gbc = rF.get()
                c.op("pool", lambda e: e.tensor_copy(out=v4(gbc.t[:, 0:512]), in_=gsc[:, i, 0:4].unsqueeze(2).broadcast_to([128, 4, 128])),
                     reads=[gk], writes=[gbc])
                bR = bank()
                for h in range(4):
                    c.op("pe", lambda e, h=h: e.matmul(pb[bR][:, hb(h)], lhsT=gbc.t[:, hb(h)],
                                                       rhs=cf[:, TRI, :], start=True, stop=True), reads=[gbc, "cf"], writes=[PB(bR)])
                EG = rF.get()
                c.op("act", lambda e: e.activation(out=EG.t[:, 0:512], in_=pb[bR][:, :], func=AF.Exp), reads=[PB(bR)], writes=[EG])
                qg = rB.get()
                c.op("pool", lambda e: e.tensor_tensor(out=v4(qg.t[:, :]), in0=cqT[:, 0:4, c0:c0 + 128], in1=v4(EG.t[:, 0:512]),
                                                       op=ALU.mult), reads=kq_reads + [EG], writes=[qg])
                if G2CUT <= 9:
                    continue
                bO = bank()
                for h in range(4):
                    c.op("pe", lambda e, h=h: e.matmul(pb[bO][:, hb(h)], lhsT=qg.t[:, hb(h)], rhs=Sgdnb[:, h, :],
                                                       start=True, stop=False), reads=[qg, "Sgdnb"], writes=[PB(bO)])
                    c.op("pe", lambda e, h=h: e.matmul(pb[bO][:, hb(h)], lhsT=qkm.t[:, hb(h)], rhs=U.t[:, hb(h)],
                                                       start=False, stop=True), reads=[qkm, U], writes=[PB(bO)])
                if G2CUT <= 10:
                    continue
                bS = bank()
                for h in range(4):
                    c.op("pe", lambda e, h=h: e.matmul(pb[bS][:, hb(h)], lhsT=kd.t[:, hb(h)], rhs=U.t[:, hb(h)],
                                                       start=True, stop=True), reads=[kd, U], writes=[PB(bS)])
                for h in range(4):
                    c.op("dve", lambda e, h=h: e.scalar_tensor_tensor(out=Sgdn[:, h, :], in0=Sgdn[:, h, :], scalar=ex.t[:, 8 + h:9 + h],
                                                                      op0=ALU.mult, in1=pb[bS][:, hb(h)], op1=ALU.add),
                         reads=[PB(bS), "Sgdn", ex], writes=["Sgdn"])
                c.op("act", lambda e: e.activation(out=Sgdnb[:], in_=Sgdn[:], func=AF.Copy), reads=["Sgdn"], writes=["Sgdnb"])
                if G2CUT <= 11:
                    continue
                bz = bank()
                mm8(bz, 128, 512, lambda k: hT[:, k, c0:c0 + 128], lambda k: T7[:, k, :], [k7, ("hT", i)])
                gzs = rB.get()
                c.op("act", lambda e: e.activation(out=gzs.t[:, :], in_=pb[bz][:, :], func=AF.Silu), reads=[PB(bz)], writes=[gzs])
                on = rmsnorm_gdn(bO, 128)
                pendG = onorm_tail(i, 128, c0, on, gzs, o_gT, "ogT", gng_t, "gng", True, defer=True)
            if pendG is not None:
                pendG()
            if with_sample:
                sample_gdn(T7, k7)

        def cq_cg0(c0):
            return [0 if c0 < 512 else 512]

        not_sample_block = [False]

        def sample_gdn(T7, k7):
            i, c0 = SI, SC0
            gk = gdn_scalars(i, NS, c0)
            sa = rS.get()
            c.op("act", lambda e: e.activation(out=sa.t[:NS, 0:4], in_=gsc[:NS, i, 0:4], func=AF.Exp), reads=[gk], writes=[sa])
            ad = rF.get()
            c.op("pool", lambda e: e.memset(ad.t[:, 0:NS * 4], 0.0), writes=[ad])
            c.op("dve", lambda e: e.tensor_tensor(out=ad.t[:NS, 0:NS * 4].rearrange("p (t h) -> p t h", h=4),
                                                  in0=sa.t[:NS, 0:4].unsqueeze(1).broadcast_to([NS, NS, 4]),
                                                  in1=cf[:NS, IDF, :NS].unsqueeze(2).broadcast_to([NS, NS, 4]), op=ALU.mult),
                 reads=[sa, "cf", ad], writes=[ad])
            ba = bank()
            c.op("pe", lambda e: e.matmul(pb[ba][:, 0:NS * 4], lhsT=cf[:, ONES, :], rhs=ad.t[:, 0:NS * 4], start=True, stop=True),
                 reads=[ad, "cf"], writes=[PB(ba)])
            c.op("act", lambda e: e.activation(out=s_abc[:].rearrange("p t h -> p (t h)"), in_=pb[ba][:, 0:NS * 4], func=AF.Copy),
                 reads=[PB(ba)], writes=["s_abc"])
            cq_reads = [("cqT", s, SC0) for s in range(12)]
            bT = bank()
            for h in range(4):
                c.op("pe", lambda e, h=h: e.transpose(out=pbb[bT][:NS, hb(h)], in_=cqT[:, 4 + h, c0:c0 + NS], identity=identb),
                     reads=cq_reads + ["cb"], writes=[PB(bT)])
            bT2 = bank()
            for h in range(4):
                c.op("pe", lambda e, h=h: e.transpose(out=pbb[bT2][:NS, hb(h)], in_=cqT[:, 8 + h, c0:c0 + NS], identity=identb),
                     reads=cq_reads + ["cb"], writes=[PB(bT2)])
            c.op("act", lambda e: e.activation(out=s_ks[:NS, :], in_=pbb[bT][:NS, 0:512], func=AF.Copy), reads=[PB(bT)], writes=["s_ks"])
            c.op("act", lambda e: e.activation(out=s_vs[:NS, :], in_=pbb[bT2][:NS, 0:512], func=AF.Copy), reads=[PB(bT2)], writes=["s_vs"])
            build_masked(s_km, "s_km", lambda h: cqT[:, 4 + h, c0:c0 + NS], cq_reads)
            build_masked(s_qm, "s_qm", lambda h: cqT[:, h, c0:c0 + NS], cq_reads)
            for h in range(4):
                S0gs, S0bs = [], []
                for tg in range(4):
                    S0g = rF.get()
                    c.dma("sp", lambda e, S0g=S0g, tg=tg: e.dma_start(
                        out=v4(S0g.t[:, 0:512]), in_=sgdn_in[tg * 4:(tg + 1) * 4, h].rearrange("t d e -> d t e")),
                        writes=[S0g])
                    S0b = rB.get()
                    c.op("act", lambda e, S0b=S0b, S0g=S0g: e.activation(out=S0b.t[:, :], in_=S0g.t[:, 0:512], func=AF.Copy),
                         reads=[S0g], writes=[S0b])
                    S0gs.append(S0g)
                    S0bs.append(S0b)
                bK = bank()
                for t in range(NS):
                    c.op("pe", lambda e, t=t: e.matmul(pb[bK][:NS, 0:128], lhsT=s_km[:, h, t, :], rhs=S0bs[t // 4].t[:, hb(t % 4)],
                                                       start=(t == 0), stop=(t == NS - 1)), reads=["s_km", S0bs[t // 4]], writes=[PB(bK)])
                uu = rF.get()
                c.op("dve", lambda e: e.scalar_tensor_tensor(out=uu.t[:NS, 0:128], in0=pb[bK][:NS, 0:128], scalar=sa.t[:NS, h:h + 1],
                                                             op0=ALU.mult, in1=s_vs[:NS, hb(h)], op1=ALU.subtract),
                     reads=[PB(bK), sa, "s_vs"], writes=[uu])
                c.op("dve", lambda e: e.tensor_scalar(out=uu.t[:NS, 0:128], in0=uu.t[:NS, 0:128], scalar1=gsc[:NS, i, 8 + h:9 + h],
                                                      scalar2=None, op0=ALU.mult), reads=[uu, gk], writes=[uu])
                snbs = []
                for tg in range(4):
                    ub = make_ublk(uu.t[:NS, 0:128], [uu], tg)
                    a_bc = s_abc[:, tg * 4:(tg + 1) * 4, h].unsqueeze(2).broadcast_to([128, 4, 128])
                    snbs.append(sample_state_update(h, tg, S0gs[tg], s_ks[:NS, hb(h)], ub, None, a_bc, nsg_s))
                bQ = bank()
                for t in range(NS):
                    c.op("pe", lambda e, t=t: e.matmul(pb[bQ][:NS, 0:128], lhsT=s_qm[:, h, t, :], rhs=snbs[t // 4].t[:, hb(t % 4)],
                                                       start=(t == 0), stop=(t == NS - 1)), reads=["s_qm", snbs[t // 4]], writes=[PB(bQ)])
                c.op("act", lambda e, h=h: e.activation(out=s_os[:NS, hb(h)], in_=pb[bQ][:NS, 0:128], func=AF.Copy),
                     reads=[PB(bQ), "s_os"], writes=["s_os"])
            bz = bank()
            mm8(bz, NS, 512, lambda k: hT[:, k, c0:c0 + NS], lambda k: T7[:, k, :], [k7, ("hT", i)])
            c.op("act", lambda e: e.activation(out=s_gate[:NS, :], in_=pb[bz][:NS, :], func=AF.Silu), reads=[PB(bz)], writes=["s_gate"])
            on = rmsnorm_gdn(None, NS, s_os[:NS, :], "s_os")
            onorm_tail(i, NS, c0, on, None, o_gT, "ogT", gng_t, "gng", True)

        def phase_M(tiles, cgs, after_tile=None):
            load_gp(1)
            yk = []
            for half in range(2):
                Tgr = wload(w_in, 0, 8, 4104 + half * 512, 512)
                Tgg = wload(w_in, 0, 8, 5128 + half * 512, 512)
                Trb = wload(W["w_ret_branch"], 0, 4, half * 512, 512)
                Tgb = wload(W["w_gdn_branch"], 0, 4, half * 512, 512)
                for cc in range(4):
                    ch = half * 4 + cc
                    for (c0, n, tl) in cgs:
                        hk = [("hT", t) for t in tl]
                        b1, b2, b3, b4 = bank(), bank(), bank(), bank()
                        mm8(b1, 128, n, lambda k: Tgr[0][:, k, cc * 128:(cc + 1) * 128], lambda k: hT[:, k, c0:c0 + n], [Tgr[1]] + hk)
                        mm8(b2, 128, n, lambda k: Tgg[0][:, k, cc * 128:(cc + 1) * 128], lambda k: hT[:, k, c0:c0 + n], [Tgg[1]] + hk)
                        mm8(b3, 128, n, lambda k: Trb[0][:, k, cc * 128:(cc + 1) * 128], lambda k: o_rT[:, k, c0:c0 + n],
                            [Trb[1]] + [("orT", t) for t in tl], nk=4)
                        mm8(b4, 128, n, lambda k: Tgb[0][:, k, cc * 128:(cc + 1) * 128], lambda k: o_gT[:, k, c0:c0 + n],
                            [Tgb[1]] + [("ogT", t) for t in tl], nk=4)
                        s1 = rF.get()
                        c.op("act", lambda e: e.activation(out=s1.t[:, 0:n], in_=pb[b1][:, 0:n], func=AF.Sigmoid), reads=[PB(b1)], writes=[s1])
                        s2 = rF.get()
                        c.op("act", lambda e: e.activation(out=s2.t[:, 0:n], in_=pb[b2][:, 0:n], func=AF.Sigmoid), reads=[PB(b2)], writes=[s2])
                        c.op("dve", lambda e: e.tensor_tensor(out=s1.t[:, 0:n], in0=s1.t[:, 0:n], in1=pb[b3][:, 0:n], op=ALU.mult),
                             reads=[s1, PB(b3)], writes=[s1])
                        c.op("dve", lambda e: e.tensor_tensor(out=s2.t[:, 0:n], in0=s2.t[:, 0:n], in1=pb[b4][:, 0:n], op=ALU.mult),
                             reads=[s2, PB(b4)], writes=[s2])
                        ykey = ("yT", ch, c0)
                        s0_mix_keys.add(ykey)
                        c.op("pool", lambda e: e.tensor_tensor(out=yT[:, ch, c0:c0 + n], in0=s1.t[:, 0:n], in1=s2.t[:, 0:n], op=ALU.add),
                             reads=[s1, s2], writes=[ykey])
            Wo = [wload(W["w_out"], 0, 8, hf * 512, 512) for hf in range(2)]
            pend = [None]
            for (i, nrows, c0, row0) in tiles:
                cgc0 = [cc for (cc, n, tl) in cgs if cc <= c0 < cc + n][0]
                bs = []
                for hf in range(2):
                    b = bank()
                    bs.append(b)
                    mm8(b, nrows, 512, lambda k: yT[:, k, c0:c0 + nrows], lambda k: Wo[hf][0][:, k, :],
                        [Wo[hf][1]] + [("yT", k_, cgc0) for k_ in range(8)])
                if pend[0] is not None:
                    pend[0]()
                    pend[0] = None
                post(i, nrows, bs[0], bs[1], 1.0)
                if after_tile is not None:
                    pend[0] = after_tile(i, nrows, c0)
            if pend[0] is not None:
                pend[0]()

        def mk_tiles(g0, n, sample):
            t = [(i, 128, i * 128, (g0 + i) * 128) for i in range(n)]
            if sample:
                t.append((SI, NS, SC0, SEQ))
            return t

        blocks = [
            dict(g0=0, n=6, sample=True, cgs=[(0, 512, [0, 1, 2, 3]), (512, 256 + NS, [4, 5, SI])],
                 cgs_p=[(0, 512, [0, 1, 2, 3]), (512, 256, [4, 5])]),
            dict(g0=6, n=6, sample=False, cgs=[(0, 512, [0, 1, 2, 3]), (512, 256, [4, 5])],
                 cgs_p=[(0, 512, [0, 1, 2, 3]), (512, 256, [4, 5])]),
            dict(g0=12, n=4, sample=False, cgs=[(0, 512, [0, 1, 2, 3])], cgs_p=[(0, 512, [0, 1, 2, 3])]),
        ]
        if "oneblock" in SK:
            blocks = blocks[:1]
        for bi, blk in enumerate(blocks):
            tiles = mk_tiles(blk["g0"], blk["n"], blk["sample"])
            ptiles = [t for t in tiles if t[1] == 128]
            cgs = blk["cgs"]
            cgs_p = blk["cgs_p"]
            CQC = (0, 512, SC0)
            not_sample_block[0] = not blk["sample"]
            if bi == 0 or stage < 99:
                for (i, nrows, c0, row0) in tiles:
                    c.dma("sp", lambda e, i=i, nrows=nrows, row0=row0: e.dma_start(out=x1[:nrows, i, :],
                                                                              in_=xin[row0:row0 + nrows, :]),
                          writes=[("x1", i)])
            c.dma("sp", lambda e: e.dma_start(out=cs_t[:, 0:blk["n"]], in_=cs_in[:, blk["g0"]:blk["g0"] + blk["n"]]), writes=["cs_t"])
            if blk["sample"]:
                c.dma("sp", lambda e: e.dma_start(out=cs_t[:, SI], in_=cs_in[:, 16]), reads=["cs_t"], writes=["cs_t"])
            OVL = stage == 99
            c.region("ffn1")
            if "noffn1" not in SK:
                if bi == 0 or not OVL:
                    for (i, nrows, c0, row0) in tiles:
                        prenorm(i, nrows, c0, 0)
                if bi == 0:
                    wload_after[0] = [("x1", t_[0]) for t_ in tiles]
                ffn(tiles, cgs, W["ffn1_w_gate"], W["ffn1_w_up"], W["ffn1_w_down"], 0, final=(stage == 1),
                    after_tile=(lambda i, nrows, c0: prenormA(i, nrows, c0, 1)) if OVL else None)
                wload_after[0] = ()
            if stage == 1:
                continue
            c.fence(list(s0_mix_keys) + [("orT", t[0]) for t in tiles] + [("ogT", t[0]) for t in tiles]
                    + [("cqT", s, cc_) for s in range(12) for cc_ in CQC], list(s0_ffn_keys))
            if not OVL:
                for (i, nrows, c0, row0) in tiles:
                    prenorm(i, nrows, c0, 1)
            if blk["sample"] and "sample" not in SK and "nsconv" not in SK:
                for r in range(3):
                    stg = [rF.get() for _ in range(3)]
                    for q_ in range(3):
                        c.dma("sp", lambda e, r=r, q_=q_: e.dma_start(out=stg[q_].t[:NS, 0:512], in_=sconv_in[:, r, q_ * 512:(q_ + 1) * 512]),
                              writes=[stg[q_]])
                    bt = bank()
                    for s in range(12):
                        c.op("pe", lambda e, s=s: e.transpose(out=pb[bt][:, s * NS:(s + 1) * NS],
                                                              in_=stg[s // 4].t[:NS, (s % 4) * 128:(s % 4 + 1) * 128],
                                                              identity=cf[:NS, IDF, :NS]), reads=[stg[s // 4], "cf"], writes=[PB(bt)])
                    c.op("act", lambda e, r=r: e.activation(out=scb[:, :, r, :], in_=pb[bt][:, 0:12 * NS].rearrange("p (s t) -> p s t", s=12),
                                                            func=AF.Copy), reads=[PB(bt), "scb"] if r else [PB(bt)], writes=["scb"])
                c.dma("sp", lambda e: e.dma_start(out=nsc_s[:, 0:2, :], in_=sconv_in[:, 1:3, :]), is_output=True)
            smp = blk["sample"] and "sample" not in SK
            c.region("R")
            g1 = phase_G1(cgs_p, smp and "nsg1" not in SK, bi == len(blocks) - 1) if "G1" not in SK else iter(())
            if "R" not in SK:
                phase_R(ptiles, smp and "nsret" not in SK, g1 if OVL else None)
            for _ in g1:
                pass
            c.region("G2")
            if "G2" not in SK:
                phase_G2(ptiles[:1] if "G2one" in SK else ptiles, smp and "nsgdn" not in SK)
            c.fence([("yT", ch, g[0]) for ch in range(8) for g in cgs], [("cqT", s, cc_) for s in range(12) for cc_ in CQC])
            c.region("M")
            if "M" not in SK:
                phase_M(tiles, cgs, after_tile=(lambda i, nrows, c0: prenormA(i, nrows, c0, 2)) if OVL else None)
            if bi == len(blocks) - 1:
                c.dma("sp", lambda e: e.dma_start(out=nsr_p.rearrange("h d e -> d h e"), in_=Sret[:]), reads=["Sret"], is_output=True)
                c.dma("sp", lambda e: e.dma_start(out=nsg_p.rearrange("h d e -> d h e"), in_=Sgdn[:]), reads=["Sgdn"], is_output=True)
            if stage == 2:
                for (i, nrows, c0, row0) in tiles:
                    c.dma("sp", lambda e, i=i, nrows=nrows, row0=row0: e.dma_start(out=y_out[row0:row0 + nrows, :], in_=x1[:nrows, i, :]),
                          reads=[("x1", i)], is_output=True)
                c.fence(list(s0_ffn_keys), list(s0_mix_keys))
                continue
            c.region("ffn2")
            c.fence(list(s0_ffn_keys), list(s0_mix_keys))
            if not OVL:
                for (i, nrows, c0, row0) in tiles:
                    prenorm(i, nrows, c0, 2)
            nxt = blocks[bi + 1] if bi + 1 < len(blocks) else None
            ntiles = {t[0]: t for t in mk_tiles(nxt["g0"], nxt["n"], nxt["sample"])} if nxt else {}

            def next_block_prefetch(i, nrows, c0):
                if i in ntiles:
                    (i2, nrows2, c02, row02) = ntiles[i]
                    c.dma("sp", lambda e: e.dma_start(out=x1[:nrows2, i2, :], in_=xin[row02:row02 + nrows2, :]),
                          writes=[("x1", i2)])
                    return prenormA(i2, nrows2, c02, 0)
                return None

            ffn(tiles, cgs, W["ffn2_w_gate"], W["ffn2_w_up"], W["ffn2_w_down"], 2, final=True,
                after_tile=next_block_prefetch if OVL else None)

        c.finish("sp")
        c.run_block()
    return nc


def host_consts():
    f32 = np.float32
    p = np.arange(128)
    cf = np.zeros((128, NCF, 128), f32)
    cf[:, 0, :] = np.eye(128)
    cf[:, 1, :] = (p[:, None] <= p[None, :])
    cf[:, 2, :] = (p[:, None] > p[None, :])
    cf[:, 3, :] = (p[:, None] < p[None, :])
    cf[:, 4, :] = 1.0
    cf[:, 5:7, :] = np.eye(NS, dtype=f32).reshape(1, 2, 128)
    lg = np.log(np.array(GAMMA, np.float64))
    rt = np.zeros((128, 2, 4, 128), np.float64)
    for h in range(4):
        dmt = np.exp((p[None, :] - p[:, None]) * lg[h]) * (p[:, None] <= p[None, :]) * DK ** -0.5
        rt[:, 0, h, :] = dmt
        rt[:, 1, h, :] = np.exp((p[None, :] + 1) * lg[h])
    kdec = np.exp((127 - p[:, None]) * lg[None, :]) * DK ** -0.5
    inv = (10000.0 ** (-(np.arange(0, 128, 2, dtype=f32)) / f32(128))).astype(f32)
    pos = np.concatenate([np.arange(SEQ, dtype=f32), np.full((128,), 16384.0, f32)])
    ang = (pos[:, None] * inv[None, :]).astype(f32)
    cs = np.stack([np.cos(ang), np.sin(ang), -np.sin(ang)], axis=1).astype(f32)
    cs = cs.reshape(17, 128, 3, 64).transpose(1, 0, 2, 3)
    return dict(constf=cf, rettab=rt.astype(f32), kdec=kdec.astype(f32), cossin=np.ascontiguousarray(cs))


_CACHE = {}


def kernel(**inp):
    f32 = np.float32
    stage = inp.pop("_stage", 99)
    debug = inp.pop("_debug", False)
    ncores = inp.pop("_cores", 8)
    key = (stage, debug, tuple(sorted(DBG_SKIP)))
    if key not in _CACHE:
        _CACHE[key] = build(stage, debug)
    nc = _CACHE[key]
    hc = host_consts()
    g = lambda n: np.asarray(inp[n], f32)[0]
    shared = {}
    for nm in ["ffn1_w_gate", "ffn1_w_up", "ffn1_w_down", "w_in", "w_ret_branch", "w_gdn_branch", "w_out",
               "ffn2_w_gate", "ffn2_w_up", "ffn2_w_down"]:
        shared[nm] = np.ascontiguousarray(g(nm))
    gpre = np.concatenate([g(n).reshape(8, 128).T for n in ("ffn1_pre_g", "mix_pre_g", "ffn2_pre_g")], axis=1)
    shared["gpre"] = np.ascontiguousarray(gpre)
    shared["gpost"] = np.ascontiguousarray(np.stack([g("ffn1_post_g"), g("mix_post_g"), g("ffn2_post_g")]))
    shared["ret_norm_g"] = g("ret_norm_g").reshape(1, 512)
    shared["gdn_norm_g"] = g("gdn_norm_g").reshape(1, 128)
    cw = g("gdn_conv_w")
    shared["convw"] = np.ascontiguousarray(cw.reshape(4, 12, 128).transpose(2, 1, 0))
    shared["a_log"] = g("gdn_a_log").reshape(1, 4)
    shared["dt_bias"] = g("gdn_dt_bias").reshape(1, 4)
    shared.update(hc)
    xp = np.asarray(inp["x_prompt"], f32)
    xs = np.asarray(inp["x_sample"], f32)
    sr = np.asarray(inp["state_ret"], f32)[0]
    sg = np.asarray(inp["state_gdn"], f32)[0]
    sc = np.asarray(inp["state_conv"], f32)[0]
    in_maps = []
    for b in range(ncores):
        m = dict(shared)
        m["xin"] = np.ascontiguousarray(np.concatenate([xp[b], xs[b * NS:(b + 1) * NS, 0, :]], axis=0))
        m["sret"] = np.ascontiguousarray(sr[b * NS:(b + 1) * NS])
        m["sgdn"] = np.ascontiguousarray(sg[b * NS:(b + 1) * NS])
        m["sconv"] = np.ascontiguousarray(sc[b * NS:(b + 1) * NS])
        in_maps.append(m)
    res = run_bass_kernel_spmd(nc, in_maps, core_ids=list(range(ncores)))
    R = list(res.results) + [res.results[0]] * (8 - ncores)
    yp = np.stack([R[b]["y"][:SEQ] for b in range(8)])
    ys = np.concatenate([R[b]["y"][SEQ:] for b in range(8)])[:, None, :]
    nrp = np.stack([R[b]["nsr_p"] for b in range(8)])[None]
    ngp = np.stack([R[b]["nsg_p"] for b in range(8)])[None]
    ncp = np.stack([R[b]["nsc_p"] for b in range(8)])[None]
    nrs = np.concatenate([R[b]["nsr_s"] for b in range(8)])[None]
    ngs = np.concatenate([R[b]["nsg_s"] for b in range(8)])[None]
    ncs = np.concatenate([R[b]["nsc_s"] for b in range(8)])[None]
    out = (yp, ys, nrp, ngp, ncp, nrs, ngs, ncs)
    if debug:
        return out, [R[b]["dbg"] for b in range(8)]
    return tuple(np.ascontiguousarray(o, dtype=f32) for o in out)
```

```python
import numpy as np
from contextlib import ExitStack
import concourse.bass as bass
import concourse.mybir as mybir
from concourse.bass_utils import run_bass_kernel_spmd

F32 = mybir.dt.float32
BF16 = mybir.dt.bfloat16
AF = mybir.ActivationFunctionType
ALU = mybir.AluOpType
AX = mybir.AxisListType

ENGS = ("pe", "act", "dve", "pool", "sp")

D = 1024
DFF = 2816
NJ = 22
DIN = 6152
SEQ = 2048
NS = 16
EPS = 1e-6
GAMMA = [1.0 - 2.0 ** (-5.0 - h) for h in range(4)]
DK = 128
NCF = 7


class H:
    __slots__ = ("t", "key", "rot", "idx", "gen")

    def __init__(self, t, key, rot, idx, gen):
        self.t, self.key, self.rot, self.idx, self.gen = t, key, rot, idx, gen


def _res(keys):
    out = []
    for k in keys:
        if isinstance(k, H):
            assert k.rot.gen[k.idx] == k.gen, "stale scratch tile %s" % (k.key,)
            out.append(k.key)
        else:
            out.append(k)
    return out


class _Rec:
    def __init__(self):
        self.call = None

    def __getattr__(self, name):
        if name.startswith("__"):
            raise AttributeError(name)

        def f(*a, **kw):
            self.call = (name, a, kw)
            return self
        return f

    def then_inc(self, *a, **k):
        return self


def _free_size(ap):
    try:
        n = 1
        for d in ap.shape[1:]:
            n *= int(d)
        return n
    except Exception:
        return 256


SCHED = True
PSUM_EXCL = True
HYBRID_K = None
TABLE_AWARE = False
REGION_FLUSH = False
KEEP_ORDER = set()
SCHED_REGIONS = {"ffn1", "ffn2", "G2", "M", "R"}


class Ctx:
    NDSEM = 28

    def __init__(self, nc, es, same_eng_sync=("act", "dve", "pool")):
        self.nc = nc
        self.es = es
        self.same_eng_sync = same_eng_sync
        self.q = {e: [] for e in ENGS}
        self.cnt = {e: 0 for e in ENGS}
        self.waited = {e: {} for e in ENGS}
        self.sems = {}
        self.EPOCH = 2000
        for i in range(self.NDSEM):
            self.sems[("d", i)] = es.enter_context(nc.semaphore("s_d%d" % i))
        self.ndma = 0
        self.res = {}
        self.out_events = []
        self.rec = SCHED
        self.nodes = []

    def sb(self, name, shape, dt):
        return self.es.enter_context(self.nc.sbuf_tensor(name, list(shape), dt))

    def ps(self, name, shape, dt):
        return self.es.enter_context(self.nc.psum_tensor(name, list(shape), dt))

    def _deps(self, reads, writes, eng=None):
        deps = []
        for r in reads:
            st = self.res.get(r)
            if st and st["w"]:
                deps.append(st["w"])
            if st and PSUM_EXCL and isinstance(r, tuple) and r[0] == "pb":
                for ev in st["r"].items():
                    if ev[0][0] != eng:
                        deps.append(ev)
        for w in writes:
            st = self.res.get(w)
            if st:
                if st["w"]:
                    deps.append(st["w"])
                deps.extend(st["r"].items())
        return deps

    def _emit_waits(self, eng, deps):
        wd = self.waited[eng]
        best = {}
        for (k, v) in deps:
            if k[0] == eng and (eng == "pe" or eng not in self.same_eng_sync):
                continue
            if wd.get(k, 0) < v and best.get(k, 0) < v:
                best[k] = v
        for k, v in best.items():
            wd[k] = v
            self._eo(eng).wait_ge(self.sems[k], v)

    def _update(self, ev, reads, writes):
        for r in reads:
            st = self.res.setdefault(r, {"w": None, "r": {}})
            if st["r"].get(ev[0], 0) < ev[1]:
                st["r"][ev[0]] = ev[1]
        for w in writes:
            self.res[w] = {"w": ev, "r": {}}

    def fence(self, new_keys, old_keys):
        if self.rec:
            self.nodes.append(("fence", list(new_keys), list(old_keys)))
            return
        self._fence(new_keys, old_keys)

    def _fence(self, new_keys, old_keys):
        acc = {}
        for ok in old_keys:
            st = self.res.get(ok)
            if not st:
                continue
            evs = list(st["r"].items())
            if st["w"]:
                evs.append(st["w"])
            for k, v in evs:
                if acc.get(k, 0) < v:
                    acc[k] = v
        for nk in new_keys:
            st = self.res.setdefault(nk, {"w": None, "r": {}})
            for k, v in acc.items():
                if st["r"].get(k, 0) < v:
                    st["r"][k] = v

    def op(self, eng, fn, reads=(), writes=()):
        reads, writes = _res(reads), _res(writes)
        if self.rec:
            r = _Rec()
            fn(r)
            self.nodes.append(("op", eng, r.call, reads, writes))
            return None
        return self._op(eng, fn, reads, writes)

    def _op(self, eng, fn, reads, writes):
        deps = self._deps(reads, writes, eng)
        self._emit_waits(eng, deps)
        self.cnt[eng] += 1
        n = self.cnt[eng] - 1
        sk = (eng, n // self.EPOCH)
        if sk not in self.sems:
            self.sems[sk] = self.es.enter_context(self.nc.semaphore("s_%s_%d" % sk))
        ev = (sk, n % self.EPOCH + 1)
        fn(self._eo(eng)).then_inc(self.sems[sk], 1)
        self._update(ev, reads, writes)
        return ev

    def dma(self, queue, fn, reads=(), writes=(), is_output=False):
        reads, writes = _res(reads), _res(writes)
        if self.rec:
            r = _Rec()
            fn(r)
            self.nodes.append(("dma", queue, r.call, reads, writes, is_output))
            return None
        return self._dma(queue, fn, reads, writes, is_output)

    def _dma(self, queue, fn, reads, writes, is_output):
        n = self.ndma
        self.ndma += 1
        k = ("d", n % self.NDSEM)
        rnd = n // self.NDSEM
        deps = self._deps(reads, writes)
        if rnd > 0:
            deps.append((k, 16 * rnd))
        self._emit_waits(queue, deps)
        ev = (k, 16 * (rnd + 1))
        fn(self._eo(queue)).then_inc(self.sems[k], 16)
        self._update(ev, reads, writes)
        if is_output:
            self.out_events.append(ev)
        return ev

    def _eo(self, e):
        nc = self.nc
        return {"pe": nc.tensor, "act": nc.scalar, "dve": nc.vector, "pool": nc.gpsimd, "sp": nc.sync}[e]

    def region(self, name):
        if not REGION_FLUSH:
            return
        self.flush()
        self.rec = SCHED and (SCHED_REGIONS is None or name in SCHED_REGIONS)

    def finish(self, eng="sp"):
        self.flush()
        self._emit_waits(eng, self.out_events)

    def _est(self, nd):
        kind, eng, call = nd[0], nd[1], nd[2]
        name, a, kw = call
        if kind == "dma":
            out = kw.get("out", a[0] if a else None)
            nbytes = _free_size(out) * int(out.shape[0]) * 4 if out is not None else 65536
            lat = 2.0 + nbytes / 250e3
            return (1.0 if eng == "pool" else 0.06), lat
        if eng == "pe":
            if name == "transpose":
                return 0.1, 0.1
            rhs = kw.get("rhs")
            n = _free_size(rhs) if rhs is not None else 128
            d = 0.04 + n / 2600.0
            if rhs is not None and rhs.dtype == F32:
                d *= 4
            return d, d
        out = kw.get("out", a[0] if a else None)
        n = _free_size(out) if out is not None else 256
        if eng == "act":
            d = 0.30 + n / 1200.0
        elif eng == "dve":
            d = 0.16 + n / 1200.0
        else:
            d = 0.25 + n / 550.0
        return d, d

    def flush(self):
        import heapq
        nodes, self.nodes = self.nodes, []
        if not nodes:
            return
        self.rec = False
        N = len(nodes)
        lastw, readers = {}, {}
        preds = [None] * N
        succs = [[] for _ in range(N)]
        for idx, nd in enumerate(nodes):
            if nd[0] == "fence":
                rd, wr = (), list(nd[1]) + list(nd[2])
            else:
                rd, wr = nd[3], nd[4]
            p = set()
            for r in rd:
                w = lastw.get(r)
                if w is not None:
                    p.add(w)
                if PSUM_EXCL and isinstance(r, tuple) and r[0] == "pb":
                    for q in readers.get(r, ()):
                        if nodes[q][1] != nd[1]:
                            p.add(q)
            for w_ in wr:
                w = lastw.get(w_)
                if w is not None:
                    p.add(w)
                rs = readers.get(w_)
                if rs:
                    p.update(rs)
            p.discard(idx)
            preds[idx] = p
            for q in p:
                succs[q].append(idx)
            for r in rd:
                readers.setdefault(r, set()).add(idx)
            for w_ in wr:
                lastw[w_] = idx
                readers[w_] = set()
        if KEEP_ORDER:
            last_e = {}
            for idx, nd in enumerate(nodes):
                if nd[0] == "fence":
                    continue
                e = nd[1] if nd[0] == "op" else "dma_" + nd[1]
                if e in KEEP_ORDER:
                    q = last_e.get(e)
                    if q is not None and q not in preds[idx]:
                        preds[idx].add(q)
                        succs[q].append(idx)
                    last_e[e] = idx
        occ = [0.0] * N
        lat = [0.0] * N
        engs = [None] * N
        aset = [0] * N
        for i, nd in enumerate(nodes):
            if nd[0] != "fence":
                occ[i], lat[i] = self._est(nd)
                engs[i] = nd[1]
                if nd[0] == "op" and nd[1] == "act":
                    f_ = nd[2][2].get("func")
                    if f_ == AF.Silu:
                        aset[i] = 1
                    elif f_ == AF.Exp or f_ == AF.Ln:
                        aset[i] = 2
                    elif f_ == AF.Sigmoid:
                        aset[i] = 3
        cur_set = [0]
        TBL = 1.3
        blev = [0.0] * N
        for i in range(N - 1, -1, -1):
            b = 0.0
            for q in succs[i]:
                if blev[q] > b:
                    b = blev[q]
            blev[i] = b + lat[i]
        LAT = 0.15
        indeg = [len(p) for p in preds]
        finish = [0.0] * N
        eng_free = {e: 0.0 for e in ENGS}
        pending = {e: [] for e in ENGS}
        avail = {e: [] for e in ENGS}
        order = []

        def ready_time(i):
            t = 0.0
            for q in preds[i]:
                l = 0.0 if (engs[q] == "pe" and engs[i] == "pe") or engs[q] is None else LAT
                if finish[q] + l > t:
                    t = finish[q] + l
            return t

        def release(i):
            stack = [i]
            while stack:
                j = stack.pop()
                for q in succs[j]:
                    indeg[q] -= 1
                    if indeg[q] == 0:
                        rt = ready_time(q)
                        if engs[q] is None:
                            finish[q] = rt
                            order.append(q)
                            stack.append(q)
                        else:
                            heapq.heappush(pending[engs[q]], (rt, q))

        for i in range(N):
            if indeg[i] == 0:
                if engs[i] is None:
                    finish[i] = 0.0
                    order.append(i)
                    release(i)
                else:
                    heapq.heappush(pending[engs[i]], (0.0, i))
        nsched = sum(1 for e in engs if e is not None)
        done = 0
        while done < nsched:
            best = None
            for e in ENGS:
                pe_, av = pending[e], avail[e]
                while pe_ and pe_[0][0] <= eng_free[e]:
                    rt, q = heapq.heappop(pe_)
                    heapq.heappush(av, (-blev[q], q))
                if av:
                    if e == "act" and TABLE_AWARE:
                        pick = None
                        for it in heapq.nsmallest(8, av):
                            if aset[it[1]] == 0 or aset[it[1]] == cur_set[0]:
                                pick = it
                                break
                        if pick is None:
                            pick = av[0]
                            cand = (eng_free[e] + TBL, pick[0], pick[1], e, pick)
                        else:
                            cand = (eng_free[e], pick[0], pick[1], e, pick)
                    else:
                        cand = (eng_free[e], av[0][0], av[0][1], e, True)
                elif pe_:
                    cand = (pe_[0][0], -blev[pe_[0][1]], pe_[0][1], e, False)
                else:
                    continue
                if best is None or cand[:3] < best[:3]:
                    best = cand
            st, _, i, e, from_av = best
            if from_av is True:
                heapq.heappop(avail[e])
            elif from_av is False:
                heapq.heappop(pending[e])
            else:
                avail[e].remove(from_av)
                heapq.heapify(avail[e])
            if e == "act" and aset[i] != 0:
                if not from_av and aset[i] != cur_set[0] and TABLE_AWARE:
                    st += TBL
                cur_set[0] = aset[i]
            finish[i] = st + lat[i]
            eng_free[e] = st + occ[i]
            order.append(i)
            done += 1
            release(i)
        assert len(order) == N, (len(order), N)
        if HYBRID_K is not None:
            pre = order[:HYBRID_K]
            ps_ = set(pre)
            order = pre + [i for i in range(N) if i not in ps_]
            self.hyb_info = (N, [nodes[i][:2] + (nodes[i][2][0],) if nodes[i][0] != "fence" else ("fence",) for i in order[max(0, HYBRID_K - 3):HYBRID_K + 1]])
        self.sched_makespan = getattr(self, "sched_makespan", 0.0) + (max(finish) if finish else 0.0)
        for i in order:
            nd = nodes[i]
            if nd[0] == "fence":
                self._fence(nd[1], nd[2])
            elif nd[0] == "op":
                name, a, kw = nd[2]
                self._op(nd[1], lambda e, name=name, a=a, kw=kw: getattr(e, name)(*a, **kw), nd[3], nd[4])
            else:
                name, a, kw = nd[2]
                self._dma(nd[1], lambda e, name=name, a=a, kw=kw: getattr(e, name)(*a, **kw), nd[3], nd[4], nd[5])
        self.rec = True

    def replay(self, e, engobj):
        for item in self.q[e]:
            if item[0] == "wait":
                engobj.wait_ge(self.sems[item[1]], item[2])
            elif item[0] == "op":
                item[1](engobj).then_inc(self.sems[e], 1)
            else:
                item[1](engobj).then_inc(self.sems[item[2]], 16)

    def run_block(self):
        return
        nc = self.nc
        with nc.Block() as block:
            @block.sync
            def _(e):
                self.replay("sp", e)

            @block.tensor
            def _(e):
                self.replay("pe", e)

            @block.scalar
            def _(e):
                self.replay("act", e)

            @block.vector
            def _(e):
                self.replay("dve", e)

            @block.gpsimd
            def _(e):
                self.replay("pool", e)


class Rot:
    def __init__(self, c, name, n, shape, dt):
        self.t = [c.sb("%s%d" % (name, i), shape, dt) for i in range(n)]
        self.name = name
        self.gen = [0] * n
        self.i = 0

    def get(self):
        i = self.i
        self.i = (i + 1) % len(self.t)
        self.gen[i] += 1
        return H(self.t[i], (self.name, i), self, i, self.gen[i])


DBG_SKIP = set()


def build(stage=99, debug=False):
    nc = bass.Bass("TRN2", target_bir_lowering=False)
    SK = DBG_SKIP
    G2CUT = 99
    for f_ in SK:
        if f_.startswith('cut'):
            G2CUT = int(f_[3:])

    def din(name, shape):
        return nc.dram_tensor(name, list(shape), F32, kind="ExternalInput").ap()

    def dout(name, shape):
        return nc.dram_tensor(name, list(shape), F32, kind="ExternalOutput").ap()

    xin = din("xin", [SEQ + NS, D])
    sret_in = din("sret", [NS, 4, 128, 128])
    sgdn_in = din("sgdn", [NS, 4, 128, 128])
    sconv_in = din("sconv", [NS, 3, 1536])
    W = {}
    for nm, shp in [("ffn1_w_gate", [D, DFF]), ("ffn1_w_up", [D, DFF]), ("ffn1_w_down", [DFF, D]),
                    ("w_in", [D, DIN]), ("w_ret_branch", [512, D]), ("w_gdn_branch", [512, D]),
                    ("w_out", [D, D]),
                    ("ffn2_w_gate", [D, DFF]), ("ffn2_w_up", [D, DFF]), ("ffn2_w_down", [DFF, D])]:
        W[nm] = din(nm, shp)
    w_in = W["w_in"]
    gpre_in = din("gpre", [128, 24])
    gpost_in = din("gpost", [3, D])
    rng_in = din("ret_norm_g", [1, 512])
    gng_in = din("gdn_norm_g", [1, 128])
    convw_in = din("convw", [128, 12, 4])
    alog_in = din("a_log", [1, 4])
    dtb_in = din("dt_bias", [1, 4])
    cs_in = din("cossin", [128, 17, 3, 64])
    cf_in = din("constf", [128, NCF, 128])
    rt_in = din("rettab", [128, 2, 4, 128])
    kdec_in = din("kdec", [128, 4])

    y_out = dout("y", [SEQ + NS, D])
    nsr_p = dout("nsr_p", [4, 128, 128])
    nsg_p = dout("nsg_p", [4, 128, 128])
    nsc_p = dout("nsc_p", [3, 1536])
    nsr_s = dout("nsr_s", [NS, 4, 128, 128])
    nsg_s = dout("nsg_s", [NS, 4, 128, 128])
    nsc_s = dout("nsc_s", [NS, 3, 1536])

    with ExitStack() as es:
        c = Ctx(nc, es)
        es.enter_context(nc.allow_non_contiguous_dma(reason="tiny strided state outputs"))
        NTB = 7
        CB = 784
        SC0 = 768
        SI = 6
        x1 = c.sb("x1", [128, NTB, D], F32)
        hT = c.sb("hT", [128, 8, CB], BF16)
        S0 = c.sb("S0", [128, NJ * CB], BF16)
        aT = S0[:, :].rearrange("p (j t) -> p j t", j=NJ)
        o_rT = S0[:, 0:4 * CB].rearrange("p (h t) -> p h t", h=4)
        o_gT = S0[:, 4 * CB:8 * CB].rearrange("p (h t) -> p h t", h=4)
        cqT = S0[:, 8 * CB:20 * CB].rearrange("p (s t) -> p s t", s=12)
        yT = S0[:, 8 * CB:16 * CB].rearrange("p (k t) -> p k t", k=8)
        NSLOT = 7
        ring = [c.sb("ring%d" % i, [128, 8, 512], BF16) for i in range(NSLOT)]
        wsmall = c.sb("wsmall", [128, 8, 8], BF16)
        gp = c.sb("gp", [128, D], F32)
        gpre = c.sb("gpre_sb", [128, 24], F32)
        cs_t = c.sb("cs_t", [128, NTB, 3, 64], F32)
        cf = c.sb("cf", [128, NCF, 128], F32)
        cb = c.sb("cb", [128, NCF, 128], BF16)
        rettab = c.sb("rettab_sb", [128, 2, 4, 128], F32)
        kdec = c.sb("kdec_sb", [128, 4], F32)
        rng_t = c.sb("rng_t", [128, 512], F32)
        gng_t = c.sb("gng_t", [128, 128], F32)
        convw = c.sb("convw_sb", [128, 12, 4], F32)
        alog_t = c.sb("alog_t", [128, 4], F32)
        dtb_t = c.sb("dtb_t", [128, 4], F32)
        nega_t = c.sb("nega_t", [128, 4], F32)
        nhalf = c.sb("nhalf", [128, 4], F32)
        epst = c.sb("epst", [128, 4], F32)
        Sret = c.sb("Sret", [128, 4, 128], F32)
        Sretb = c.sb("Sretb", [128, 4, 128], BF16)
        Sgdn = c.sb("Sgdn", [128, 4, 128], F32)
        Sgdnb = c.sb("Sgdnb", [128, 4, 128], BF16)
        car = c.sb("car", [128, 12, 3], F32)
        scb = c.sb("scb", [128, 12, 3, NS], F32)
        gsc = c.sb("gsc", [128, NTB, 12], F32)
        s_qm = c.sb("s_qm", [128, 4, NS, NS], BF16)
        s_km = c.sb("s_km", [128, 4, NS, NS], BF16)
        s_os = c.sb("s_os", [128, 512], F32)
        s_vs = c.sb("s_vs", [128, 512], F32)
        s_ks = c.sb("s_ks", [128, 512], BF16)
        s_abc = c.sb("s_abc", [128, NS, 4], F32)
        s_gate = c.sb("s_gate", [128, 512], BF16)
        rF = Rot(c, "rF", 9, [128, 520], F32)
        rB = Rot(c, "rB", 12, [128, 512], BF16)
        rBB = Rot(c, "rBB", 2, [128, 1024], BF16)
        rS = Rot(c, "rS", 24, [128, 16], F32)
        pb = [c.ps("pb%d" % i, [128, 512], F32) for i in range(8)]
        pbb = [p.bitcast(BF16) for p in pb]
        bank_i = [0]
        bank_gen = [0] * 8

        class BK(int):
            pass

        def bank():
            b = bank_i[0]
            bank_i[0] = (b + 1) % 8
            bank_gen[b] += 1
            r = BK(b)
            r.gen = bank_gen[b]
            return r

        IDF, TRI, SU, MASKS, ONES, DI0, DI1 = 0, 1, 2, 3, 4, 5, 6
        def PB(b):
            assert bank_gen[int(b)] == b.gen, "stale psum bank %d" % int(b)
            return ("pb", int(b))

        def v4(ap):
            return ap.rearrange("p (h e) -> p h e", h=4)

        def hb(h):
            return slice(h * 128, (h + 1) * 128)

        def ld(dst, src, key):
            c.dma("sp", lambda e: e.dma_start(out=dst, in_=src), writes=[key])

        ld(gpre[:], gpre_in, "gpre")
        ld(cf[:], cf_in, "cf")
        ld(rettab[:], rt_in, "rettab")
        ld(kdec[:], kdec_in, "kdec")
        ld(convw[:], convw_in, "convw")
        ld(rng_t[:], rng_in.partition_broadcast(128), "rng")
        ld(gng_t[:], gng_in.partition_broadcast(128), "gng")
        ld(alog_t[:], alog_in.partition_broadcast(128), "alog")
        ld(dtb_t[:], dtb_in.partition_broadcast(128), "dtb")
        c.op("dve", lambda e: e.tensor_copy(out=cb[:], in_=cf[:]), reads=["cf"], writes=["cb"])
        c.op("pool", lambda e: e.memset(nhalf[:], -0.5), writes=["nhalf"])
        c.op("pool", lambda e: e.memset(epst[:], EPS), writes=["epst"])
        c.op("act", lambda e: e.activation(out=nega_t[:], in_=alog_t[:], func=AF.Exp), reads=["alog"], writes=["nega"])
        c.op("dve", lambda e: e.tensor_scalar(out=nega_t[:], in0=nega_t[:], scalar1=-1.0, scalar2=None, op0=ALU.mult),
             reads=["nega"], writes=["nega"])
        for t_, k_ in ((Sret, "Sret"), (Sgdn, "Sgdn"), (Sretb, "Sretb"), (Sgdnb, "Sgdnb"), (car, "car")):
            c.op("pool", lambda e, t_=t_: e.memset(t_[:], 0.0), writes=[k_])
        identb = cb[:, IDF, :]

        wstate = {"n": 0}

        wload_after = [()]

        def wload(src, r0, kc, c0, ncols):
            s = wstate["n"] % NSLOT
            wstate["n"] += 1
            t, key = ring[s], ("ring", s)
            v = src[r0 * 128:(r0 + kc) * 128, c0:c0 + ncols].rearrange("(c p) n -> p c n", p=128)
            c.dma("pool", lambda e: e.dma_start(out=t[:, 0:kc, 0:ncols], in_=v), reads=list(wload_after[0]), writes=[key])
            return t, key

        def mm8(b, nrows_out, ncols, lhsT_fn, rhs_fn, reads, nk=8):
            for k in range(nk):
                lt, rt_ = lhsT_fn(k), rhs_fn(k)
                c.op("pe", lambda e, k=k, lt=lt, rt_=rt_: e.matmul(pb[b][:nrows_out, 0:ncols], lhsT=lt, rhs=rt_,
                                                                  start=(k == 0), stop=(k == nk - 1)),
                     reads=reads, writes=[PB(b)])

        def prenormA(i, nrows, c0, gidx):
            kx = ("x1", i)
            jt = rBB.get()
            ss = rS.get()
            c.op("act", lambda e: e.activation(out=jt.t[:nrows, :], in_=x1[:nrows, i, :], func=AF.Square,
                                               accum_out=ss.t[:nrows, 0:1]), reads=[kx], writes=[jt, ss])
            c.op("dve", lambda e: e.tensor_scalar(out=ss.t[:nrows, 1:2], in0=ss.t[:nrows, 0:1], scalar1=1.0 / D,
                                                  scalar2=EPS, op0=ALU.mult, op1=ALU.add), reads=[ss], writes=[ss])
            c.op("pool", lambda e: e.tensor_tensor(out=ss.t[:nrows, 2:3], in0=ss.t[:nrows, 1:2], in1=nhalf[:nrows, 0:1],
                                                   op=ALU.pow), reads=[ss, "nhalf"], writes=[ss])
            hn = rBB.get()
            c.op("dve", lambda e: e.tensor_scalar(out=hn.t[:nrows, :], in0=x1[:nrows, i, :], scalar1=ss.t[:nrows, 2:3],
                                                  scalar2=None, op0=ALU.mult), reads=[kx, ss], writes=[hn])

            def partB():
                b = bank()
                for k in range(8):
                    c.op("pe", lambda e, k=k: e.transpose(out=pbb[b][:, k * 128:k * 128 + nrows],
                                                          in_=hn.t[:nrows, k * 128:(k + 1) * 128],
                                                          identity=cb[:nrows, IDF, :nrows]),
                         reads=[hn, "cb"], writes=[PB(b)])
                src = pbb[b][:, :].rearrange("p (k t) -> p k t", k=8)[:, :, 0:nrows]
                gb = gpre[:, gidx * 8:(gidx + 1) * 8].unsqueeze(2).broadcast_to([128, 8, nrows])
                c.op("dve", lambda e: e.tensor_tensor(out=hT[:, :, c0:c0 + nrows], in0=src, in1=gb, op=ALU.mult),
                     reads=[PB(b), "gpre"], writes=[("hT", i)])
            return partB

        def prenorm(i, nrows, c0, gidx):
            prenormA(i, nrows, c0, gidx)()

        def load_gp(row):
            c.dma("sp", lambda e: e.dma_start(out=gp[:], in_=gpost_in[row:row + 1, :].partition_broadcast(128)),
                  writes=["gp"])

        def post(i, nrows, b0, b1, scale, final_row0=None):
            ss = rS.get()
            jt = rBB.get()
            c.op("act", lambda e: e.activation(out=jt.t[:nrows, 0:512], in_=pb[b0][:nrows, :], func=AF.Square,
                                               accum_out=ss.t[:nrows, 0:1]), reads=[PB(b0)], writes=[jt, ss])
            c.op("act", lambda e: e.activation(out=jt.t[:nrows, 512:1024], in_=pb[b1][:nrows, :], func=AF.Square,
                                               accum_out=ss.t[:nrows, 1:2]), reads=[PB(b1)], writes=[jt, ss])
            c.op("dve", lambda e: e.tensor_tensor(out=ss.t[:nrows, 2:3], in0=ss.t[:nrows, 0:1], in1=ss.t[:nrows, 1:2],
                                                  op=ALU.add), reads=[ss], writes=[ss])
            m = 1.0 / (scale * scale)
            c.op("dve", lambda e: e.tensor_scalar(out=ss.t[:nrows, 3:4], in0=ss.t[:nrows, 2:3], scalar1=m / D,
                                                  scalar2=EPS * m, op0=ALU.mult, op1=ALU.add), reads=[ss], writes=[ss])
            c.op("pool", lambda e: e.tensor_tensor(out=ss.t[:nrows, 4:5], in0=ss.t[:nrows, 3:4], in1=nhalf[:nrows, 0:1],
                                                   op=ALU.pow), reads=[ss, "nhalf"], writes=[ss])
            for hf, bb in ((0, b0), (1, b1)):
                t = rF.get()
                c.op("dve", lambda e, t=t, bb=bb, hf=hf: e.scalar_tensor_tensor(
                    out=t.t[:nrows, 0:512], in0=pb[bb][:nrows, :], scalar=ss.t[:nrows, 4:5], op0=ALU.mult,
                    in1=gp[:nrows, hf * 512:(hf + 1) * 512], op1=ALU.mult),
                    reads=[PB(bb), ss, "gp"], writes=[t])
                c.op("pool", lambda e, t=t, hf=hf: e.tensor_tensor(
                    out=x1[:nrows, i, hf * 512:(hf + 1) * 512], in0=x1[:nrows, i, hf * 512:(hf + 1) * 512],
                    in1=t.t[:nrows, 0:512], op=ALU.add), reads=[t, ("x1", i)], writes=[("x1", i)])
            if final_row0 is not None:
                c.dma("sp", lambda e: e.dma_start(out=y_out[final_row0:final_row0 + nrows, :], in_=x1[:nrows, i, :]),
                      reads=[("x1", i)], is_output=True)

        s0_ffn_keys = set()
        s0_mix_keys = set()

        def ffn(tiles, cgs, wg, wu, wd, gpost_row, final, after_tile=None):
            load_gp(gpost_row)
            ntl = (NJ + 3) // 4
            loaded = {}

            def load_gu(g):
                nj = min(4, NJ - g * 4)
                loaded[g] = (wload(wg, 0, 8, g * 512, nj * 128), wload(wu, 0, 8, g * 512, nj * 128), nj)

            load_gu(0)
            dts = []
            dspecs = [(hf, r0, kc) for hf in range(2) for (r0, kc) in ((0, 8), (8, 8), (16, 6))]
            for g in range(ntl):
                if g + 1 < ntl:
                    load_gu(g + 1)
                else:
                    for (hf, r0, kc) in dspecs[:NSLOT - 2]:
                        dts.append(wload(wd, r0, kc, hf * 512, 512))
                (gt, gk), (ut, uk), nj = loaded.pop(g)
                for jj in range(nj):
                    j = g * 4 + jj
                    bgs = [bank() for _ in cgs]
                    bus = [bank() for _ in cgs]
                    for (wt, wk, bks) in ((gt, gk, bgs), (ut, uk, bus)):
                        for k in range(8):
                            for ci, (c0, n, tl) in enumerate(cgs):
                                b = bks[ci]
                                c.op("pe", lambda e, k=k, b=b, c0=c0, n=n, wt=wt: e.matmul(
                                    pb[b][:, 0:n], lhsT=wt[:, k, jj * 128:(jj + 1) * 128], rhs=hT[:, k, c0:c0 + n],
                                    start=(k == 0), stop=(k == 7)),
                                    reads=[wk] + [("hT", t) for t in tl], writes=[PB(b)])
                    for ci, (c0, n, tl) in enumerate(cgs):
                        bg, bu = bgs[ci], bus[ci]
                        sg = rF.get()
                        c.op("act", lambda e, sg=sg, bg=bg, n=n: e.activation(out=sg.t[:, 0:n], in_=pb[bg][:, 0:n],
                                                                          func=AF.Silu),
                             reads=[PB(bg)], writes=[sg])
                        s0_ffn_keys.add(("aT", j, c0))
                        c.op("dve", lambda e, sg=sg, bu=bu, n=n, c0=c0, j=j: e.tensor_tensor(
                            out=aT[:, j, c0:c0 + n], in0=sg.t[:, 0:n], in1=pb[bu][:, 0:n], op=ALU.mult),
                            reads=[sg, PB(bu)], writes=[("aT", j, c0)])
            for (hf, r0, kc) in dspecs[NSLOT - 2:]:
                dts.append(wload(wd, r0, kc, hf * 512, 512))
            pend = [None]
            for (i, nrows, c0, row0) in tiles:
                bs = []
                cgc0 = [cc for (cc, n, tl) in cgs if cc <= c0 < cc + n][0]
                for hf in range(2):
                    b = bank()
                    bs.append(b)
                    for j in range(NJ):
                        dt_, dk_ = dts[hf * 3 + j // 8]
                        c.op("pe", lambda e, j=j, b=b, dt_=dt_: e.matmul(
                            pb[b][:nrows, :], lhsT=aT[:, j, c0:c0 + nrows], rhs=dt_[:, j % 8, :],
                            start=(j == 0), stop=(j == NJ - 1)),
                            reads=[dk_, ("aT", j, cgc0)], writes=[PB(b)])
                if pend[0] is not None:
                    pend[0]()
                    pend[0] = None
                post(i, nrows, bs[0], bs[1], 0.5, final_row0=(row0 if final else None))
                if after_tile is not None:
                    pend[0] = after_tile(i, nrows, c0)
            if pend[0] is not None:
                pend[0]()

        def rope(b, i, nrows, out_ap, out_key):
            q4 = pb[b][:nrows, :].rearrange("p (h two d) -> p h two d", h=4, two=2)
            t1 = rF.get()
            t1v = t1.t[:nrows, 0:512].rearrange("p (h two d) -> p h two d", h=4, two=2)
            cosb = cs_t[:nrows, i, 0, :].unsqueeze(1).unsqueeze(1).broadcast_to([nrows, 4, 2, 64])
            c.op("dve", lambda e: e.tensor_tensor(out=t1v, in0=q4, in1=cosb, op=ALU.mult),
                 reads=[PB(b), "cs_t"], writes=[t1])
            u = rF.get()
            uv = u.t[:nrows, 0:512].rearrange("p (h two d) -> p h two d", h=4, two=2)
            nsb = cs_t[:nrows, i, 2, :].unsqueeze(1).broadcast_to([nrows, 4, 64])
            sb_ = cs_t[:nrows, i, 1, :].unsqueeze(1).broadcast_to([nrows, 4, 64])
            c.op("dve", lambda e: e.tensor_tensor(out=uv[:, :, 0, :], in0=q4[:, :, 1, :], in1=nsb, op=ALU.mult),
                 reads=[PB(b), "cs_t"], writes=[u])
            c.op("dve", lambda e: e.tensor_tensor(out=uv[:, :, 1, :], in0=q4[:, :, 0, :], in1=sb_, op=ALU.mult),
                 reads=[PB(b), "cs_t", u], writes=[u])
            c.op("pool", lambda e: e.tensor_tensor(out=out_ap, in0=t1.t[:nrows, 0:512], in1=u.t[:nrows, 0:512],
                                                   op=ALU.add), reads=[t1, u], writes=[out_key])

        def ret_proj(i, nrows, c0, Ts):
            bs = []
            for (T, kT) in Ts:
                b = bank()
                bs.append(b)
                mm8(b, nrows, 512, lambda k: hT[:, k, c0:c0 + nrows], lambda k, T=T: T[:, k, :], [kT, ("hT", i)])
            return bs

        def onorm_tail(i, nrows, c0, on, gate, dstT, dkey, gtab, gkey, gbc, defer=False):
            if gbc:
                gi = gtab[:nrows, :].unsqueeze(1).broadcast_to([nrows, 4, 128])
                c.op("pool", lambda e: e.tensor_tensor(out=v4(on.t[:nrows, 0:512]), in0=v4(on.t[:nrows, 0:512]), in1=gi,
                                                       op=ALU.mult), reads=[on, gkey], writes=[on])
            else:
                c.op("pool", lambda e: e.tensor_tensor(out=on.t[:nrows, 0:512], in0=on.t[:nrows, 0:512],
                                                       in1=gtab[:nrows, :], op=ALU.mult), reads=[on, gkey], writes=[on])
            orn = rB.get()
            g_ap = s_gate[:nrows, :] if gate is None else gate.t[:nrows, :]
            g_k = "s_gate" if gate is None else gate
            c.op("pool", lambda e: e.tensor_tensor(out=orn.t[:nrows, :], in0=on.t[:nrows, 0:512], in1=g_ap,
                                                   op=ALU.mult), reads=[on, g_k], writes=[orn])
            s0_mix_keys.add((dkey, i))

            def partB():
                bt = bank()
                for h in range(4):
                    c.op("pe", lambda e, h=h: e.transpose(out=pbb[bt][:, h * 128:h * 128 + nrows], in_=orn.t[:nrows, hb(h)],
                                                          identity=cb[:nrows, IDF, :nrows]),
                         reads=[orn, "cb"], writes=[PB(bt)])
                src = pbb[bt][:, 0:512].rearrange("p (h t) -> p h t", h=4)[:, :, 0:nrows]
                c.op("act", lambda e: e.activation(out=dstT[:, :, c0:c0 + nrows], in_=src, func=AF.Copy),
                     reads=[PB(bt)], writes=[(dkey, i)])
            if defer:
                return partB
            partB()
            return None

        def groupnorm_ret(bo, nrows, src=None, skey=None):
            sm = rS.get()
            sm2 = rS.get()
            if src is None:
                src, skey = pb[bo][:nrows, :], PB(bo)
            c.op("dve", lambda e: e.tensor_reduce(out=sm.t[:nrows, 0:4], in_=v4(src), axis=AX.X, op=ALU.add),
                 reads=[skey], writes=[sm])
            sq = rF.get()
            c.op("act", lambda e: e.activation(out=sq.t[:nrows, 0:512], in_=src, func=AF.Square),
                 reads=[skey], writes=[sq])
            c.op("dve", lambda e: e.tensor_reduce(out=sm.t[:nrows, 4:8], in_=v4(sq.t[:nrows, 0:512]), axis=AX.X, op=ALU.add),
                 reads=[sq, sm], writes=[sm])
            c.op("dve", lambda e: e.tensor_scalar(out=sm.t[:nrows, 8:12], in0=sm.t[:nrows, 0:4], scalar1=1.0 / 128,
                                                  scalar2=None, op0=ALU.mult), reads=[sm], writes=[sm])
            c.op("dve", lambda e: e.tensor_tensor(out=sm.t[:nrows, 12:16], in0=sm.t[:nrows, 8:12], in1=sm.t[:nrows, 8:12],
                                                  op=ALU.mult), reads=[sm], writes=[sm])
            c.op("dve", lambda e: e.scalar_tensor_tensor(out=sm2.t[:nrows, 0:4], in0=sm.t[:nrows, 4:8], scalar=1.0 / 128,
                                                         op0=ALU.mult, in1=sm.t[:nrows, 12:16], op1=ALU.subtract),
                 reads=[sm], writes=[sm2])
            c.op("dve", lambda e: e.tensor_scalar(out=sm2.t[:nrows, 4:8], in0=sm2.t[:nrows, 0:4], scalar1=EPS, scalar2=None,
                                                  op0=ALU.add), reads=[sm2], writes=[sm2])
            c.op("pool", lambda e: e.tensor_tensor(out=sm2.t[:nrows, 8:12], in0=sm2.t[:nrows, 4:8], in1=nhalf[:nrows, 0:4],
                                                   op=ALU.pow), reads=[sm2, "nhalf"], writes=[sm2])
            on = rF.get()
            for h in range(4):
                c.op("dve", lambda e, h=h: e.tensor_scalar(out=on.t[:nrows, hb(h)], in0=src[:, hb(h)],
                                                           scalar1=sm.t[:nrows, 8 + h:9 + h], scalar2=sm2.t[:nrows, 8 + h:9 + h],
                                                           op0=ALU.subtract, op1=ALU.mult),
                     reads=[skey, sm, sm2, on] if h else [skey, sm, sm2], writes=[on])
            return on

        def rmsnorm_gdn(bo, nrows, src=None, skey=None):
            sq = rF.get()
            sm = rS.get()
            if src is None:
                src, skey = pb[bo][:nrows, :], PB(bo)
            c.op("act", lambda e: e.activation(out=sq.t[:nrows, 0:512], in_=src, func=AF.Square),
                 reads=[skey], writes=[sq])
            c.op("dve", lambda e: e.tensor_reduce(out=sm.t[:nrows, 0:4], in_=v4(sq.t[:nrows, 0:512]), axis=AX.X, op=ALU.add),
                 reads=[sq], writes=[sm])
            c.op("dve", lambda e: e.tensor_scalar(out=sm.t[:nrows, 4:8], in0=sm.t[:nrows, 0:4], scalar1=1.0 / 128,
                                                  scalar2=EPS, op0=ALU.mult, op1=ALU.add), reads=[sm], writes=[sm])
            c.op("pool", lambda e: e.tensor_tensor(out=sm.t[:nrows, 8:12], in0=sm.t[:nrows, 4:8], in1=nhalf[:nrows, 0:4],
                                                   op=ALU.pow), reads=[sm, "nhalf"], writes=[sm])
            on = rF.get()
            c.op("dve", lambda e: e.tensor_tensor(out=v4(on.t[:nrows, 0:512]), in0=v4(src),
                                                  in1=sm.t[:nrows, 8:12].unsqueeze(2).broadcast_to([nrows, 4, 128]),
                                                  op=ALU.mult), reads=[skey, sm], writes=[on])
            return on

        def phase_R(ptiles, with_sample, g1=None):
            Ts = [wload(w_in, 0, 8, cc, 512) for cc in (0, 512, 1024, 1536)]

            def pump(n, site="x"):
                if ("np_" + site) in SK:
                    return
                if g1 is not None:
                    for _ in range(n):
                        next(g1, None)
            pump(1)
            pendR = None
            for (i, nrows, c0, row0) in ptiles:
                bq, bk, bv, bg = ret_proj(i, 128, c0, Ts)
                if pendR is not None:
                    pendR()
                    pendR = None
                pump(1, "a")
                qr = rB.get()
                rope(bq, i, 128, qr.t[:, :], qr)
                kf = rF.get()
                rope(bk, i, 128, kf.t[:, 0:512], kf)
                krb = rB.get()
                c.op("act", lambda e: e.activation(out=krb.t[:, :], in_=kf.t[:, 0:512], func=AF.Copy), reads=[kf], writes=[krb])
                kd = rB.get()
                c.op("pool", lambda e: e.tensor_tensor(out=v4(kd.t[:, :]), in0=v4(kf.t[:, 0:512]),
                                                       in1=kdec[:, :].unsqueeze(2).broadcast_to([128, 4, 128]), op=ALU.mult),
                     reads=[kf, "kdec"], writes=[kd])
                vb = rB.get()
                c.op("act", lambda e: e.activation(out=vb.t[:, :], in_=pb[bv][:, :], func=AF.Copy), reads=[PB(bv)], writes=[vb])
                rgs = rB.get()
                c.op("act", lambda e: e.activation(out=rgs.t[:, :], in_=pb[bg][:, :], func=AF.Silu), reads=[PB(bg)], writes=[rgs])
                bt = bank()
                for h in range(4):
                    c.op("pe", lambda e, h=h: e.transpose(out=pbb[bt][:, hb(h)], in_=qr.t[:, hb(h)], identity=identb),
                         reads=[qr, "cb"], writes=[PB(bt)])
                for h in range(4):
                    c.op("pe", lambda e, h=h: e.transpose(out=pbb[bt][:, hb(4 + h)], in_=krb.t[:, hb(h)], identity=identb),
                         reads=[krb, "cb"], writes=[PB(bt)])
                qkT = rBB.get()
                c.op("act", lambda e: e.activation(out=qkT.t[:, :], in_=pbb[bt][:, 0:1024], func=AF.Copy),
                     reads=[PB(bt)], writes=[qkT])
                qg = rB.get()
                c.op("pool", lambda e: e.tensor_tensor(out=qg.t[:, :], in0=qkT.t[:, 0:512],
                                                       in1=rettab[:, 1, :, :].rearrange("p h i -> p (h i)"), op=ALU.mult),
                     reads=[qkT, "rettab"], writes=[qg])
                pump(5, "b")
                bsc = bank()
                for h in range(4):
                    c.op("pe", lambda e, h=h: e.matmul(pb[bsc][:, hb(h)], lhsT=qkT.t[:, hb(4 + h)], rhs=qkT.t[:, hb(h)],
                                                       start=True, stop=True), reads=[qkT], writes=[PB(bsc)])
                sT = rB.get()
                c.op("dve", lambda e: e.tensor_tensor(out=sT.t[:, :], in0=pb[bsc][:, :],
                                                      in1=rettab[:, 0, :, :].rearrange("p h i -> p (h i)"), op=ALU.mult),
                     reads=[PB(bsc), "rettab"], writes=[sT])
                bo = bank()
                for h in range(4):
                    c.op("pe", lambda e, h=h: e.matmul(pb[bo][:, hb(h)], lhsT=sT.t[:, hb(h)], rhs=vb.t[:, hb(h)],
                                                       start=True, stop=False), reads=[sT, vb], writes=[PB(bo)])
                    c.op("pe", lambda e, h=h: e.matmul(pb[bo][:, hb(h)], lhsT=qg.t[:, hb(h)], rhs=Sretb[:, h, :],
                                                       start=False, stop=True), reads=[qg, "Sretb"], writes=[PB(bo)])
                bS = bank()
                for h in range(4):
                    c.op("pe", lambda e, h=h: e.matmul(pb[bS][:, hb(h)], lhsT=kd.t[:, hb(h)], rhs=vb.t[:, hb(h)],
                                                       start=True, stop=True), reads=[kd, vb], writes=[PB(bS)])
                for h in range(4):
                    c.op("dve", lambda e, h=h: e.scalar_tensor_tensor(out=Sret[:, h, :], in0=Sret[:, h, :],
                                                                      scalar=float(GAMMA[h] ** 128), op0=ALU.mult,
                                                                      in1=pb[bS][:, hb(h)], op1=ALU.add),
                         reads=[PB(bS), "Sret"], writes=["Sret"])
                c.op("act", lambda e: e.activation(out=Sretb[:], in_=Sret[:], func=AF.Copy), reads=["Sret"], writes=["Sretb"])
                on = groupnorm_ret(bo, 128)
                pendR = onorm_tail(i, 128, c0, on, rgs, o_rT, "orT", rng_t, "rng", False, defer=True)
            if pendR is not None:
                pendR()
            if with_sample:
                c.region("Rs")
                sample_ret(Ts)
                c.region("R2")

        def build_masked(dst, dkey, srcT_fn, src_reads):
            di = cf[:, DI0:DI0 + 2, :].rearrange("p a (b m) -> p (a b) m", m=NS)
            for h in range(4):
                c.op("dve", lambda e, h=h: e.tensor_tensor(out=dst[:, h, :, :],
                                                           in0=srcT_fn(h).unsqueeze(1).broadcast_to([128, NS, NS]),
                                                           in1=di, op=ALU.mult),
                     reads=src_reads + ["cf"] + ([dkey] if h else []), writes=[dkey])

        def sample_state_update(h, tg, S0g, lhs_tok, ublk, a_scalar, a_bc, out_dram):
            bO = bank()
            c.op("pe", lambda e: e.matmul(pb[bO][:, :], lhsT=lhs_tok, rhs=ublk.t[:NS, :], start=True, stop=True),
                 reads=[ublk, "s_ks"], writes=[PB(bO)])
            if a_scalar is not None:
                c.op("dve", lambda e: e.scalar_tensor_tensor(out=S0g.t[:, 0:512], in0=S0g.t[:, 0:512], scalar=a_scalar,
                                                             op0=ALU.mult, in1=pb[bO][:, :], op1=ALU.add),
                     reads=[S0g, PB(bO)], writes=[S0g])
            else:
                c.op("dve", lambda e: e.tensor_tensor(out=v4(S0g.t[:, 0:512]), in0=v4(S0g.t[:, 0:512]), in1=a_bc,
                                                      op=ALU.mult), reads=[S0g, "s_abc"], writes=[S0g])
                c.op("dve", lambda e: e.tensor_tensor(out=S0g.t[:, 0:512], in0=S0g.t[:, 0:512], in1=pb[bO][:, :],
                                                      op=ALU.add), reads=[S0g, PB(bO)], writes=[S0g])
            c.dma("sp", lambda e: e.dma_start(out=out_dram[tg * 4:(tg + 1) * 4, h].rearrange("t d e -> d t e"),
                                              in_=v4(S0g.t[:, 0:512])), reads=[S0g], is_output=True)
            Snb = rB.get()
            c.op("act", lambda e: e.activation(out=Snb.t[:, :], in_=S0g.t[:, 0:512], func=AF.Copy), reads=[S0g], writes=[Snb])
            return Snb

        def make_ublk(u_ap, u_reads, tg):
            ub = rB.get()
            c.op("dve", lambda e: e.tensor_tensor(out=v4(ub.t[:NS, :]), in0=u_ap.unsqueeze(1).broadcast_to([NS, 4, 128]),
                                                  in1=cf[:NS, IDF, tg * 4:(tg + 1) * 4].unsqueeze(2).broadcast_to([NS, 4, 128]),
                                                  op=ALU.mult), reads=u_reads + ["cf"], writes=[ub])
            return ub

        def sample_ret(Ts):
            i, c0 = SI, SC0
            bq, bk, bv, bg = ret_proj(i, NS, c0, Ts)
            qr = rB.get()
            rope(bq, i, NS, qr.t[:NS, :], qr)
            kf = rF.get()
            rope(bk, i, NS, kf.t[:NS, 0:512], kf)
            c.op("act", lambda e: e.activation(out=s_ks[:NS, :], in_=kf.t[:NS, 0:512], func=AF.Copy, scale=float(DK ** -0.5)),
                 reads=[kf], writes=["s_ks"])
            c.op("act", lambda e: e.activation(out=s_vs[:NS, :], in_=pb[bv][:NS, :], func=AF.Copy), reads=[PB(bv)], writes=["s_vs"])
            c.op("act", lambda e: e.activation(out=s_gate[:NS, :], in_=pb[bg][:NS, :], func=AF.Silu), reads=[PB(bg)], writes=["s_gate"])
            bt = bank()
            for h in range(4):
                c.op("pe", lambda e, h=h: e.transpose(out=pbb[bt][:, h * NS:(h + 1) * NS], in_=qr.t[:NS, hb(h)],
                                                      identity=cb[:NS, IDF, :NS]), reads=[qr, "cb"], writes=[PB(bt)])
            qTs = rB.get()
            c.op("act", lambda e: e.activation(out=qTs.t[:, 0:4 * NS], in_=pbb[bt][:, 0:4 * NS], func=AF.Copy),
                 reads=[PB(bt)], writes=[qTs])
            build_masked(s_qm, "s_qm", lambda h: qTs.t[:, h * NS:(h + 1) * NS], [qTs])
            for h in range(4):
                snbs = []
                for tg in range(4):
                    S0g = rF.get()
                    c.dma("sp", lambda e, S0g=S0g, tg=tg: e.dma_start(
                        out=v4(S0g.t[:, 0:512]), in_=sret_in[tg * 4:(tg + 1) * 4, h].rearrange("t d e -> d t e")),
                        writes=[S0g])
                    ub = make_ublk(s_vs[:NS, hb(h)], ["s_vs"], tg)
                    snbs.append(sample_state_update(h, tg, S0g, s_ks[:NS, hb(h)], ub, float(GAMMA[h]), None, nsr_s))
                bQ = bank()
                for t in range(NS):
                    c.op("pe", lambda e, t=t: e.matmul(pb[bQ][:NS, 0:128], lhsT=s_qm[:, h, t, :],
                                                       rhs=snbs[t // 4].t[:, hb(t % 4)], start=(t == 0), stop=(t == NS - 1)),
                         reads=["s_qm", snbs[t // 4]], writes=[PB(bQ)])
                c.op("act", lambda e, h=h: e.activation(out=s_os[:NS, hb(h)], in_=pb[bQ][:NS, 0:128], func=AF.Copy),
                     reads=[PB(bQ), "s_os"], writes=["s_os"])
            on = groupnorm_ret(None, NS, s_os[:NS, :], "s_os")
            onorm_tail(i, NS, c0, on, None, o_rT, "orT", rng_t, "rng", False)

        def phase_G1(cgs_p, with_sample, last_block):
            g1w = [wload(w_in, 0, 8, 2048 + T * 512, 512) for T in range(3)]
            yield
            for T in range(3):
                Tt, kT = g1w[T]
                for cc in range(4):
                    s = T * 4 + cc
                    groups = [(c0, n, tl, False) for (c0, n, tl) in cgs_p]
                    if with_sample:
                        groups.append((SC0, NS, [SI], True))
                    for (c0, n, tl, is_s) in groups:
                        b = bank()
                        mm8(b, 128, n, lambda k: Tt[:, k, cc * 128:(cc + 1) * 128], lambda k: hT[:, k, c0:c0 + n],
                            [kT] + [("hT", t) for t in tl])
                        acc = rF.get()
                        if not is_s:
                            ub = rF.get()
                            c.op("pool", lambda e: e.tensor_copy(out=ub.t[:, 0:3], in_=car[:, s, :]), reads=["car"], writes=[ub])
                            c.op("act", lambda e: e.activation(out=ub.t[:, 3:3 + n], in_=pb[b][:, 0:n], func=AF.Copy),
                                 reads=[PB(b), ub], writes=[ub])
                            c.op("pool", lambda e: e.tensor_copy(out=car[:, s, :], in_=ub.t[:, n:n + 3]), reads=[ub, "car"],
                                 writes=["car"])
                            c.op("dve", lambda e: e.tensor_scalar(out=acc.t[:, 0:n], in0=ub.t[:, 3:3 + n], scalar1=convw[:, s, 3:4],
                                                                  scalar2=None, op0=ALU.mult), reads=[ub, "convw"], writes=[acc])
                            for tap in (2, 1, 0):
                                c.op("dve", lambda e, tap=tap: e.scalar_tensor_tensor(
                                    out=acc.t[:, 0:n], in0=ub.t[:, tap:tap + n], scalar=convw[:, s, tap:tap + 1], op0=ALU.mult,
                                    in1=acc.t[:, 0:n], op1=ALU.add), reads=[ub, "convw", acc], writes=[acc])
                        else:
                            us = rF.get()
                            c.op("pool", lambda e: e.memset(us.t[:, 0:128], 0.0), writes=[us])
                            c.op("act", lambda e: e.activation(out=us.t[:, 0:n], in_=pb[b][:, 0:n], func=AF.Copy),
                                 reads=[PB(b), us], writes=[us])
                            c.op("dve", lambda e: e.tensor_scalar(out=acc.t[:, 0:n], in0=us.t[:, 0:n], scalar1=convw[:, s, 3:4],
                                                                  scalar2=None, op0=ALU.mult), reads=[us, "convw"], writes=[acc])
                            for tap in (2, 1, 0):
                                c.op("dve", lambda e, tap=tap: e.scalar_tensor_tensor(
                                    out=acc.t[:, 0:n], in0=scb[:, s, tap, :], scalar=convw[:, s, tap:tap + 1], op0=ALU.mult,
                                    in1=acc.t[:, 0:n], op1=ALU.add), reads=["scb", "convw", acc], writes=[acc])
                            bt = bank()
                            c.op("pe", lambda e: e.transpose(out=pb[bt][:, 0:128], in_=us.t[:, 0:128], identity=cf[:, IDF, :]),
                                 reads=[us, "cf"], writes=[PB(bt)])
                            ut = rF.get()
                            c.op("act", lambda e: e.activation(out=ut.t[:NS, 0:128], in_=pb[bt][:NS, 0:128], func=AF.Copy),
                                 reads=[PB(bt)], writes=[ut])
                            c.dma("sp", lambda e: e.dma_start(out=nsc_s[:, 2, s * 128:(s + 1) * 128], in_=ut.t[:NS, 0:128]),
                                  reads=[ut], is_output=True)
                        ck = ("cqT", s, c0)
                        s0_mix_keys.add(ck)
                        if s >= 8:
                            c.op("act", lambda e: e.activation(out=cqT[:, s, c0:c0 + n], in_=acc.t[:, 0:n], func=AF.Silu),
                                 reads=[acc], writes=[ck])
                        else:
                            cs = rF.get()
                            c.op("act", lambda e: e.activation(out=cs.t[:, 0:n], in_=acc.t[:, 0:n], func=AF.Silu),
                                 reads=[acc], writes=[cs])
                            sqb = rB.get()
                            c.op("pool", lambda e: e.tensor_tensor(out=sqb.t[:, 0:n], in0=cs.t[:, 0:n], in1=cs.t[:, 0:n],
                                                                   op=ALU.mult), reads=[cs], writes=[sqb])
                            b2 = bank()
                            c.op("pe", lambda e: e.matmul(pb[b2][:, 0:n], lhsT=cb[:, ONES, :], rhs=sqb.t[:, 0:n],
                                                          start=True, stop=True), reads=[sqb, "cb"], writes=[PB(b2)])
                            rr = rF.get()
                            c.op("act", lambda e: e.activation(out=rr.t[:, 0:n], in_=pb[b2][:, 0:n], func=AF.Ln, bias=epst[:, 0:1]),
                                 reads=[PB(b2), "epst"], writes=[rr])
                            c.op("act", lambda e: e.activation(out=rr.t[:, 0:n], in_=rr.t[:, 0:n], func=AF.Exp, scale=-0.5),
                                 reads=[rr], writes=[rr])
                            sc_ = float(DK ** -0.5) if s < 4 else 1.0
                            c.op("dve", lambda e: e.scalar_tensor_tensor(out=cqT[:, s, c0:c0 + n], in0=cs.t[:, 0:n], scalar=sc_,
                                                                         op0=ALU.mult, in1=rr.t[:, 0:n], op1=ALU.mult),
                                 reads=[cs, rr], writes=[ck])
                        yield
            if last_block:
                for r in range(3):
                    cc_ = rF.get()
                    c.op("dve", lambda e, r=r: e.tensor_copy(out=cc_.t[:, 0:12], in_=car[:, :, r]), reads=["car"], writes=[cc_])
                    bt = bank()
                    c.op("pe", lambda e: e.transpose(out=pb[bt][:12, 0:128], in_=cc_.t[:, 0:12], identity=cf[:, IDF, :]),
                         reads=[cc_, "cf"], writes=[PB(bt)])
                    co = rF.get()
                    c.op("act", lambda e: e.activation(out=co.t[:12, 0:128], in_=pb[bt][:12, 0:128], func=AF.Copy),
                         reads=[PB(bt)], writes=[co])
                    c.dma("sp", lambda e, r=r: e.dma_start(out=nsc_p[r, :].rearrange("(s p) -> s p", p=128), in_=co.t[:12, 0:128]),
                          reads=[co], is_output=True)

        def gdn_scalars(i, nrows, c0):
            ba = bank()
            mm8(ba, nrows, 8, lambda k: hT[:, k, c0:c0 + nrows], lambda k: wsmall[:, k, :], ["wsmall", ("hT", i)])
            sA = rS.get()
            sB = rS.get()
            c.op("dve", lambda e: e.tensor_tensor(out=sA.t[:nrows, 0:4], in0=pb[ba][:nrows, 0:4], in1=dtb_t[:nrows, :],
                                                  op=ALU.add), reads=[PB(ba), "dtb"], writes=[sA])
            c.op("act", lambda e: e.activation(out=sA.t[:nrows, 4:8], in_=sA.t[:nrows, 0:4], func=AF.Abs), reads=[sA], writes=[sA])
            c.op("act", lambda e: e.activation(out=sA.t[:nrows, 8:12], in_=sA.t[:nrows, 4:8], func=AF.Exp, scale=-1.0),
                 reads=[sA], writes=[sA])
            c.op("dve", lambda e: e.tensor_scalar(out=sA.t[:nrows, 8:12], in0=sA.t[:nrows, 8:12], scalar1=1.0, scalar2=None,
                                                  op0=ALU.add), reads=[sA], writes=[sA])
            c.op("act", lambda e: e.activation(out=sA.t[:nrows, 8:12], in_=sA.t[:nrows, 8:12], func=AF.Ln),
                 reads=[sA], writes=[sA])
            c.op("dve", lambda e: e.tensor_scalar(out=sA.t[:nrows, 12:16], in0=sA.t[:nrows, 0:4], scalar1=0.0, scalar2=None,
                                                  op0=ALU.max), reads=[sA], writes=[sA])
            c.op("dve", lambda e: e.tensor_tensor(out=sB.t[:nrows, 0:4], in0=sA.t[:nrows, 12:16], in1=sA.t[:nrows, 8:12],
                                                  op=ALU.add), reads=[sA], writes=[sB])
            gk = ("gsc", i)
            c.op("dve", lambda e: e.tensor_tensor(out=gsc[:nrows, i, 0:4], in0=sB.t[:nrows, 0:4], in1=nega_t[:nrows, :],
                                                  op=ALU.mult), reads=[sB, "nega"], writes=[gk])
            c.op("act", lambda e: e.activation(out=sB.t[:nrows, 4:8], in_=pb[ba][:nrows, 4:8], func=AF.Exp, scale=-1.0),
                 reads=[PB(ba), sB], writes=[sB])
            c.op("dve", lambda e: e.tensor_scalar(out=sB.t[:nrows, 8:12], in0=sB.t[:nrows, 4:8], scalar1=1.0, scalar2=None,
                                                  op0=ALU.add), reads=[sB], writes=[sB])
            c.op("dve", lambda e: e.reciprocal(out=gsc[:nrows, i, 4:8], in_=sB.t[:nrows, 8:12]), reads=[sB, gk], writes=[gk])
            c.op("dve", lambda e: e.tensor_scalar(out=gsc[:nrows, i, 8:12], in0=gsc[:nrows, i, 4:8], scalar1=-1.0, scalar2=None,
                                                  op0=ALU.mult), reads=[gk], writes=[gk])
            return gk, ba

        def phase_G2(ptiles, with_sample):
            wsf = rF.get()
            c.dma("sp", lambda e: e.dma_start(out=wsf.t[:, 0:64].rearrange("p (c n) -> p c n", n=8),
                                              in_=w_in[:, 4096:4104].rearrange("(c p) n -> p c n", p=128)), writes=[wsf])
            c.op("dve", lambda e: e.tensor_copy(out=wsmall[:, :, :], in_=wsf.t[:, 0:64].rearrange("p (c n) -> p c n", n=8)),
                 reads=[wsf], writes=["wsmall"])
            T7, k7 = wload(w_in, 0, 8, 3584, 512)
            pendG = None
            for (i, nrows, c0, row0) in ptiles:
                gk, bG = gdn_scalars(i, 128, c0)
                if G2CUT <= 1:
                    continue
                for col, m in ((16, TRI), (20, SU), (24, ONES)):
                    c.op("pe", lambda e, col=col, m=m: e.matmul(pb[bG][:, col:col + 4], lhsT=cf[:, m, :], rhs=gsc[:, i, 0:4],
                                                                start=True, stop=True), reads=[gk, "cf"], writes=[PB(bG)])
                ex = rS.get()
                c.op("act", lambda e: e.activation(out=ex.t[:, 0:12], in_=pb[bG][:, 16:28], func=AF.Exp), reads=[PB(bG)], writes=[ex])
                gsu = rF.get()
                c.op("pool", lambda e: e.tensor_tensor(out=v4(gsu.t[:, 0:512]),
                                                       in0=cf[:, SU, :].unsqueeze(1).broadcast_to([128, 4, 128]),
                                                       in1=gsc[:, i, 0:4].unsqueeze(2).broadcast_to([128, 4, 128]), op=ALU.mult),
                     reads=[gk, "cf"], writes=[gsu])
                bD = bank()
                for h in range(4):
                    c.op("pe", lambda e, h=h: e.matmul(pb[bD][:, hb(h)], lhsT=gsu.t[:, hb(h)], rhs=cf[:, TRI, :],
                                                       start=True, stop=True), reads=[gsu, "cf"], writes=[PB(bD)])
                E = rF.get()
                c.op("act", lambda e: e.activation(out=E.t[:, 0:512], in_=pb[bD][:, :], func=AF.Exp), reads=[PB(bD)], writes=[E])
                EMS = rF.get()
                c.op("pool", lambda e: e.tensor_tensor(out=v4(EMS.t[:, 0:512]), in0=v4(E.t[:, 0:512]),
                                                       in1=cf[:, MASKS, :].unsqueeze(1).broadcast_to([128, 4, 128]), op=ALU.mult),
                     reads=[E, "cf"], writes=[EMS])
                c.op("pool", lambda e: e.tensor_tensor(out=v4(E.t[:, 0:512]), in0=v4(E.t[:, 0:512]),
                                                       in1=cf[:, TRI, :].unsqueeze(1).broadcast_to([128, 4, 128]), op=ALU.mult),
                     reads=[E, "cf"], writes=[E])
                kq_reads = [("cqT", s, cc) for s in range(8) for cc in [cg0 for cg0 in cq_cg0(c0)]]
                if G2CUT <= 2:
                    continue
                if pendG is not None:
                    pendG()
                    pendG = None
                bK = bank()
                for h in range(4):
                    c.op("pe", lambda e, h=h: e.matmul(pb[bK][:, hb(h)], lhsT=cqT[:, 4 + h, c0:c0 + 128], rhs=cqT[:, 4 + h, c0:c0 + 128],
                                                       start=True, stop=True), reads=kq_reads, writes=[PB(bK)])
                Y = rB.get()
                for h in range(4):
                    c.op("dve", lambda e, h=h: e.scalar_tensor_tensor(out=Y.t[:, hb(h)], in0=pb[bK][:, hb(h)],
                                                                      scalar=gsc[:, i, 8 + h:9 + h], op0=ALU.mult,
                                                                      in1=EMS.t[:, hb(h)], op1=ALU.mult),
                         reads=[PB(bK), gk, EMS] + ([Y] if h else []), writes=[Y])
                if G2CUT <= 3:
                    continue
                bX = bank()
                for h in range(4):
                    c.op("pe", lambda e, h=h: e.transpose(out=pbb[bX][:, hb(h)], in_=Y.t[:, hb(h)], identity=identb),
                         reads=[Y, "cb"], writes=[PB(bX)])
                X = rB.get()
                c.op("act", lambda e: e.activation(out=X.t[:, :], in_=pbb[bX][:, 0:512], func=AF.Copy), reads=[PB(bX)], writes=[X])
                PT = rB.get()
                c.op("pool", lambda e: e.tensor_tensor(out=v4(PT.t[:, :]), in0=v4(Y.t[:, :]),
                                                       in1=cb[:, IDF, :].unsqueeze(1).broadcast_to([128, 4, 128]), op=ALU.add),
                     reads=[Y, "cb"], writes=[PT])
                if G2CUT <= 4:
                    continue
                for step in range(6):
                    bXn = bank()
                    for h in range(4):
                        c.op("pe", lambda e, h=h, X=X, Y=Y: e.matmul(pb[bXn][:, hb(h)], lhsT=Y.t[:, hb(h)], rhs=X.t[:, hb(h)],
                                                                    start=True, stop=True), reads=[X, Y], writes=[PB(bXn)])
                    if step < 5:
                        bYn = bank()
                        for h in range(4):
                            c.op("pe", lambda e, h=h, X=X, Y=Y: e.matmul(pb[bYn][:, hb(h)], lhsT=X.t[:, hb(h)], rhs=Y.t[:, hb(h)],
                                                                        start=True, stop=True), reads=[X, Y], writes=[PB(bYn)])
                    Xn = rB.get()
                    c.op("act", lambda e, Xn=Xn: e.activation(out=Xn.t[:, :], in_=pb[bXn][:, :], func=AF.Copy),
                         reads=[PB(bXn)], writes=[Xn])
                    if step < 5:
                        Yn = rB.get()
                        c.op("dve", lambda e, Yn=Yn: e.tensor_copy(out=Yn.t[:, :], in_=pb[bYn][:, :]), reads=[PB(bYn)], writes=[Yn])
                    bP = bank()
                    for h in range(4):
                        c.op("pe", lambda e, h=h, Xn=Xn, PT=PT: e.matmul(pb[bP][:, hb(h)], lhsT=Xn.t[:, hb(h)], rhs=PT.t[:, hb(h)],
                                                                        start=True, stop=True), reads=[Xn, PT], writes=[PB(bP)])
                    PTn = rB.get()
                    c.op("dve", lambda e, PTn=PTn, PT=PT: e.tensor_tensor(out=PTn.t[:, :], in0=pb[bP][:, :], in1=PT.t[:, :],
                                                                         op=ALU.add), reads=[PB(bP), PT], writes=[PTn])
                    X, PT = Xn, PTn
                    if step < 5:
                        Y = Yn
                if G2CUT <= 5:
                    continue
                bQ = bank()
                for h in range(4):
                    c.op("pe", lambda e, h=h: e.matmul(pb[bQ][:, hb(h)], lhsT=cqT[:, 4 + h, c0:c0 + 128], rhs=cqT[:, h, c0:c0 + 128],
                                                       start=True, stop=True), reads=kq_reads, writes=[PB(bQ)])
                qkm = rB.get()
                c.op("dve", lambda e: e.tensor_tensor(out=qkm.t[:, :], in0=pb[bQ][:, :], in1=E.t[:, 0:512], op=ALU.mult),
                     reads=[PB(bQ), E], writes=[qkm])
                if 'suba' in SK:
                    continue
                bT = bank()
                kv_reads = [("cqT", s, cg0) for s in range(4, 12) for cg0 in cq_cg0(c0)]
                for h in range(4):
                    c.op("pe", lambda e, h=h: e.transpose(out=pbb[bT][:, hb(h)], in_=cqT[:, 4 + h, c0:c0 + 128], identity=identb),
                         reads=kv_reads + ["cb"], writes=[PB(bT)])
                for h in range(4):
                    c.op("pe", lambda e, h=h: e.transpose(out=pbb[bT][:, hb(4 + h)], in_=cqT[:, 8 + h, c0:c0 + 128], identity=identb),
                         reads=kv_reads + ["cb"], writes=[PB(bT)])
                if 'subb' in SK:
                    continue
                kg = rB.get()
                c.op("dve", lambda e: e.tensor_tensor(out=v4(kg.t[:, :]), in0=v4(pbb[bT][:, 0:512]),
                                                      in1=ex.t[:, 0:4].unsqueeze(2).broadcast_to([128, 4, 128]), op=ALU.mult),
                     reads=[PB(bT), ex], writes=[kg])
                if 'subc' in SK:
                    continue
                kd = rB.get()
                c.op("dve", lambda e: e.tensor_tensor(out=v4(kd.t[:, :]), in0=v4(pbb[bT][:, 0:512]),
                                                      in1=ex.t[:, 4:8].unsqueeze(2).broadcast_to([128, 4, 128]), op=ALU.mult),
                     reads=[PB(bT), ex], writes=[kd])
                if 'subd' in SK:
                    continue
                vt = rB.get()
                c.op("act", lambda e: e.activation(out=vt.t[:, :], in_=pbb[bT][:, 512:1024], func=AF.Copy), reads=[PB(bT)], writes=[vt])
                if G2CUT <= 6:
                    continue
                bW = bank()
                for h in range(4):
                    c.op("pe", lambda e, h=h: e.matmul(pb[bW][:, hb(h)], lhsT=kg.t[:, hb(h)], rhs=PT.t[:, hb(h)],
                                                       start=True, stop=True), reads=[kg, PT], writes=[PB(bW)])
                NW = rB.get()
                c.op("act", lambda e: e.activation(out=NW.t[:, :], in_=pb[bW][:, :], func=AF.Copy, scale=-1.0),
                     reads=[PB(bW)], writes=[NW])
                if G2CUT <= 7:
                    continue
                bU = bank()
                for h in range(4):
                    c.op("pe", lambda e, h=h: e.matmul(pb[bU][:, hb(h)], lhsT=PT.t[:, hb(h)], rhs=vt.t[:, hb(h)],
                                                       start=True, stop=False), reads=[PT, vt], writes=[PB(bU)])
                    c.op("pe", lambda e, h=h: e.matmul(pb[bU][:, hb(h)], lhsT=NW.t[:, hb(h)], rhs=Sgdnb[:, h, :],
                                                       start=False, stop=True), reads=[NW, "Sgdnb"], writes=[PB(bU)])
                U = rB.get()
                c.op("dve", lambda e: e.tensor_tensor(out=v4(U.t[:, :]), in0=v4(pb[bU][:, :]),
                                                      in1=gsc[:, i, 4:8].unsqueeze(2).broadcast_to([128, 4, 128]), op=ALU.mult),
                     reads=[PB(bU), gk], writes=[U])
                if G2CUT <= 8:
                    continue
                gbc = rF.get()
                c.op("pool", lambda e: e.tensor_copy(out=v4(gbc.t[:, 0:512]), in_=gsc[:, i, 0:4].unsqueeze(2).broadcast_to([128, 4, 128])),
                     reads=[gk], writes=[gbc])
                bR = bank()
                for h in range(4):
                    c.op("pe", lambda e, h=h: e.matmul(pb[bR][:, hb(h)], lhsT=gbc.t[:, hb(h)],
                                                       rhs=cf[:, TRI, :], start=True, stop=True), reads=[gbc, "cf"], writes=[PB(bR)])
                EG = rF.get()
                c.op("act", lambda e: e.activation(out=EG.t[:, 0:512], in_=pb[bR][:, :], func=AF.Exp), reads=[PB(bR)], writes=[EG])
                qg = rB.get()
                c.op("pool", lambda e: e.tensor_tensor(out=v4(qg.t[:, :]), in0=cqT[:, 0:4, c0:c0 + 128], in1=v4(EG.t[:, 0:512]),
                                                       op=ALU.mult), reads=kq_reads + [EG], writes=[qg])
                if G2CUT <= 9:
                    continue
                bO = bank()
                for h in range(4):
                    c.op("pe", lambda e, h=h: e.matmul(pb[bO][:, hb(h)], lhsT=qg.t[:, hb(h)], rhs=Sgdnb[:, h, :],
                                                       start=True, stop=False), reads=[qg, "Sgdnb"], writes=[PB(bO)])
                    c.op("pe", lambda e, h=h: e.matmul(pb[bO][:, hb(h)], lhsT=qkm.t[:, hb(h)], rhs=U.t[:, hb(h)],
                                                       start=False, stop=True), reads=[qkm, U], writes=[PB(bO)])
                if G2CUT <= 10:
                    continue
                bS = bank()
                for h in range(4):
                    c.op("pe", lambda e, h=h: e.matmul(pb[bS][:, hb(h)], lhsT=kd.t[:, hb(h)], rhs=U.t[:, hb(h)],
                                                       start=True, stop=True), reads=[kd, U], writes=[PB(bS)])
                for h in range(4):
                    c.op("dve", lambda e, h=h: e.scalar_tensor_tensor(out=Sgdn[:, h, :], in0=Sgdn[:, h, :], scalar=ex.t[:, 8 + h:9 + h],
                                                                      op0=ALU.mult, in1=pb[bS][:, hb(h)], op1=ALU.add),
                         reads=[PB(bS), "Sgdn", ex], writes=["Sgdn"])
                c.op("act", lambda e: e.activation(out=Sgdnb[:], in_=Sgdn[:], func=AF.Copy), reads=["Sgdn"], writes=["Sgdnb"])
                if G2CUT <= 11:
                    continue
                bz = bank()
                mm8(bz, 128, 512, lambda k: hT[:, k, c0:c0 + 128], lambda k: T7[:, k, :], [k7, ("hT", i)])
                gzs = rB.get()
                c.op("act", lambda e: e.activation(out=gzs.t[:, :], in_=pb[bz][:, :], func=AF.Silu), reads=[PB(bz)], writes=[gzs])
                on = rmsnorm_gdn(bO, 128)
                pendG = onorm_tail(i, 128, c0, on, gzs, o_gT, "ogT", gng_t, "gng", True, defer=True)
            if pendG is not None:
                pendG()
            if with_sample:
                sample_gdn(T7, k7)

        def cq_cg0(c0):
            return [0 if c0 < 512 else 512]

        not_sample_block = [False]

        def sample_gdn(T7, k7):
            i, c0 = SI, SC0
            gk, _ba = gdn_scalars(i, NS, c0)
            sa = rS.get()
            c.op("act", lambda e: e.activation(out=sa.t[:NS, 0:4], in_=gsc[:NS, i, 0:4], func=AF.Exp), reads=[gk], writes=[sa])
            ad = rF.get()
            c.op("pool", lambda e: e.memset(ad.t[:, 0:NS * 4], 0.0), writes=[ad])
            c.op("dve", lambda e: e.tensor_tensor(out=ad.t[:NS, 0:NS * 4].rearrange("p (t h) -> p t h", h=4),
                                                  in0=sa.t[:NS, 0:4].unsqueeze(1).broadcast_to([NS, NS, 4]),
                                                  in1=cf[:NS, IDF, :NS].unsqueeze(2).broadcast_to([NS, NS, 4]), op=ALU.mult),
                 reads=[sa, "cf", ad], writes=[ad])
            ba = bank()
            c.op("pe", lambda e: e.matmul(pb[ba][:, 0:NS * 4], lhsT=cf[:, ONES, :], rhs=ad.t[:, 0:NS * 4], start=True, stop=True),
                 reads=[ad, "cf"], writes=[PB(ba)])
            c.op("act", lambda e: e.activation(out=s_abc[:].rearrange("p t h -> p (t h)"), in_=pb[ba][:, 0:NS * 4], func=AF.Copy),
                 reads=[PB(ba)], writes=["s_abc"])
            cq_reads = [("cqT", s, SC0) for s in range(12)]
            bT = bank()
            for h in range(4):
                c.op("pe", lambda e, h=h: e.transpose(out=pbb[bT][:NS, hb(h)], in_=cqT[:, 4 + h, c0:c0 + NS], identity=identb),
                     reads=cq_reads + ["cb"], writes=[PB(bT)])
            bT2 = bank()
            for h in range(4):
                c.op("pe", lambda e, h=h: e.transpose(out=pbb[bT2][:NS, hb(h)], in_=cqT[:, 8 + h, c0:c0 + NS], identity=identb),
                     reads=cq_reads + ["cb"], writes=[PB(bT2)])
            c.op("act", lambda e: e.activation(out=s_ks[:NS, :], in_=pbb[bT][:NS, 0:512], func=AF.Copy), reads=[PB(bT)], writes=["s_ks"])
            c.op("act", lambda e: e.activation(out=s_vs[:NS, :], in_=pbb[bT2][:NS, 0:512], func=AF.Copy), reads=[PB(bT2)], writes=["s_vs"])
            build_masked(s_km, "s_km", lambda h: cqT[:, 4 + h, c0:c0 + NS], cq_reads)
            build_masked(s_qm, "s_qm", lambda h: cqT[:, h, c0:c0 + NS], cq_reads)
            for h in range(4):
                S0gs, S0bs = [], []
                for tg in range(4):
                    S0g = rF.get()
                    c.dma("sp", lambda e, S0g=S0g, tg=tg: e.dma_start(
                        out=v4(S0g.t[:, 0:512]), in_=sgdn_in[tg * 4:(tg + 1) * 4, h].rearrange("t d e -> d t e")),
                        writes=[S0g])
                    S0b = rB.get()
                    c.op("act", lambda e, S0b=S0b, S0g=S0g: e.activation(out=S0b.t[:, :], in_=S0g.t[:, 0:512], func=AF.Copy),
                         reads=[S0g], writes=[S0b])
                    S0gs.append(S0g)
                    S0bs.append(S0b)
                bK = bank()
                for t in range(NS):
                    c.op("pe", lambda e, t=t: e.matmul(pb[bK][:NS, 0:128], lhsT=s_km[:, h, t, :], rhs=S0bs[t // 4].t[:, hb(t % 4)],
                                                       start=(t == 0), stop=(t == NS - 1)), reads=["s_km", S0bs[t // 4]], writes=[PB(bK)])
                uu = rF.get()
                c.op("dve", lambda e: e.scalar_tensor_tensor(out=uu.t[:NS, 0:128], in0=pb[bK][:NS, 0:128], scalar=sa.t[:NS, h:h + 1],
                                                             op0=ALU.mult, in1=s_vs[:NS, hb(h)], op1=ALU.subtract),
                     reads=[PB(bK), sa, "s_vs"], writes=[uu])
                c.op("dve", lambda e: e.tensor_scalar(out=uu.t[:NS, 0:128], in0=uu.t[:NS, 0:128], scalar1=gsc[:NS, i, 8 + h:9 + h],
                                                      scalar2=None, op0=ALU.mult), reads=[uu, gk], writes=[uu])
                snbs = []
                for tg in range(4):
                    ub = make_ublk(uu.t[:NS, 0:128], [uu], tg)
                    a_bc = s_abc[:, tg * 4:(tg + 1) * 4, h].unsqueeze(2).broadcast_to([128, 4, 128])
                    snbs.append(sample_state_update(h, tg, S0gs[tg], s_ks[:NS, hb(h)], ub, None, a_bc, nsg_s))
                bQ = bank()
                for t in range(NS):
                    c.op("pe", lambda e, t=t: e.matmul(pb[bQ][:NS, 0:128], lhsT=s_qm[:, h, t, :], rhs=snbs[t // 4].t[:, hb(t % 4)],
                                                       start=(t == 0), stop=(t == NS - 1)), reads=["s_qm", snbs[t // 4]], writes=[PB(bQ)])
                c.op("act", lambda e, h=h: e.activation(out=s_os[:NS, hb(h)], in_=pb[bQ][:NS, 0:128], func=AF.Copy),
                     reads=[PB(bQ), "s_os"], writes=["s_os"])
            bz = bank()
            mm8(bz, NS, 512, lambda k: hT[:, k, c0:c0 + NS], lambda k: T7[:, k, :], [k7, ("hT", i)])
            c.op("act", lambda e: e.activation(out=s_gate[:NS, :], in_=pb[bz][:NS, :], func=AF.Silu), reads=[PB(bz)], writes=["s_gate"])
            on = rmsnorm_gdn(None, NS, s_os[:NS, :], "s_os")
            onorm_tail(i, NS, c0, on, None, o_gT, "ogT", gng_t, "gng", True)

        def phase_M(tiles, cgs, after_tile=None):
            load_gp(1)
            yk = []
            for half in range(2):
                Tgr = wload(w_in, 0, 8, 4104 + half * 512, 512)
                Tgg = wload(w_in, 0, 8, 5128 + half * 512, 512)
                Trb = wload(W["w_ret_branch"], 0, 4, half * 512, 512)
                Tgb = wload(W["w_gdn_branch"], 0, 4, half * 512, 512)
                for cc in range(4):
                    ch = half * 4 + cc
                    for (c0, n, tl) in cgs:
                        hk = [("hT", t) for t in tl]
                        b1, b2, b3, b4 = bank(), bank(), bank(), bank()
                        mm8(b1, 128, n, lambda k: Tgr[0][:, k, cc * 128:(cc + 1) * 128], lambda k: hT[:, k, c0:c0 + n], [Tgr[1]] + hk)
                        mm8(b2, 128, n, lambda k: Tgg[0][:, k, cc * 128:(cc + 1) * 128], lambda k: hT[:, k, c0:c0 + n], [Tgg[1]] + hk)
                        mm8(b3, 128, n, lambda k: Trb[0][:, k, cc * 128:(cc + 1) * 128], lambda k: o_rT[:, k, c0:c0 + n],
                            [Trb[1]] + [("orT", t) for t in tl], nk=4)
                        mm8(b4, 128, n, lambda k: Tgb[0][:, k, cc * 128:(cc + 1) * 128], lambda k: o_gT[:, k, c0:c0 + n],
                            [Tgb[1]] + [("ogT", t) for t in tl], nk=4)
                        s1 = rF.get()
                        c.op("act", lambda e: e.activation(out=s1.t[:, 0:n], in_=pb[b1][:, 0:n], func=AF.Sigmoid), reads=[PB(b1)], writes=[s1])
                        s2 = rF.get()
                        c.op("act", lambda e: e.activation(out=s2.t[:, 0:n], in_=pb[b2][:, 0:n], func=AF.Sigmoid), reads=[PB(b2)], writes=[s2])
                        c.op("dve", lambda e: e.tensor_tensor(out=s1.t[:, 0:n], in0=s1.t[:, 0:n], in1=pb[b3][:, 0:n], op=ALU.mult),
                             reads=[s1, PB(b3)], writes=[s1])
                        c.op("dve", lambda e: e.tensor_tensor(out=s2.t[:, 0:n], in0=s2.t[:, 0:n], in1=pb[b4][:, 0:n], op=ALU.mult),
                             reads=[s2, PB(b4)], writes=[s2])
                        ykey = ("yT", ch, c0)
                        s0_mix_keys.add(ykey)
                        c.op("pool", lambda e: e.tensor_tensor(out=yT[:, ch, c0:c0 + n], in0=s1.t[:, 0:n], in1=s2.t[:, 0:n], op=ALU.add),
                             reads=[s1, s2], writes=[ykey])
            Wo = [wload(W["w_out"], 0, 8, hf * 512, 512) for hf in range(2)]
            pend = [None]
            for (i, nrows, c0, row0) in tiles:
                cgc0 = [cc for (cc, n, tl) in cgs if cc <= c0 < cc + n][0]
                bs = []
                for hf in range(2):
                    b = bank()
                    bs.append(b)
                    mm8(b, nrows, 512, lambda k: yT[:, k, c0:c0 + nrows], lambda k: Wo[hf][0][:, k, :],
                        [Wo[hf][1]] + [("yT", k_, cgc0) for k_ in range(8)])
                if pend[0] is not None:
                    pend[0]()
                    pend[0] = None
                post(i, nrows, bs[0], bs[1], 1.0)
                if after_tile is not None:
                    pend[0] = after_tile(i, nrows, c0)
            if pend[0] is not None:
                pend[0]()

        def mk_tiles(g0, n, sample):
            t = [(i, 128, i * 128, (g0 + i) * 128) for i in range(n)]
            if sample:
                t.append((SI, NS, SC0, SEQ))
            return t

        blocks = [
            dict(g0=0, n=6, sample=True, cgs=[(0, 512, [0, 1, 2, 3]), (512, 256 + NS, [4, 5, SI])],
                 cgs_p=[(0, 512, [0, 1, 2, 3]), (512, 256, [4, 5])]),
            dict(g0=6, n=6, sample=False, cgs=[(0, 512, [0, 1, 2, 3]), (512, 256, [4, 5])],
                 cgs_p=[(0, 512, [0, 1, 2, 3]), (512, 256, [4, 5])]),
            dict(g0=12, n=4, sample=False, cgs=[(0, 512, [0, 1, 2, 3])], cgs_p=[(0, 512, [0, 1, 2, 3])]),
        ]
        if "oneblock" in SK:
            blocks = blocks[:1]
        for bi, blk in enumerate(blocks):
            tiles = mk_tiles(blk["g0"], blk["n"], blk["sample"])
            ptiles = [t for t in tiles if t[1] == 128]
            cgs = blk["cgs"]
            cgs_p = blk["cgs_p"]
            CQC = (0, 512, SC0)
            not_sample_block[0] = not blk["sample"]
            if bi == 0 or stage < 99:
                for (i, nrows, c0, row0) in tiles:
                    c.dma("sp", lambda e, i=i, nrows=nrows, row0=row0: e.dma_start(out=x1[:nrows, i, :],
                                                                              in_=xin[row0:row0 + nrows, :]),
                          writes=[("x1", i)])
            c.dma("sp", lambda e: e.dma_start(out=cs_t[:, 0:blk["n"]], in_=cs_in[:, blk["g0"]:blk["g0"] + blk["n"]]), writes=["cs_t"])
            if blk["sample"]:
                c.dma("sp", lambda e: e.dma_start(out=cs_t[:, SI], in_=cs_in[:, 16]), reads=["cs_t"], writes=["cs_t"])
            OVL = stage == 99
            c.region("ffn1")
            if "noffn1" not in SK:
                if bi == 0 or not OVL:
                    for (i, nrows, c0, row0) in tiles:
                        prenorm(i, nrows, c0, 0)
                if bi == 0:
                    wload_after[0] = [("x1", t_[0]) for t_ in tiles]
                ffn(tiles, cgs, W["ffn1_w_gate"], W["ffn1_w_up"], W["ffn1_w_down"], 0, final=(stage == 1),
                    after_tile=(lambda i, nrows, c0: prenormA(i, nrows, c0, 1)) if OVL else None)
                wload_after[0] = ()
            if stage == 1:
                continue
            c.fence(list(s0_mix_keys) + [("orT", t[0]) for t in tiles] + [("ogT", t[0]) for t in tiles]
                    + [("cqT", s, cc_) for s in range(12) for cc_ in CQC], list(s0_ffn_keys))
            if not OVL:
                for (i, nrows, c0, row0) in tiles:
                    prenorm(i, nrows, c0, 1)
            if blk["sample"] and "sample" not in SK and "nsconv" not in SK:
                for r in range(3):
                    stg = [rF.get() for _ in range(3)]
                    for q_ in range(3):
                        c.dma("sp", lambda e, r=r, q_=q_: e.dma_start(out=stg[q_].t[:NS, 0:512], in_=sconv_in[:, r, q_ * 512:(q_ + 1) * 512]),
                              writes=[stg[q_]])
                    bt = bank()
                    for s in range(12):
                        c.op("pe", lambda e, s=s: e.transpose(out=pb[bt][:, s * NS:(s + 1) * NS],
                                                              in_=stg[s // 4].t[:NS, (s % 4) * 128:(s % 4 + 1) * 128],
                                                              identity=cf[:NS, IDF, :NS]), reads=[stg[s // 4], "cf"], writes=[PB(bt)])
                    c.op("act", lambda e, r=r: e.activation(out=scb[:, :, r, :], in_=pb[bt][:, 0:12 * NS].rearrange("p (s t) -> p s t", s=12),
                                                            func=AF.Copy), reads=[PB(bt), "scb"] if r else [PB(bt)], writes=["scb"])
                c.dma("sp", lambda e: e.dma_start(out=nsc_s[:, 0:2, :], in_=sconv_in[:, 1:3, :]), is_output=True)
            smp = blk["sample"] and "sample" not in SK
            c.region("R")
            g1 = phase_G1(cgs_p, smp and "nsg1" not in SK, bi == len(blocks) - 1) if "G1" not in SK else iter(())
            if "R" not in SK:
                phase_R(ptiles, smp and "nsret" not in SK, g1 if OVL else None)
            for _ in g1:
                pass
            c.region("G2")
            if "G2" not in SK:
                phase_G2(ptiles[:1] if "G2one" in SK else ptiles, smp and "nsgdn" not in SK)
            c.fence([("yT", ch, g[0]) for ch in range(8) for g in cgs], [("cqT", s, cc_) for s in range(12) for cc_ in CQC])
            c.region("M")
            if "M" not in SK:
                phase_M(tiles, cgs, after_tile=(lambda i, nrows, c0: prenormA(i, nrows, c0, 2)) if OVL else None)
            if bi == len(blocks) - 1:
                c.dma("sp", lambda e: e.dma_start(out=nsr_p.rearrange("h d e -> d h e"), in_=Sret[:]), reads=["Sret"], is_output=True)
                c.dma("sp", lambda e: e.dma_start(out=nsg_p.rearrange("h d e -> d h e"), in_=Sgdn[:]), reads=["Sgdn"], is_output=True)
            if stage == 2:
                for (i, nrows, c0, row0) in tiles:
                    c.dma("sp", lambda e, i=i, nrows=nrows, row0=row0: e.dma_start(out=y_out[row0:row0 + nrows, :], in_=x1[:nrows, i, :]),
                          reads=[("x1", i)], is_output=True)
                c.fence(list(s0_ffn_keys), list(s0_mix_keys))
                continue
            c.region("ffn2")
            c.fence(list(s0_ffn_keys), list(s0_mix_keys))
            if not OVL:
                for (i, nrows, c0, row0) in tiles:
                    prenorm(i, nrows, c0, 2)
            nxt = blocks[bi + 1] if bi + 1 < len(blocks) else None
            ntiles = {t[0]: t for t in mk_tiles(nxt["g0"], nxt["n"], nxt["sample"])} if nxt else {}

            def next_block_prefetch(i, nrows, c0):
                if i in ntiles:
                    (i2, nrows2, c02, row02) = ntiles[i]
                    c.dma("sp", lambda e: e.dma_start(out=x1[:nrows2, i2, :], in_=xin[row02:row02 + nrows2, :]),
                          writes=[("x1", i2)])
                    return prenormA(i2, nrows2, c02, 0)
                return None

            ffn(tiles, cgs, W["ffn2_w_gate"], W["ffn2_w_up"], W["ffn2_w_down"], 2, final=True,
                after_tile=next_block_prefetch if OVL else None)

        c.finish("sp")
        c.run_block()
    return nc


def host_consts():
    f32 = np.float32
    p = np.arange(128)
    cf = np.zeros((128, NCF, 128), f32)
    cf[:, 0, :] = np.eye(128)
    cf[:, 1, :] = (p[:, None] <= p[None, :])
    cf[:, 2, :] = (p[:, None] > p[None, :])
    cf[:, 3, :] = (p[:, None] < p[None, :])
    cf[:, 4, :] = 1.0
    cf[:, 5:7, :] = np.eye(NS, dtype=f32).reshape(1, 2, 128)
    lg = np.log(np.array(GAMMA, np.float64))
    rt = np.zeros((128, 2, 4, 128), np.float64)
    for h in range(4):
        dmt = np.exp((p[None, :] - p[:, None]) * lg[h]) * (p[:, None] <= p[None, :]) * DK ** -0.5
        rt[:, 0, h, :] = dmt
        rt[:, 1, h, :] = np.exp((p[None, :] + 1) * lg[h])
    kdec = np.exp((127 - p[:, None]) * lg[None, :]) * DK ** -0.5
    inv = (10000.0 ** (-(np.arange(0, 128, 2, dtype=f32)) / f32(128))).astype(f32)
    pos = np.concatenate([np.arange(SEQ, dtype=f32), np.full((128,), 16384.0, f32)])
    ang = (pos[:, None] * inv[None, :]).astype(f32)
    cs = np.stack([np.cos(ang), np.sin(ang), -np.sin(ang)], axis=1).astype(f32)
    cs = cs.reshape(17, 128, 3, 64).transpose(1, 0, 2, 3)
    return dict(constf=cf, rettab=rt.astype(f32), kdec=kdec.astype(f32), cossin=np.ascontiguousarray(cs))


_CACHE = {}


def kernel(**inp):
    f32 = np.float32
    stage = inp.pop("_stage", 99)
    debug = inp.pop("_debug", False)
    ncores = inp.pop("_cores", 8)
    key = (stage, debug, tuple(sorted(DBG_SKIP)))
    if key not in _CACHE:
        _CACHE[key] = build(stage, debug)
    nc = _CACHE[key]
    hc = host_consts()
    g = lambda n: np.asarray(inp[n], f32)[0]
    shared = {}
    for nm in ["ffn1_w_gate", "ffn1_w_up", "ffn1_w_down", "w_in", "w_ret_branch", "w_gdn_branch", "w_out",
               "ffn2_w_gate", "ffn2_w_up", "ffn2_w_down"]:
        shared[nm] = np.ascontiguousarray(g(nm))
    gpre = np.concatenate([g(n).reshape(8, 128).T for n in ("ffn1_pre_g", "mix_pre_g", "ffn2_pre_g")], axis=1)
    shared["gpre"] = np.ascontiguousarray(gpre)
    shared["gpost"] = np.ascontiguousarray(np.stack([g("ffn1_post_g"), g("mix_post_g"), g("ffn2_post_g")]))
    shared["ret_norm_g"] = g("ret_norm_g").reshape(1, 512)
    shared["gdn_norm_g"] = g("gdn_norm_g").reshape(1, 128)
    cw = g("gdn_conv_w")
    shared["convw"] = np.ascontiguousarray(cw.reshape(4, 12, 128).transpose(2, 1, 0))
    shared["a_log"] = g("gdn_a_log").reshape(1, 4)
    shared["dt_bias"] = g("gdn_dt_bias").reshape(1, 4)
    shared.update(hc)
    xp = np.asarray(inp["x_prompt"], f32)
    xs = np.asarray(inp["x_sample"], f32)
    sr = np.asarray(inp["state_ret"], f32)[0]
    sg = np.asarray(inp["state_gdn"], f32)[0]
    sc = np.asarray(inp["state_conv"], f32)[0]
    in_maps = []
    for b in range(ncores):
        m = dict(shared)
        m["xin"] = np.ascontiguousarray(np.concatenate([xp[b], xs[b * NS:(b + 1) * NS, 0, :]], axis=0))
        m["sret"] = np.ascontiguousarray(sr[b * NS:(b + 1) * NS])
        m["sgdn"] = np.ascontiguousarray(sg[b * NS:(b + 1) * NS])
        m["sconv"] = np.ascontiguousarray(sc[b * NS:(b + 1) * NS])
        in_maps.append(m)
    res = run_bass_kernel_spmd(nc, in_maps, core_ids=list(range(ncores)))
    R = list(res.results) + [res.results[0]] * (8 - ncores)
    yp = np.stack([R[b]["y"][:SEQ] for b in range(8)])
    ys = np.concatenate([R[b]["y"][SEQ:] for b in range(8)])[:, None, :]
    nrp = np.stack([R[b]["nsr_p"] for b in range(8)])[None]
    ngp = np.stack([R[b]["nsg_p"] for b in range(8)])[None]
    ncp = np.stack([R[b]["nsc_p"] for b in range(8)])[None]
    nrs = np.concatenate([R[b]["nsr_s"] for b in range(8)])[None]
    ngs = np.concatenate([R[b]["nsg_s"] for b in range(8)])[None]
    ncs = np.concatenate([R[b]["nsc_s"] for b in range(8)])[None]
    out = (yp, ys, nrp, ngp, ncp, nrs, ngs, ncs)
    if debug:
        return out, [R[b]["dbg"] for b in range(8)]
    return tuple(np.ascontiguousarray(o, dtype=f32) for o in out)
```

```python
import numpy as np
from contextlib import ExitStack
import concourse.bass as bass
import concourse.mybir as mybir
from concourse.bass_utils import run_bass_kernel_spmd

F32 = mybir.dt.float32
BF16 = mybir.dt.bfloat16
AF = mybir.ActivationFunctionType
ALU = mybir.AluOpType
AX = mybir.AxisListType

ENGS = ("pe", "act", "dve", "pool", "sp")

D = 1024
DFF = 2816
NJ = 22
DIN = 6152
SEQ = 2048
NS = 16
EPS = 1e-6
GAMMA = [1.0 - 2.0 ** (-5.0 - h) for h in range(4)]
DK = 128
NCF = 7


class H:
    __slots__ = ("t", "key", "rot", "idx", "gen")

    def __init__(self, t, key, rot, idx, gen):
        self.t, self.key, self.rot, self.idx, self.gen = t, key, rot, idx, gen


def _res(keys):
    out = []
    for k in keys:
        if isinstance(k, H):
            assert k.rot.gen[k.idx] == k.gen, "stale scratch tile %s" % (k.key,)
            out.append(k.key)
        else:
            out.append(k)
    return out


class _Rec:
    def __init__(self):
        self.call = None

    def __getattr__(self, name):
        if name.startswith("__"):
            raise AttributeError(name)

        def f(*a, **kw):
            self.call = (name, a, kw)
            return self
        return f

    def then_inc(self, *a, **k):
        return self


def _free_size(ap):
    try:
        n = 1
        for d in ap.shape[1:]:
            n *= int(d)
        return n
    except Exception:
        return 256


SCHED = True
PSUM_EXCL = True
HYBRID_K = None
TABLE_AWARE = False
REGION_FLUSH = False
KEEP_ORDER = set()
SCHED_REGIONS = {"ffn1", "ffn2", "G2", "M", "R"}


class Ctx:
    NDSEM = 28

    def __init__(self, nc, es, same_eng_sync=("act", "dve", "pool")):
        self.nc = nc
        self.es = es
        self.same_eng_sync = same_eng_sync
        self.q = {e: [] for e in ENGS}
        self.cnt = {e: 0 for e in ENGS}
        self.waited = {e: {} for e in ENGS}
        self.sems = {}
        self.EPOCH = 2000
        for i in range(self.NDSEM):
            self.sems[("d", i)] = es.enter_context(nc.semaphore("s_d%d" % i))
        self.ndma = 0
        self.res = {}
        self.out_events = []
        self.rec = SCHED
        self.nodes = []

    def sb(self, name, shape, dt):
        return self.es.enter_context(self.nc.sbuf_tensor(name, list(shape), dt))

    def ps(self, name, shape, dt):
        return self.es.enter_context(self.nc.psum_tensor(name, list(shape), dt))

    def _deps(self, reads, writes, eng=None):
        deps = []
        for r in reads:
            st = self.res.get(r)
            if st and st["w"]:
                deps.append(st["w"])
            if st and PSUM_EXCL and isinstance(r, tuple) and r[0] == "pb":
                for ev in st["r"].items():
                    if ev[0][0] != eng:
                        deps.append(ev)
        for w in writes:
            st = self.res.get(w)
            if st:
                if st["w"]:
                    deps.append(st["w"])
                deps.extend(st["r"].items())
        return deps

    def _emit_waits(self, eng, deps):
        wd = self.waited[eng]
        best = {}
        for (k, v) in deps:
            if k[0] == eng and (eng == "pe" or eng not in self.same_eng_sync):
                continue
            if wd.get(k, 0) < v and best.get(k, 0) < v:
                best[k] = v
        for k, v in best.items():
            wd[k] = v
            self._eo(eng).wait_ge(self.sems[k], v)

    def _update(self, ev, reads, writes):
        for r in reads:
            st = self.res.setdefault(r, {"w": None, "r": {}})
            if st["r"].get(ev[0], 0) < ev[1]:
                st["r"][ev[0]] = ev[1]
        for w in writes:
            self.res[w] = {"w": ev, "r": {}}

    def fence(self, new_keys, old_keys):
        if self.rec:
            self.nodes.append(("fence", list(new_keys), list(old_keys)))
            return
        self._fence(new_keys, old_keys)

    def _fence(self, new_keys, old_keys):
        acc = {}
        for ok in old_keys:
            st = self.res.get(ok)
            if not st:
                continue
            evs = list(st["r"].items())
            if st["w"]:
                evs.append(st["w"])
            for k, v in evs:
                if acc.get(k, 0) < v:
                    acc[k] = v
        for nk in new_keys:
            st = self.res.setdefault(nk, {"w": None, "r": {}})
            for k, v in acc.items():
                if st["r"].get(k, 0) < v:
                    st["r"][k] = v

    def op(self, eng, fn, reads=(), writes=()):
        reads, writes = _res(reads), _res(writes)
        if self.rec:
            r = _Rec()
            fn(r)
            self.nodes.append(("op", eng, r.call, reads, writes))
            return None
        return self._op(eng, fn, reads, writes)

    def _op(self, eng, fn, reads, writes):
        deps = self._deps(reads, writes, eng)
        self._emit_waits(eng, deps)
        self.cnt[eng] += 1
        n = self.cnt[eng] - 1
        sk = (eng, n // self.EPOCH)
        if sk not in self.sems:
            self.sems[sk] = self.es.enter_context(self.nc.semaphore("s_%s_%d" % sk))
        ev = (sk, n % self.EPOCH + 1)
        fn(self._eo(eng)).then_inc(self.sems[sk], 1)
        self._update(ev, reads, writes)
        return ev

    def dma(self, queue, fn, reads=(), writes=(), is_output=False):
        reads, writes = _res(reads), _res(writes)
        if self.rec:
            r = _Rec()
            fn(r)
            self.nodes.append(("dma", queue, r.call, reads, writes, is_output))
            return None
        return self._dma(queue, fn, reads, writes, is_output)

    def _dma(self, queue, fn, reads, writes, is_output):
        n = self.ndma
        self.ndma += 1
        k = ("d", n % self.NDSEM)
        rnd = n // self.NDSEM
        deps = self._deps(reads, writes)
        if rnd > 0:
            deps.append((k, 16 * rnd))
        self._emit_waits(queue, deps)
        ev = (k, 16 * (rnd + 1))
        fn(self._eo(queue)).then_inc(self.sems[k], 16)
        self._update(ev, reads, writes)
        if is_output:
            self.out_events.append(ev)
        return ev

    def _eo(self, e):
        nc = self.nc
        return {"pe": nc.tensor, "act": nc.scalar, "dve": nc.vector, "pool": nc.gpsimd, "sp": nc.sync}[e]

    def region(self, name):
        if not REGION_FLUSH:
            return
        self.flush()
        self.rec = SCHED and (SCHED_REGIONS is None or name in SCHED_REGIONS)

    def finish(self, eng="sp"):
        self.flush()
        self._emit_waits(eng, self.out_events)

    def _est(self, nd):
        kind, eng, call = nd[0], nd[1], nd[2]
        name, a, kw = call
        if kind == "dma":
            out = kw.get("out", a[0] if a else None)
            nbytes = _free_size(out) * int(out.shape[0]) * 4 if out is not None else 65536
            lat = 2.0 + nbytes / 250e3
            return (1.0 if eng == "pool" else 0.06), lat
        if eng == "pe":
            if name == "transpose":
                return 0.1, 0.1
            rhs = kw.get("rhs")
            n = _free_size(rhs) if rhs is not None else 128
            d = 0.04 + n / 2600.0
            if rhs is not None and rhs.dtype == F32:
                d *= 4
            return d, d
        out = kw.get("out", a[0] if a else None)
        n = _free_size(out) if out is not None else 256
        if eng == "act":
            d = 0.30 + n / 1200.0
        elif eng == "dve":
            d = 0.16 + n / 1200.0
        else:
            d = 0.25 + n / 550.0
        return d, d

    def flush(self):
        import heapq
        nodes, self.nodes = self.nodes, []
        if not nodes:
            return
        self.rec = False
        N = len(nodes)
        lastw, readers = {}, {}
        preds = [None] * N
        succs = [[] for _ in range(N)]
        for idx, nd in enumerate(nodes):
            if nd[0] == "fence":
                rd, wr = (), list(nd[1]) + list(nd[2])
            else:
                rd, wr = nd[3], nd[4]
            p = set()
            for r in rd:
                w = lastw.get(r)
                if w is not None:
                    p.add(w)
                if PSUM_EXCL and isinstance(r, tuple) and r[0] == "pb":
                    for q in readers.get(r, ()):
                        if nodes[q][1] != nd[1]:
                            p.add(q)
            for w_ in wr:
                w = lastw.get(w_)
                if w is not None:
                    p.add(w)
                rs = readers.get(w_)
                if rs:
                    p.update(rs)
            p.discard(idx)
            preds[idx] = p
            for q in p:
                succs[q].append(idx)
            for r in rd:
                readers.setdefault(r, set()).add(idx)
            for w_ in wr:
                lastw[w_] = idx
                readers[w_] = set()
        if KEEP_ORDER:
            last_e = {}
            for idx, nd in enumerate(nodes):
                if nd[0] == "fence":
                    continue
                e = nd[1] if nd[0] == "op" else "dma_" + nd[1]
                if e in KEEP_ORDER:
                    q = last_e.get(e)
                    if q is not None and q not in preds[idx]:
                        preds[idx].add(q)
                        succs[q].append(idx)
                    last_e[e] = idx
        occ = [0.0] * N
        lat = [0.0] * N
        engs = [None] * N
        aset = [0] * N
        for i, nd in enumerate(nodes):
            if nd[0] != "fence":
                occ[i], lat[i] = self._est(nd)
                engs[i] = nd[1]
                if nd[0] == "op" and nd[1] == "act":
                    f_ = nd[2][2].get("func")
                    if f_ == AF.Silu:
                        aset[i] = 1
                    elif f_ == AF.Exp or f_ == AF.Ln:
                        aset[i] = 2
                    elif f_ == AF.Sigmoid:
                        aset[i] = 3
        cur_set = [0]
        TBL = 1.3
        blev = [0.0] * N
        for i in range(N - 1, -1, -1):
            b = 0.0
            for q in succs[i]:
                if blev[q] > b:
                    b = blev[q]
            blev[i] = b + lat[i]
        LAT = 0.15
        indeg = [len(p) for p in preds]
        finish = [0.0] * N
        eng_free = {e: 0.0 for e in ENGS}
        pending = {e: [] for e in ENGS}
        avail = {e: [] for e in ENGS}
        order = []

        def ready_time(i):
            t = 0.0
            for q in preds[i]:
                l = 0.0 if (engs[q] == "pe" and engs[i] == "pe") or engs[q] is None else LAT
                if finish[q] + l > t:
                    t = finish[q] + l
            return t

        def release(i):
            stack = [i]
            while stack:
                j = stack.pop()
                for q in succs[j]:
                    indeg[q] -= 1
                    if indeg[q] == 0:
                        rt = ready_time(q)
                        if engs[q] is None:
                            finish[q] = rt
                            order.append(q)
                            stack.append(q)
                        else:
                            heapq.heappush(pending[engs[q]], (rt, q))

        for i in range(N):
            if indeg[i] == 0:
                if engs[i] is None:
                    finish[i] = 0.0
                    order.append(i)
                    release(i)
                else:
                    heapq.heappush(pending[engs[i]], (0.0, i))
        nsched = sum(1 for e in engs if e is not None)
        done = 0
        while done < nsched:
            best = None
            for e in ENGS:
                pe_, av = pending[e], avail[e]
                while pe_ and pe_[0][0] <= eng_free[e]:
                    rt, q = heapq.heappop(pe_)
                    heapq.heappush(av, (-blev[q], q))
                if av:
                    if e == "act" and TABLE_AWARE:
                        pick = None
                        for it in heapq.nsmallest(8, av):
                            if aset[it[1]] == 0 or aset[it[1]] == cur_set[0]:
                                pick = it
                                break
                        if pick is None:
                            pick = av[0]
                            cand = (eng_free[e] + TBL, pick[0], pick[1], e, pick)
                        else:
                            cand = (eng_free[e], pick[0], pick[1], e, pick)
                    else:
                        cand = (eng_free[e], av[0][0], av[0][1], e, True)
                elif pe_:
                    cand = (pe_[0][0], -blev[pe_[0][1]], pe_[0][1], e, False)
                else:
                    continue
                if best is None or cand[:3] < best[:3]:
                    best = cand
            st, _, i, e, from_av = best
            if from_av is True:
                heapq.heappop(avail[e])
            elif from_av is False:
                heapq.heappop(pending[e])
            else:
                avail[e].remove(from_av)
                heapq.heapify(avail[e])
            if e == "act" and aset[i] != 0:
                if not from_av and aset[i] != cur_set[0] and TABLE_AWARE:
                    st += TBL
                cur_set[0] = aset[i]
            finish[i] = st + lat[i]
            eng_free[e] = st + occ[i]
            order.append(i)
            done += 1
            release(i)
        assert len(order) == N, (len(order), N)
        if HYBRID_K is not None:
            pre = order[:HYBRID_K]
            ps_ = set(pre)
            order = pre + [i for i in range(N) if i not in ps_]
            self.hyb_info = (N, [nodes[i][:2] + (nodes[i][2][0],) if nodes[i][0] != "fence" else ("fence",) for i in order[max(0, HYBRID_K - 3):HYBRID_K + 1]])
        self.sched_makespan = getattr(self, "sched_makespan", 0.0) + (max(finish) if finish else 0.0)
        for i in order:
            nd = nodes[i]
            if nd[0] == "fence":
                self._fence(nd[1], nd[2])
            elif nd[0] == "op":
                name, a, kw = nd[2]
                self._op(nd[1], lambda e, name=name, a=a, kw=kw: getattr(e, name)(*a, **kw), nd[3], nd[4])
            else:
                name, a, kw = nd[2]
                self._dma(nd[1], lambda e, name=name, a=a, kw=kw: getattr(e, name)(*a, **kw), nd[3], nd[4], nd[5])
        self.rec = True

    def replay(self, e, engobj):
        for item in self.q[e]:
            if item[0] == "wait":
                engobj.wait_ge(self.sems[item[1]], item[2])
            elif item[0] == "op":
                item[1](engobj).then_inc(self.sems[e], 1)
            else:
                item[1](engobj).then_inc(self.sems[item[2]], 16)

    def run_block(self):
        return
        nc = self.nc
        with nc.Block() as block:
            @block.sync
            def _(e):
                self.replay("sp", e)

            @block.tensor
            def _(e):
                self.replay("pe", e)

            @block.scalar
            def _(e):
                self.replay("act", e)

            @block.vector
            def _(e):
                self.replay("dve", e)

            @block.gpsimd
            def _(e):
                self.replay("pool", e)


class Rot:
    def __init__(self, c, name, n, shape, dt):
        self.t = [c.sb("%s%d" % (name, i), shape, dt) for i in range(n)]
        self.name = name
        self.gen = [0] * n
        self.i = 0

    def get(self):
        i = self.i
        self.i = (i + 1) % len(self.t)
        self.gen[i] += 1
        return H(self.t[i], (self.name, i), self, i, self.gen[i])


DBG_SKIP = set()


def build(stage=99, debug=False):
    nc = bass.Bass("TRN2", target_bir_lowering=False)
    SK = DBG_SKIP
    G2CUT = 99
    for f_ in SK:
        if f_.startswith('cut'):
            G2CUT = int(f_[3:])

    def din(name, shape):
        return nc.dram_tensor(name, list(shape), F32, kind="ExternalInput").ap()

    def dout(name, shape):
        return nc.dram_tensor(name, list(shape), F32, kind="ExternalOutput").ap()

    xin = din("xin", [SEQ + NS, D])
    sret_in = din("sret", [NS, 4, 128, 128])
    sgdn_in = din("sgdn", [NS, 4, 128, 128])
    sconv_in = din("sconv", [NS, 3, 1536])
    W = {}
    for nm, shp in [("ffn1_w_gate", [D, DFF]), ("ffn1_w_up", [D, DFF]), ("ffn1_w_down", [DFF, D]),
                    ("w_in", [D, DIN]), ("w_ret_branch", [512, D]), ("w_gdn_branch", [512, D]),
                    ("w_out", [D, D]),
                    ("ffn2_w_gate", [D, DFF]), ("ffn2_w_up", [D, DFF]), ("ffn2_w_down", [DFF, D])]:
        W[nm] = din(nm, shp)
    w_in = W["w_in"]
    gpre_in = din("gpre", [128, 24])
    gpost_in = din("gpost", [3, D])
    rng_in = din("ret_norm_g", [1, 512])
    gng_in = din("gdn_norm_g", [1, 128])
    convw_in = din("convw", [128, 12, 4])
    alog_in = din("a_log", [1, 4])
    dtb_in = din("dt_bias", [1, 4])
    cs_in = din("cossin", [128, 17, 3, 64])
    cf_in = din("constf", [128, NCF, 128])
    rt_in = din("rettab", [128, 2, 4, 128])
    kdec_in = din("kdec", [128, 4])

    y_out = dout("y", [SEQ + NS, D])
    nsr_p = dout("nsr_p", [4, 128, 128])
    nsg_p = dout("nsg_p", [4, 128, 128])
    nsc_p = dout("nsc_p", [3, 1536])
    nsr_s = dout("nsr_s", [NS, 4, 128, 128])
    nsg_s = dout("nsg_s", [NS, 4, 128, 128])
    nsc_s = dout("nsc_s", [NS, 3, 1536])

    with ExitStack() as es:
        c = Ctx(nc, es)
        es.enter_context(nc.allow_non_contiguous_dma(reason="tiny strided state outputs"))
        NTB = 7
        CB = 784
        SC0 = 768
        SI = 6
        x1 = c.sb("x1", [128, NTB, D], F32)
        hT = c.sb("hT", [128, 8, CB], BF16)
        S0 = c.sb("S0", [128, NJ * CB], BF16)
        aT = S0[:, :].rearrange("p (j t) -> p j t", j=NJ)
        o_rT = S0[:, 0:4 * CB].rearrange("p (h t) -> p h t", h=4)
        o_gT = S0[:, 4 * CB:8 * CB].rearrange("p (h t) -> p h t", h=4)
        cqT = S0[:, 8 * CB:20 * CB].rearrange("p (s t) -> p s t", s=12)
        yT = S0[:, 8 * CB:16 * CB].rearrange("p (k t) -> p k t", k=8)
        NSLOT = 7
        ring = [c.sb("ring%d" % i, [128, 8, 512], BF16) for i in range(NSLOT)]
        wsmall = c.sb("wsmall", [128, 8, 8], BF16)
        gp = c.sb("gp", [128, D], F32)
        gpre = c.sb("gpre_sb", [128, 24], F32)
        cs_t = c.sb("cs_t", [128, NTB, 3, 64], F32)
        cf = c.sb("cf", [128, NCF, 128], F32)
        cb = c.sb("cb", [128, NCF, 128], BF16)
        rettab = c.sb("rettab_sb", [128, 2, 4, 128], F32)
        kdec = c.sb("kdec_sb", [128, 4], F32)
        rng_t = c.sb("rng_t", [128, 512], F32)
        gng_t = c.sb("gng_t", [128, 128], F32)
        convw = c.sb("convw_sb", [128, 12, 4], F32)
        alog_t = c.sb("alog_t", [128, 4], F32)
        dtb_t = c.sb("dtb_t", [128, 4], F32)
        nega_t = c.sb("nega_t", [128, 4], F32)
        nhalf = c.sb("nhalf", [128, 4], F32)
        epst = c.sb("epst", [128, 4], F32)
        Sret = c.sb("Sret", [128, 4, 128], F32)
        Sretb = c.sb("Sretb", [128, 4, 128], BF16)
        Sgdn = c.sb("Sgdn", [128, 4, 128], F32)
        Sgdnb = c.sb("Sgdnb", [128, 4, 128], BF16)
        car = c.sb("car", [128, 12, 3], F32)
        scb = c.sb("scb", [128, 12, 3, NS], F32)
        gsc = c.sb("gsc", [128, NTB, 12], F32)
        s_qm = c.sb("s_qm", [128, 4, NS, NS], BF16)
        s_km = c.sb("s_km", [128, 4, NS, NS], BF16)
        s_os = c.sb("s_os", [128, 512], F32)
        s_vs = c.sb("s_vs", [128, 512], F32)
        s_ks = c.sb("s_ks", [128, 512], BF16)
        s_abc = c.sb("s_abc", [128, NS, 4], F32)
        s_gate = c.sb("s_gate", [128, 512], BF16)
        rF = Rot(c, "rF", 9, [128, 520], F32)
        rB = Rot(c, "rB", 12, [128, 512], BF16)
        rBB = Rot(c, "rBB", 2, [128, 1024], BF16)
        rS = Rot(c, "rS", 24, [128, 16], F32)
        pb = [c.ps("pb%d" % i, [128, 512], F32) for i in range(8)]
        pbb = [p.bitcast(BF16) for p in pb]
        bank_i = [0]
        bank_gen = [0] * 8

        class BK(int):
            pass

        def bank():
            b = bank_i[0]
            bank_i[0] = (b + 1) % 8
            bank_gen[b] += 1
            r = BK(b)
            r.gen = bank_gen[b]
            return r

        IDF, TRI, SU, MASKS, ONES, DI0, DI1 = 0, 1, 2, 3, 4, 5, 6
        def PB(b):
            assert bank_gen[int(b)] == b.gen, "stale psum bank %d" % int(b)
            return ("pb", int(b))

        def v4(ap):
            return ap.rearrange("p (h e) -> p h e", h=4)

        def hb(h):
            return slice(h * 128, (h + 1) * 128)

        def ld(dst, src, key):
            c.dma("sp", lambda e: e.dma_start(out=dst, in_=src), writes=[key])

        ld(gpre[:], gpre_in, "gpre")
        ld(cf[:], cf_in, "cf")
        ld(rettab[:], rt_in, "rettab")
        ld(kdec[:], kdec_in, "kdec")
        ld(convw[:], convw_in, "convw")
        ld(rng_t[:], rng_in.partition_broadcast(128), "rng")
        ld(gng_t[:], gng_in.partition_broadcast(128), "gng")
        ld(alog_t[:], alog_in.partition_broadcast(128), "alog")
        ld(dtb_t[:], dtb_in.partition_broadcast(128), "dtb")
        c.op("dve", lambda e: e.tensor_copy(out=cb[:], in_=cf[:]), reads=["cf"], writes=["cb"])
        c.op("pool", lambda e: e.memset(nhalf[:], -0.5), writes=["nhalf"])
        c.op("pool", lambda e: e.memset(epst[:], EPS), writes=["epst"])
        c.op("act", lambda e: e.activation(out=nega_t[:], in_=alog_t[:], func=AF.Exp), reads=["alog"], writes=["nega"])
        c.op("dve", lambda e: e.tensor_scalar(out=nega_t[:], in0=nega_t[:], scalar1=-1.0, scalar2=None, op0=ALU.mult),
             reads=["nega"], writes=["nega"])
        for t_, k_ in ((Sret, "Sret"), (Sgdn, "Sgdn"), (Sretb, "Sretb"), (Sgdnb, "Sgdnb"), (car, "car")):
            c.op("pool", lambda e, t_=t_: e.memset(t_[:], 0.0), writes=[k_])
        identb = cb[:, IDF, :]

        wstate = {"n": 0}

        wload_after = [()]

        def wload(src, r0, kc, c0, ncols):
            s = wstate["n"] % NSLOT
            wstate["n"] += 1
            t, key = ring[s], ("ring", s)
            v = src[r0 * 128:(r0 + kc) * 128, c0:c0 + ncols].rearrange("(c p) n -> p c n", p=128)
            c.dma("pool", lambda e: e.dma_start(out=t[:, 0:kc, 0:ncols], in_=v), reads=list(wload_after[0]), writes=[key])
            return t, key

        def mm8(b, nrows_out, ncols, lhsT_fn, rhs_fn, reads, nk=8):
            for k in range(nk):
                lt, rt_ = lhsT_fn(k), rhs_fn(k)
                c.op("pe", lambda e, k=k, lt=lt, rt_=rt_: e.matmul(pb[b][:nrows_out, 0:ncols], lhsT=lt, rhs=rt_,
                                                                  start=(k == 0), stop=(k == nk - 1)),
                     reads=reads, writes=[PB(b)])

        def prenormA(i, nrows, c0, gidx):
            kx = ("x1", i)
            jt = rBB.get()
            ss = rS.get()
            c.op("act", lambda e: e.activation(out=jt.t[:nrows, :], in_=x1[:nrows, i, :], func=AF.Square,
                                               accum_out=ss.t[:nrows, 0:1]), reads=[kx], writes=[jt, ss])
            c.op("dve", lambda e: e.tensor_scalar(out=ss.t[:nrows, 1:2], in0=ss.t[:nrows, 0:1], scalar1=1.0 / D,
                                                  scalar2=EPS, op0=ALU.mult, op1=ALU.add), reads=[ss], writes=[ss])
            c.op("pool", lambda e: e.tensor_tensor(out=ss.t[:nrows, 2:3], in0=ss.t[:nrows, 1:2], in1=nhalf[:nrows, 0:1],
                                                   op=ALU.pow), reads=[ss, "nhalf"], writes=[ss])
            hn = rBB.get()
            c.op("dve", lambda e: e.tensor_scalar(out=hn.t[:nrows, :], in0=x1[:nrows, i, :], scalar1=ss.t[:nrows, 2:3],
                                                  scalar2=None, op0=ALU.mult), reads=[kx, ss], writes=[hn])

            def partB():
                b = bank()
                for k in range(8):
                    c.op("pe", lambda e, k=k: e.transpose(out=pbb[b][:, k * 128:k * 128 + nrows],
                                                          in_=hn.t[:nrows, k * 128:(k + 1) * 128],
                                                          identity=cb[:nrows, IDF, :nrows]),
                         reads=[hn, "cb"], writes=[PB(b)])
                src = pbb[b][:, :].rearrange("p (k t) -> p k t", k=8)[:, :, 0:nrows]
                gb = gpre[:, gidx * 8:(gidx + 1) * 8].unsqueeze(2).broadcast_to([128, 8, nrows])
                c.op("dve", lambda e: e.tensor_tensor(out=hT[:, :, c0:c0 + nrows], in0=src, in1=gb, op=ALU.mult),
                     reads=[PB(b), "gpre"], writes=[("hT", i)])
            return partB

        def prenorm(i, nrows, c0, gidx):
            prenormA(i, nrows, c0, gidx)()

        def load_gp(row):
            c.dma("sp", lambda e: e.dma_start(out=gp[:], in_=gpost_in[row:row + 1, :].partition_broadcast(128)),
                  writes=["gp"])

        def post(i, nrows, b0, b1, scale, final_row0=None):
            ss = rS.get()
            jt = rBB.get()
            c.op("act", lambda e: e.activation(out=jt.t[:nrows, 0:512], in_=pb[b0][:nrows, :], func=AF.Square,
                                               accum_out=ss.t[:nrows, 0:1]), reads=[PB(b0)], writes=[jt, ss])
            c.op("act", lambda e: e.activation(out=jt.t[:nrows, 512:1024], in_=pb[b1][:nrows, :], func=AF.Square,
                                               accum_out=ss.t[:nrows, 1:2]), reads=[PB(b1)], writes=[jt, ss])
            c.op("dve", lambda e: e.tensor_tensor(out=ss.t[:nrows, 2:3], in0=ss.t[:nrows, 0:1], in1=ss.t[:nrows, 1:2],
                                                  op=ALU.add), reads=[ss], writes=[ss])
            m = 1.0 / (scale * scale)
            c.op("dve", lambda e: e.tensor_scalar(out=ss.t[:nrows, 3:4], in0=ss.t[:nrows, 2:3], scalar1=m / D,
                                                  scalar2=EPS * m, op0=ALU.mult, op1=ALU.add), reads=[ss], writes=[ss])
            c.op("pool", lambda e: e.tensor_tensor(out=ss.t[:nrows, 4:5], in0=ss.t[:nrows, 3:4], in1=nhalf[:nrows, 0:1],
                                                   op=ALU.pow), reads=[ss, "nhalf"], writes=[ss])
            for hf, bb in ((0, b0), (1, b1)):
                t = rF.get()
                c.op("dve", lambda e, t=t, bb=bb, hf=hf: e.scalar_tensor_tensor(
                    out=t.t[:nrows, 0:512], in0=pb[bb][:nrows, :], scalar=ss.t[:nrows, 4:5], op0=ALU.mult,
                    in1=gp[:nrows, hf * 512:(hf + 1) * 512], op1=ALU.mult),
                    reads=[PB(bb), ss, "gp"], writes=[t])
                c.op("pool", lambda e, t=t, hf=hf: e.tensor_tensor(
                    out=x1[:nrows, i, hf * 512:(hf + 1) * 512], in0=x1[:nrows, i, hf * 512:(hf + 1) * 512],
                    in1=t.t[:nrows, 0:512], op=ALU.add), reads=[t, ("x1", i)], writes=[("x1", i)])
            if final_row0 is not None:
                c.dma("sp", lambda e: e.dma_start(out=y_out[final_row0:final_row0 + nrows, :], in_=x1[:nrows, i, :]),
                      reads=[("x1", i)], is_output=True)

        s0_ffn_keys = set()
        s0_mix_keys = set()

        def ffn(tiles, cgs, wg, wu, wd, gpost_row, final, after_tile=None):
            load_gp(gpost_row)
            ntl = (NJ + 3) // 4
            loaded = {}

            def load_gu(g):
                nj = min(4, NJ - g * 4)
                loaded[g] = (wload(wg, 0, 8, g * 512, nj * 128), wload(wu, 0, 8, g * 512, nj * 128), nj)

            load_gu(0)
            dts = []
            dspecs = [(hf, r0, kc) for hf in range(2) for (r0, kc) in ((0, 8), (8, 8), (16, 6))]
            for g in range(ntl):
                if g + 1 < ntl:
                    load_gu(g + 1)
                else:
                    for (hf, r0, kc) in dspecs[:NSLOT - 2]:
                        dts.append(wload(wd, r0, kc, hf * 512, 512))
                (gt, gk), (ut, uk), nj = loaded.pop(g)
                for jj in range(nj):
                    j = g * 4 + jj
                    bgs = [bank() for _ in cgs]
                    bus = [bank() for _ in cgs]
                    for (wt, wk, bks) in ((gt, gk, bgs), (ut, uk, bus)):
                        for k in range(8):
                            for ci, (c0, n, tl) in enumerate(cgs):
                                b = bks[ci]
                                c.op("pe", lambda e, k=k, b=b, c0=c0, n=n, wt=wt: e.matmul(
                                    pb[b][:, 0:n], lhsT=wt[:, k, jj * 128:(jj + 1) * 128], rhs=hT[:, k, c0:c0 + n],
                                    start=(k == 0), stop=(k == 7)),
                                    reads=[wk] + [("hT", t) for t in tl], writes=[PB(b)])
                    for ci, (c0, n, tl) in enumerate(cgs):
                        bg, bu = bgs[ci], bus[ci]
                        sg = rF.get()
                        c.op("act", lambda e, sg=sg, bg=bg, n=n: e.activation(out=sg.t[:, 0:n], in_=pb[bg][:, 0:n],
                                                                          func=AF.Silu),
                             reads=[PB(bg)], writes=[sg])
                        s0_ffn_keys.add(("aT", j, c0))
                        c.op("dve", lambda e, sg=sg, bu=bu, n=n, c0=c0, j=j: e.tensor_tensor(
                            out=aT[:, j, c0:c0 + n], in0=sg.t[:, 0:n], in1=pb[bu][:, 0:n], op=ALU.mult),
                            reads=[sg, PB(bu)], writes=[("aT", j, c0)])
            for (hf, r0, kc) in dspecs[NSLOT - 2:]:
                dts.append(wload(wd, r0, kc, hf * 512, 512))
            pend = [None]
            for (i, nrows, c0, row0) in tiles:
                bs = []
                cgc0 = [cc for (cc, n, tl) in cgs if cc <= c0 < cc + n][0]
                for hf in range(2):
                    b = bank()
                    bs.append(b)
                    for j in range(NJ):
                        dt_, dk_ = dts[hf * 3 + j // 8]
                        c.op("pe", lambda e, j=j, b=b, dt_=dt_: e.matmul(
                            pb[b][:nrows, :], lhsT=aT[:, j, c0:c0 + nrows], rhs=dt_[:, j % 8, :],
                            start=(j == 0), stop=(j == NJ - 1)),
                            reads=[dk_, ("aT", j, cgc0)], writes=[PB(b)])
                if pend[0] is not None:
                    pend[0]()
                    pend[0] = None
                post(i, nrows, bs[0], bs[1], 0.5, final_row0=(row0 if final else None))
                if after_tile is not None:
                    pend[0] = after_tile(i, nrows, c0)
            if pend[0] is not None:
                pend[0]()

        def rope(b, i, nrows, out_ap, out_key):
            q4 = pb[b][:nrows, :].rearrange("p (h two d) -> p h two d", h=4, two=2)
            t1 = rF.get()
            t1v = t1.t[:nrows, 0:512].rearrange("p (h two d) -> p h two d", h=4, two=2)
            cosb = cs_t[:nrows, i, 0, :].unsqueeze(1).unsqueeze(1).broadcast_to([nrows, 4, 2, 64])
            c.op("dve", lambda e: e.tensor_tensor(out=t1v, in0=q4, in1=cosb, op=ALU.mult),
                 reads=[PB(b), "cs_t"], writes=[t1])
            u = rF.get()
            uv = u.t[:nrows, 0:512].rearrange("p (h two d) -> p h two d", h=4, two=2)
            nsb = cs_t[:nrows, i, 2, :].unsqueeze(1).broadcast_to([nrows, 4, 64])
            sb_ = cs_t[:nrows, i, 1, :].unsqueeze(1).broadcast_to([nrows, 4, 64])
            c.op("dve", lambda e: e.tensor_tensor(out=uv[:, :, 0, :], in0=q4[:, :, 1, :], in1=nsb, op=ALU.mult),
                 reads=[PB(b), "cs_t"], writes=[u])
            c.op("dve", lambda e: e.tensor_tensor(out=uv[:, :, 1, :], in0=q4[:, :, 0, :], in1=sb_, op=ALU.mult),
                 reads=[PB(b), "cs_t", u], writes=[u])
            c.op("pool", lambda e: e.tensor_tensor(out=out_ap, in0=t1.t[:nrows, 0:512], in1=u.t[:nrows, 0:512],
                                                   op=ALU.add), reads=[t1, u], writes=[out_key])

        def ret_proj(i, nrows, c0, Ts):
            bs = []
            for (T, kT) in Ts:
                b = bank()
                bs.append(b)
                mm8(b, nrows, 512, lambda k: hT[:, k, c0:c0 + nrows], lambda k, T=T: T[:, k, :], [kT, ("hT", i)])
            return bs

        def onorm_tail(i, nrows, c0, on, gate, dstT, dkey, gtab, gkey, gbc, defer=False):
            if gbc:
                gi = gtab[:nrows, :].unsqueeze(1).broadcast_to([nrows, 4, 128])
                c.op("pool", lambda e: e.tensor_tensor(out=v4(on.t[:nrows, 0:512]), in0=v4(on.t[:nrows, 0:512]), in1=gi,
                                                       op=ALU.mult), reads=[on, gkey], writes=[on])
            else:
                c.op("pool", lambda e: e.tensor_tensor(out=on.t[:nrows, 0:512], in0=on.t[:nrows, 0:512],
                                                       in1=gtab[:nrows, :], op=ALU.mult), reads=[on, gkey], writes=[on])
            orn = rB.get()
            g_ap = s_gate[:nrows, :] if gate is None else gate.t[:nrows, :]
            g_k = "s_gate" if gate is None else gate
            c.op("pool", lambda e: e.tensor_tensor(out=orn.t[:nrows, :], in0=on.t[:nrows, 0:512], in1=g_ap,
                                                   op=ALU.mult), reads=[on, g_k], writes=[orn])
            s0_mix_keys.add((dkey, i))

            def partB():
                bt = bank()
                for h in range(4):
                    c.op("pe", lambda e, h=h: e.transpose(out=pbb[bt][:, h * 128:h * 128 + nrows], in_=orn.t[:nrows, hb(h)],
                                                          identity=cb[:nrows, IDF, :nrows]),
                         reads=[orn, "cb"], writes=[PB(bt)])
                src = pbb[bt][:, 0:512].rearrange("p (h t) -> p h t", h=4)[:, :, 0:nrows]
                c.op("act", lambda e: e.activation(out=dstT[:, :, c0:c0 + nrows], in_=src, func=AF.Copy),
                     reads=[PB(bt)], writes=[(dkey, i)])
            if defer:
                return partB
            partB()
            return None

        def groupnorm_ret(bo, nrows, src=None, skey=None):
            sm = rS.get()
            sm2 = rS.get()
            if src is None:
                src, skey = pb[bo][:nrows, :], PB(bo)
            c.op("dve", lambda e: e.tensor_reduce(out=sm.t[:nrows, 0:4], in_=v4(src), axis=AX.X, op=ALU.add),
                 reads=[skey], writes=[sm])
            sq = rF.get()
            c.op("act", lambda e: e.activation(out=sq.t[:nrows, 0:512], in_=src, func=AF.Square),
                 reads=[skey], writes=[sq])
            c.op("dve", lambda e: e.tensor_reduce(out=sm.t[:nrows, 4:8], in_=v4(sq.t[:nrows, 0:512]), axis=AX.X, op=ALU.add),
                 reads=[sq, sm], writes=[sm])
            c.op("dve", lambda e: e.tensor_scalar(out=sm.t[:nrows, 8:12], in0=sm.t[:nrows, 0:4], scalar1=1.0 / 128,
                                                  scalar2=None, op0=ALU.mult), reads=[sm], writes=[sm])
            c.op("dve", lambda e: e.tensor_tensor(out=sm.t[:nrows, 12:16], in0=sm.t[:nrows, 8:12], in1=sm.t[:nrows, 8:12],
                                                  op=ALU.mult), reads=[sm], writes=[sm])
            c.op("dve", lambda e: e.scalar_tensor_tensor(out=sm2.t[:nrows, 0:4], in0=sm.t[:nrows, 4:8], scalar=1.0 / 128,
                                                         op0=ALU.mult, in1=sm.t[:nrows, 12:16], op1=ALU.subtract),
                 reads=[sm], writes=[sm2])
            c.op("dve", lambda e: e.tensor_scalar(out=sm2.t[:nrows, 4:8], in0=sm2.t[:nrows, 0:4], scalar1=EPS, scalar2=None,
                                                  op0=ALU.add), reads=[sm2], writes=[sm2])
            c.op("pool", lambda e: e.tensor_tensor(out=sm2.t[:nrows, 8:12], in0=sm2.t[:nrows, 4:8], in1=nhalf[:nrows, 0:4],
                                                   op=ALU.pow), reads=[sm2, "nhalf"], writes=[sm2])
            on = rF.get()
            for h in range(4):
                c.op("dve", lambda e, h=h: e.tensor_scalar(out=on.t[:nrows, hb(h)], in0=src[:, hb(h)],
                                                           scalar1=sm.t[:nrows, 8 + h:9 + h], scalar2=sm2.t[:nrows, 8 + h:9 + h],
                                                           op0=ALU.subtract, op1=ALU.mult),
                     reads=[skey, sm, sm2, on] if h else [skey, sm, sm2], writes=[on])
            return on

        def rmsnorm_gdn(bo, nrows, src=None, skey=None):
            sq = rF.get()
            sm = rS.get()
            if src is None:
                src, skey = pb[bo][:nrows, :], PB(bo)
            c.op("act", lambda e: e.activation(out=sq.t[:nrows, 0:512], in_=src, func=AF.Square),
                 reads=[skey], writes=[sq])
            c.op("dve", lambda e: e.tensor_reduce(out=sm.t[:nrows, 0:4], in_=v4(sq.t[:nrows, 0:512]), axis=AX.X, op=ALU.add),
                 reads=[sq], writes=[sm])
            c.op("dve", lambda e: e.tensor_scalar(out=sm.t[:nrows, 4:8], in0=sm.t[:nrows, 0:4], scalar1=1.0 / 128,
                                                  scalar2=EPS, op0=ALU.mult, op1=ALU.add), reads=[sm], writes=[sm])
            c.op("pool", lambda e: e.tensor_tensor(out=sm.t[:nrows, 8:12], in0=sm.t[:nrows, 4:8], in1=nhalf[:nrows, 0:4],
                                                   op=ALU.pow), reads=[sm, "nhalf"], writes=[sm])
            on = rF.get()
            c.op("dve", lambda e: e.tensor_tensor(out=v4(on.t[:nrows, 0:512]), in0=v4(src),
                                                  in1=sm.t[:nrows, 8:12].unsqueeze(2).broadcast_to([nrows, 4, 128]),
                                                  op=ALU.mult), reads=[skey, sm], writes=[on])
            return on

        def phase_R(ptiles, with_sample, g1=None):
            Ts = [wload(w_in, 0, 8, cc, 512) for cc in (0, 512, 1024, 1536)]

            def pump(n, site="x"):
                if ("np_" + site) in SK:
                    return
                if g1 is not None:
                    for _ in range(n):
                        next(g1, None)
            pump(1)
            pendR = None
            for (i, nrows, c0, row0) in ptiles:
                bq, bk, bv, bg = ret_proj(i, 128, c0, Ts)
                if pendR is not None:
                    pendR()
                    pendR = None
                pump(1, "a")
                qr = rB.get()
                rope(bq, i, 128, qr.t[:, :], qr)
                kf = rF.get()
                rope(bk, i, 128, kf.t[:, 0:512], kf)
                krb = rB.get()
                c.op("act", lambda e: e.activation(out=krb.t[:, :], in_=kf.t[:, 0:512], func=AF.Copy), reads=[kf], writes=[krb])
                kd = rB.get()
                c.op("pool", lambda e: e.tensor_tensor(out=v4(kd.t[:, :]), in0=v4(kf.t[:, 0:512]),
                                                       in1=kdec[:, :].unsqueeze(2).broadcast_to([128, 4, 128]), op=ALU.mult),
                     reads=[kf, "kdec"], writes=[kd])
                vb = rB.get()
                c.op("act", lambda e: e.activation(out=vb.t[:, :], in_=pb[bv][:, :], func=AF.Copy), reads=[PB(bv)], writes=[vb])
                rgs = rB.get()
                c.op("act", lambda e: e.activation(out=rgs.t[:, :], in_=pb[bg][:, :], func=AF.Silu), reads=[PB(bg)], writes=[rgs])
                bt = bank()
                for h in range(4):
                    c.op("pe", lambda e, h=h: e.transpose(out=pbb[bt][:, hb(h)], in_=qr.t[:, hb(h)], identity=identb),
                         reads=[qr, "cb"], writes=[PB(bt)])
                for h in range(4):
                    c.op("pe", lambda e, h=h: e.transpose(out=pbb[bt][:, hb(4 + h)], in_=krb.t[:, hb(h)], identity=identb),
                         reads=[krb, "cb"], writes=[PB(bt)])
                qkT = rBB.get()
                c.op("act", lambda e: e.activation(out=qkT.t[:, :], in_=pbb[bt][:, 0:1024], func=AF.Copy),
                     reads=[PB(bt)], writes=[qkT])
                qg = rB.get()
                c.op("pool", lambda e: e.tensor_tensor(out=qg.t[:, :], in0=qkT.t[:, 0:512],
                                                       in1=rettab[:, 1, :, :].rearrange("p h i -> p (h i)"), op=ALU.mult),
                     reads=[qkT, "rettab"], writes=[qg])
                pump(5, "b")
                bsc = bank()
                for h in range(4):
                    c.op("pe", lambda e, h=h: e.matmul(pb[bsc][:, hb(h)], lhsT=qkT.t[:, hb(4 + h)], rhs=qkT.t[:, hb(h)],
                                                       start=True, stop=True), reads=[qkT], writes=[PB(bsc)])
                sT = rB.get()
                c.op("dve", lambda e: e.tensor_tensor(out=sT.t[:, :], in0=pb[bsc][:, :],
                                                      in1=rettab[:, 0, :, :].rearrange("p h i -> p (h i)"), op=ALU.mult),
                     reads=[PB(bsc), "rettab"], writes=[sT])
                bo = bank()
                for h in range(4):
                    c.op("pe", lambda e, h=h: e.matmul(pb[bo][:, hb(h)], lhsT=sT.t[:, hb(h)], rhs=vb.t[:, hb(h)],
                                                       start=True, stop=False), reads=[sT, vb], writes=[PB(bo)])
                    c.op("pe", lambda e, h=h: e.matmul(pb[bo][:, hb(h)], lhsT=qg.t[:, hb(h)], rhs=Sretb[:, h, :],
                                                       start=False, stop=True), reads=[qg, "Sretb"], writes=[PB(bo)])
                bS = bank()
                for h in range(4):
                    c.op("pe", lambda e, h=h: e.matmul(pb[bS][:, hb(h)], lhsT=kd.t[:, hb(h)], rhs=vb.t[:, hb(h)],
                                                       start=True, stop=True), reads=[kd, vb], writes=[PB(bS)])
                for h in range(4):
                    c.op("dve", lambda e, h=h: e.scalar_tensor_tensor(out=Sret[:, h, :], in0=Sret[:, h, :],
                                                                      scalar=float(GAMMA[h] ** 128), op0=ALU.mult,
                                                                      in1=pb[bS][:, hb(h)], op1=ALU.add),
                         reads=[PB(bS), "Sret"], writes=["Sret"])
                c.op("act", lambda e: e.activation(out=Sretb[:], in_=Sret[:], func=AF.Copy), reads=["Sret"], writes=["Sretb"])
                on = groupnorm_ret(bo, 128)
                pendR = onorm_tail(i, 128, c0, on, rgs, o_rT, "orT", rng_t, "rng", False, defer=True)
            if pendR is not None:
                pendR()
            if with_sample:
                c.region("Rs")
                sample_ret(Ts)
                c.region("R2")

        def build_masked(dst, dkey, srcT_fn, src_reads):
            di = cf[:, DI0:DI0 + 2, :].rearrange("p a (b m) -> p (a b) m", m=NS)
            for h in range(4):
                c.op("dve", lambda e, h=h: e.tensor_tensor(out=dst[:, h, :, :],
                                                           in0=srcT_fn(h).unsqueeze(1).broadcast_to([128, NS, NS]),
                                                           in1=di, op=ALU.mult),
                     reads=src_reads + ["cf"] + ([dkey] if h else []), writes=[dkey])

        def sample_state_update(h, tg, S0g, lhs_tok, ublk, a_scalar, a_bc, out_dram):
            bO = bank()
            c.op("pe", lambda e: e.matmul(pb[bO][:, :], lhsT=lhs_tok, rhs=ublk.t[:NS, :], start=True, stop=True),
                 reads=[ublk, "s_ks"], writes=[PB(bO)])
            if a_scalar is not None:
                c.op("dve", lambda e: e.scalar_tensor_tensor(out=S0g.t[:, 0:512], in0=S0g.t[:, 0:512], scalar=a_scalar,
                                                             op0=ALU.mult, in1=pb[bO][:, :], op1=ALU.add),
                     reads=[S0g, PB(bO)], writes=[S0g])
            else:
                c.op("dve", lambda e: e.tensor_tensor(out=v4(S0g.t[:, 0:512]), in0=v4(S0g.t[:, 0:512]), in1=a_bc,
                                                      op=ALU.mult), reads=[S0g, "s_abc"], writes=[S0g])
                c.op("dve", lambda e: e.tensor_tensor(out=S0g.t[:, 0:512], in0=S0g.t[:, 0:512], in1=pb[bO][:, :],
                                                      op=ALU.add), reads=[S0g, PB(bO)], writes=[S0g])
            c.dma("sp", lambda e: e.dma_start(out=out_dram[tg * 4:(tg + 1) * 4, h].rearrange("t d e -> d t e"),
                                              in_=v4(S0g.t[:, 0:512])), reads=[S0g], is_output=True)
            Snb = rB.get()
            c.op("act", lambda e: e.activation(out=Snb.t[:, :], in_=S0g.t[:, 0:512], func=AF.Copy), reads=[S0g], writes=[Snb])
            return Snb

        def make_ublk(u_ap, u_reads, tg):
            ub = rB.get()
            c.op("dve", lambda e: e.tensor_tensor(out=v4(ub.t[:NS, :]), in0=u_ap.unsqueeze(1).broadcast_to([NS, 4, 128]),
                                                  in1=cf[:NS, IDF, tg * 4:(tg + 1) * 4].unsqueeze(2).broadcast_to([NS, 4, 128]),
                                                  op=ALU.mult), reads=u_reads + ["cf"], writes=[ub])
            return ub

        def sample_ret(Ts):
            i, c0 = SI, SC0
            bq, bk, bv, bg = ret_proj(i, NS, c0, Ts)
            qr = rB.get()
            rope(bq, i, NS, qr.t[:NS, :], qr)
            kf = rF.get()
            rope(bk, i, NS, kf.t[:NS, 0:512], kf)
            c.op("act", lambda e: e.activation(out=s_ks[:NS, :], in_=kf.t[:NS, 0:512], func=AF.Copy, scale=float(DK ** -0.5)),
                 reads=[kf], writes=["s_ks"])
            c.op("act", lambda e: e.activation(out=s_vs[:NS, :], in_=pb[bv][:NS, :], func=AF.Copy), reads=[PB(bv)], writes=["s_vs"])
            c.op("act", lambda e: e.activation(out=s_gate[:NS, :], in_=pb[bg][:NS, :], func=AF.Silu), reads=[PB(bg)], writes=["s_gate"])
            bt = bank()
            for h in range(4):
                c.op("pe", lambda e, h=h: e.transpose(out=pbb[bt][:, h * NS:(h + 1) * NS], in_=qr.t[:NS, hb(h)],
                                                      identity=cb[:NS, IDF, :NS]), reads=[qr, "cb"], writes=[PB(bt)])
            qTs = rB.get()
            c.op("act", lambda e: e.activation(out=qTs.t[:, 0:4 * NS], in_=pbb[bt][:, 0:4 * NS], func=AF.Copy),
                 reads=[PB(bt)], writes=[qTs])
            build_masked(s_qm, "s_qm", lambda h: qTs.t[:, h * NS:(h + 1) * NS], [qTs])
            for h in range(4):
                snbs = []
                for tg in range(4):
                    S0g = rF.get()
                    c.dma("sp", lambda e, S0g=S0g, tg=tg: e.dma_start(
                        out=v4(S0g.t[:, 0:512]), in_=sret_in[tg * 4:(tg + 1) * 4, h].rearrange("t d e -> d t e")),
                        writes=[S0g])
                    ub = make_ublk(s_vs[:NS, hb(h)], ["s_vs"], tg)
                    snbs.append(sample_state_update(h, tg, S0g, s_ks[:NS, hb(h)], ub, float(GAMMA[h]), None, nsr_s))
                bQ = bank()
                for t in range(NS):
                    c.op("pe", lambda e, t=t: e.matmul(pb[bQ][:NS, 0:128], lhsT=s_qm[:, h, t, :],
                                                       rhs=snbs[t // 4].t[:, hb(t % 4)], start=(t == 0), stop=(t == NS - 1)),
                         reads=["s_qm", snbs[t // 4]], writes=[PB(bQ)])
                c.op("act", lambda e, h=h: e.activation(out=s_os[:NS, hb(h)], in_=pb[bQ][:NS, 0:128], func=AF.Copy),
                     reads=[PB(bQ), "s_os"], writes=["s_os"])
            on = groupnorm_ret(None, NS, s_os[:NS, :], "s_os")
            onorm_tail(i, NS, c0, on, None, o_rT, "orT", rng_t, "rng", False)

        def phase_G1(cgs_p, with_sample, last_block):
            g1w = [wload(w_in, 0, 8, 2048 + T * 512, 512) for T in range(3)]
            yield
            for T in range(3):
                Tt, kT = g1w[T]
                for cc in range(4):
                    s = T * 4 + cc
                    groups = [(c0, n, tl, False) for (c0, n, tl) in cgs_p]
                    if with_sample:
                        groups.append((SC0, NS, [SI], True))
                    for (c0, n, tl, is_s) in groups:
                        b = bank()
                        mm8(b, 128, n, lambda k: Tt[:, k, cc * 128:(cc + 1) * 128], lambda k: hT[:, k, c0:c0 + n],
                            [kT] + [("hT", t) for t in tl])
                        acc = rF.get()
                        if not is_s:
                            ub = rF.get()
                            c.op("pool", lambda e: e.tensor_copy(out=ub.t[:, 0:3], in_=car[:, s, :]), reads=["car"], writes=[ub])
                            c.op("act", lambda e: e.activation(out=ub.t[:, 3:3 + n], in_=pb[b][:, 0:n], func=AF.Copy),
                                 reads=[PB(b), ub], writes=[ub])
                            c.op("pool", lambda e: e.tensor_copy(out=car[:, s, :], in_=ub.t[:, n:n + 3]), reads=[ub, "car"],
                                 writes=["car"])
                            c.op("dve", lambda e: e.tensor_scalar(out=acc.t[:, 0:n], in0=ub.t[:, 3:3 + n], scalar1=convw[:, s, 3:4],
                                                                  scalar2=None, op0=ALU.mult), reads=[ub, "convw"], writes=[acc])
                            for tap in (2, 1, 0):
                                c.op("dve", lambda e, tap=tap: e.scalar_tensor_tensor(
                                    out=acc.t[:, 0:n], in0=ub.t[:, tap:tap + n], scalar=convw[:, s, tap:tap + 1], op0=ALU.mult,
                                    in1=acc.t[:, 0:n], op1=ALU.add), reads=[ub, "convw", acc], writes=[acc])
                        else:
                            us = rF.get()
                            c.op("pool", lambda e: e.memset(us.t[:, 0:128], 0.0), writes=[us])
                            c.op("act", lambda e: e.activation(out=us.t[:, 0:n], in_=pb[b][:, 0:n], func=AF.Copy),
                                 reads=[PB(b), us], writes=[us])
                            c.op("dve", lambda e: e.tensor_scalar(out=acc.t[:, 0:n], in0=us.t[:, 0:n], scalar1=convw[:, s, 3:4],
                                                                  scalar2=None, op0=ALU.mult), reads=[us, "convw"], writes=[acc])
                            for tap in (2, 1, 0):
                                c.op("dve", lambda e, tap=tap: e.scalar_tensor_tensor(
                                    out=acc.t[:, 0:n], in0=scb[:, s, tap, :], scalar=convw[:, s, tap:tap + 1], op0=ALU.mult,
                                    in1=acc.t[:, 0:n], op1=ALU.add), reads=["scb", "convw", acc], writes=[acc])
                            bt = bank()
                            c.op("pe", lambda e: e.transpose(out=pb[bt][:, 0:128], in_=us.t[:, 0:128], identity=cf[:, IDF, :]),
                                 reads=[us, "cf"], writes=[PB(bt)])
                            ut = rF.get()
                            c.op("act", lambda e: e.activation(out=ut.t[:NS, 0:128], in_=pb[bt][:NS, 0:128], func=AF.Copy),
                                 reads=[PB(bt)], writes=[ut])
                            c.dma("sp", lambda e: e.dma_start(out=nsc_s[:, 2, s * 128:(s + 1) * 128], in_=ut.t[:NS, 0:128]),
                                  reads=[ut], is_output=True)
                        ck = ("cqT", s, c0)
                        s0_mix_keys.add(ck)
                        if s >= 8:
                            c.op("act", lambda e: e.activation(out=cqT[:, s, c0:c0 + n], in_=acc.t[:, 0:n], func=AF.Silu),
                                 reads=[acc], writes=[ck])
                        else:
                            cs = rF.get()
                            c.op("act", lambda e: e.activation(out=cs.t[:, 0:n], in_=acc.t[:, 0:n], func=AF.Silu),
                                 reads=[acc], writes=[cs])
                            sqb = rB.get()
                            c.op("pool", lambda e: e.tensor_tensor(out=sqb.t[:, 0:n], in0=cs.t[:, 0:n], in1=cs.t[:, 0:n],
                                                                   op=ALU.mult), reads=[cs], writes=[sqb])
                            b2 = bank()
                            c.op("pe", lambda e: e.matmul(pb[b2][:, 0:n], lhsT=cb[:, ONES, :], rhs=sqb.t[:, 0:n],
                                                          start=True, stop=True), reads=[sqb, "cb"], writes=[PB(b2)])
                            rr = rF.get()
                            c.op("act", lambda e: e.activation(out=rr.t[:, 0:n], in_=pb[b2][:, 0:n], func=AF.Ln, bias=epst[:, 0:1]),
                                 reads=[PB(b2), "epst"], writes=[rr])
                            c.op("act", lambda e: e.activation(out=rr.t[:, 0:n], in_=rr.t[:, 0:n], func=AF.Exp, scale=-0.5),
                                 reads=[rr], writes=[rr])
                            sc_ = float(DK ** -0.5) if s < 4 else 1.0
                            c.op("dve", lambda e: e.scalar_tensor_tensor(out=cqT[:, s, c0:c0 + n], in0=cs.t[:, 0:n], scalar=sc_,
                                                                         op0=ALU.mult, in1=rr.t[:, 0:n], op1=ALU.mult),
                                 reads=[cs, rr], writes=[ck])
                        yield
            if last_block:
                for r in range(3):
                    cc_ = rF.get()
                    c.op("dve", lambda e, r=r: e.tensor_copy(out=cc_.t[:, 0:12], in_=car[:, :, r]), reads=["car"], writes=[cc_])
                    bt = bank()
                    c.op("pe", lambda e: e.transpose(out=pb[bt][:12, 0:128], in_=cc_.t[:, 0:12], identity=cf[:, IDF, :]),
                         reads=[cc_, "cf"], writes=[PB(bt)])
                    co = rF.get()
                    c.op("act", lambda e: e.activation(out=co.t[:12, 0:128], in_=pb[bt][:12, 0:128], func=AF.Copy),
                         reads=[PB(bt)], writes=[co])
                    c.dma("sp", lambda e, r=r: e.dma_start(out=nsc_p[r, :].rearrange("(s p) -> s p", p=128), in_=co.t[:12, 0:128]),
                          reads=[co], is_output=True)

        def gdn_scalars(i, nrows, c0):
            ba = bank()
            mm8(ba, nrows, 8, lambda k: hT[:, k, c0:c0 + nrows], lambda k: wsmall[:, k, :], ["wsmall", ("hT", i)])
            sA = rS.get()
            sB = rS.get()
            c.op("dve", lambda e: e.tensor_tensor(out=sA.t[:nrows, 0:4], in0=pb[ba][:nrows, 0:4], in1=dtb_t[:nrows, :],
                                                  op=ALU.add), reads=[PB(ba), "dtb"], writes=[sA])
            c.op("act", lambda e: e.activation(out=sA.t[:nrows, 4:8], in_=sA.t[:nrows, 0:4], func=AF.Abs), reads=[sA], writes=[sA])
            c.op("act", lambda e: e.activation(out=sA.t[:nrows, 8:12], in_=sA.t[:nrows, 4:8], func=AF.Exp, scale=-1.0),
                 reads=[sA], writes=[sA])
            c.op("dve", lambda e: e.tensor_scalar(out=sA.t[:nrows, 8:12], in0=sA.t[:nrows, 8:12], scalar1=1.0, scalar2=None,
                                                  op0=ALU.add), reads=[sA], writes=[sA])
            c.op("act", lambda e: e.activation(out=sA.t[:nrows, 8:12], in_=sA.t[:nrows, 8:12], func=AF.Ln),
                 reads=[sA], writes=[sA])
            c.op("dve", lambda e: e.tensor_scalar(out=sA.t[:nrows, 12:16], in0=sA.t[:nrows, 0:4], scalar1=0.0, scalar2=None,
                                                  op0=ALU.max), reads=[sA], writes=[sA])
            c.op("dve", lambda e: e.tensor_tensor(out=sB.t[:nrows, 0:4], in0=sA.t[:nrows, 12:16], in1=sA.t[:nrows, 8:12],
                                                  op=ALU.add), reads=[sA], writes=[sB])
            gk = ("gsc", i)
            c.op("dve", lambda e: e.tensor_tensor(out=gsc[:nrows, i, 0:4], in0=sB.t[:nrows, 0:4], in1=nega_t[:nrows, :],
                                                  op=ALU.mult), reads=[sB, "nega"], writes=[gk])
            c.op("act", lambda e: e.activation(out=sB.t[:nrows, 4:8], in_=pb[ba][:nrows, 4:8], func=AF.Exp, scale=-1.0),
                 reads=[PB(ba), sB], writes=[sB])
            c.op("dve", lambda e: e.tensor_scalar(out=sB.t[:nrows, 8:12], in0=sB.t[:nrows, 4:8], scalar1=1.0, scalar2=None,
                                                  op0=ALU.add), reads=[sB], writes=[sB])
            c.op("dve", lambda e: e.reciprocal(out=gsc[:nrows, i, 4:8], in_=sB.t[:nrows, 8:12]), reads=[sB, gk], writes=[gk])
            c.op("dve", lambda e: e.tensor_scalar(out=gsc[:nrows, i, 8:12], in0=gsc[:nrows, i, 4:8], scalar1=-1.0, scalar2=None,
                                                  op0=ALU.mult), reads=[gk], writes=[gk])
            return gk, ba

        def phase_G2(ptiles, with_sample):
            wsf = rF.get()
            c.dma("sp", lambda e: e.dma_start(out=wsf.t[:, 0:64].rearrange("p (c n) -> p c n", n=8),
                                              in_=w_in[:, 4096:4104].rearrange("(c p) n -> p c n", p=128)), writes=[wsf])
            c.op("dve", lambda e: e.tensor_copy(out=wsmall[:, :, :], in_=wsf.t[:, 0:64].rearrange("p (c n) -> p c n", n=8)),
                 reads=[wsf], writes=["wsmall"])
            T7, k7 = wload(w_in, 0, 8, 3584, 512)
            pendG = None
            for (i, nrows, c0, row0) in ptiles:
                gk, bG = gdn_scalars(i, 128, c0)
                if G2CUT <= 1:
                    continue
                for col, m in ((16, TRI), (20, SU), (24, ONES)):
                    c.op("pe", lambda e, col=col, m=m: e.matmul(pb[bG][:, col:col + 4], lhsT=cf[:, m, :], rhs=gsc[:, i, 0:4],
                                                                start=True, stop=True), reads=[gk, "cf"], writes=[PB(bG)])
                ex = rS.get()
                c.op("act", lambda e: e.activation(out=ex.t[:, 0:12], in_=pb[bG][:, 16:28], func=AF.Exp), reads=[PB(bG)], writes=[ex])
                gsu = rF.get()
                c.op("pool", lambda e: e.tensor_tensor(out=v4(gsu.t[:, 0:512]),
                                                       in0=cf[:, SU, :].unsqueeze(1).broadcast_to([128, 4, 128]),
                                                       in1=gsc[:, i, 0:4].unsqueeze(2).broadcast_to([128, 4, 128]), op=ALU.mult),
                     reads=[gk, "cf"], writes=[gsu])
                bD = bank()
                for h in range(4):
                    c.op("pe", lambda e, h=h: e.matmul(pb[bD][:, hb(h)], lhsT=gsu.t[:, hb(h)], rhs=cf[:, TRI, :],
                                                       start=True, stop=True), reads=[gsu, "cf"], writes=[PB(bD)])
                E = rF.get()
                c.op("act", lambda e: e.activation(out=E.t[:, 0:512], in_=pb[bD][:, :], func=AF.Exp), reads=[PB(bD)], writes=[E])
                EMS = rF.get()
                c.op("pool", lambda e: e.tensor_tensor(out=v4(EMS.t[:, 0:512]), in0=v4(E.t[:, 0:512]),
                                                       in1=cf[:, MASKS, :].unsqueeze(1).broadcast_to([128, 4, 128]), op=ALU.mult),
                     reads=[E, "cf"], writes=[EMS])
                c.op("pool", lambda e: e.tensor_tensor(out=v4(E.t[:, 0:512]), in0=v4(E.t[:, 0:512]),
                                                       in1=cf[:, TRI, :].unsqueeze(1).broadcast_to([128, 4, 128]), op=ALU.mult),
                     reads=[E, "cf"], writes=[E])
                kq_reads = [("cqT", s, cc) for s in range(8) for cc in [cg0 for cg0 in cq_cg0(c0)]]
                if G2CUT <= 2:
                    continue
                if pendG is not None:
                    pendG()
                    pendG = None
                bK = bank()
                for h in range(4):
                    c.op("pe", lambda e, h=h: e.matmul(pb[bK][:, hb(h)], lhsT=cqT[:, 4 + h, c0:c0 + 128], rhs=cqT[:, 4 + h, c0:c0 + 128],
                                                       start=True, stop=True), reads=kq_reads, writes=[PB(bK)])
                Y = rB.get()
                for h in range(4):
                    c.op("dve", lambda e, h=h: e.scalar_tensor_tensor(out=Y.t[:, hb(h)], in0=pb[bK][:, hb(h)],
                                                                      scalar=gsc[:, i, 8 + h:9 + h], op0=ALU.mult,
                                                                      in1=EMS.t[:, hb(h)], op1=ALU.mult),
                         reads=[PB(bK), gk, EMS] + ([Y] if h else []), writes=[Y])
                if G2CUT <= 3:
                    continue
                bX = bank()
                for h in range(4):
                    c.op("pe", lambda e, h=h: e.transpose(out=pbb[bX][:, hb(h)], in_=Y.t[:, hb(h)], identity=identb),
                         reads=[Y, "cb"], writes=[PB(bX)])
                X = rB.get()
                c.op("act", lambda e: e.activation(out=X.t[:, :], in_=pbb[bX][:, 0:512], func=AF.Copy), reads=[PB(bX)], writes=[X])
                PT = rB.get()
                c.op("pool", lambda e: e.tensor_tensor(out=v4(PT.t[:, :]), in0=v4(Y.t[:, :]),
                                                       in1=cb[:, IDF, :].unsqueeze(1).broadcast_to([128, 4, 128]), op=ALU.add),
                     reads=[Y, "cb"], writes=[PT])
                if G2CUT <= 4:
                    continue
                for step in range(6):
                    bXn = bank()
                    for h in range(4):
                        c.op("pe", lambda e, h=h, X=X, Y=Y: e.matmul(pb[bXn][:, hb(h)], lhsT=Y.t[:, hb(h)], rhs=X.t[:, hb(h)],
                                                                    start=True, stop=True), reads=[X, Y], writes=[PB(bXn)])
                    if step < 5:
                        bYn = bank()
                        for h in range(4):
                            c.op("pe", lambda e, h=h, X=X, Y=Y: e.matmul(pb[bYn][:, hb(h)], lhsT=X.t[:, hb(h)], rhs=Y.t[:, hb(h)],
                                                                        start=True, stop=True), reads=[X, Y], writes=[PB(bYn)])
                    Xn = rB.get()
                    c.op("act", lambda e, Xn=Xn: e.activation(out=Xn.t[:, :], in_=pb[bXn][:, :], func=AF.Copy),
                         reads=[PB(bXn)], writes=[Xn])
                    if step < 5:
                        Yn = rB.get()
                        c.op("dve", lambda e, Yn=Yn: e.tensor_copy(out=Yn.t[:, :], in_=pb[bYn][:, :]), reads=[PB(bYn)], writes=[Yn])
                    bP = bank()
                    for h in range(4):
                        c.op("pe", lambda e, h=h, Xn=Xn, PT=PT: e.matmul(pb[bP][:, hb(h)], lhsT=Xn.t[:, hb(h)], rhs=PT.t[:, hb(h)],
                                                                        start=True, stop=True), reads=[Xn, PT], writes=[PB(bP)])
                    PTn = rB.get()
                    c.op("dve", lambda e, PTn=PTn, PT=PT: e.tensor_tensor(out=PTn.t[:, :], in0=pb[bP][:, :], in1=PT.t[:, :],
                                                                         op=ALU.add), reads=[PB(bP), PT], writes=[PTn])
                    X, PT = Xn, PTn
                    if step < 5:
                        Y = Yn
                if G2CUT <= 5:
                    continue
                bQ = bank()
                for h in range(4):
                    c.op("pe", lambda e, h=h: e.matmul(pb[bQ][:, hb(h)], lhsT=cqT[:, 4 + h, c0:c0 + 128], rhs=cqT[:, h, c0:c0 + 128],
                                                       start=True, stop=True), reads=kq_reads, writes=[PB(bQ)])
                qkm = rB.get()
                c.op("dve", lambda e: e.tensor_tensor(out=qkm.t[:, :], in0=pb[bQ][:, :], in1=E.t[:, 0:512], op=ALU.mult),
                     reads=[PB(bQ), E], writes=[qkm])
                if 'suba' in SK:
                    continue
                bT = bank()
                kv_reads = [("cqT", s, cg0) for s in range(4, 12) for cg0 in cq_cg0(c0)]
                for h in range(4):
                    c.op("pe", lambda e, h=h: e.transpose(out=pbb[bT][:, hb(h)], in_=cqT[:, 4 + h, c0:c0 + 128], identity=identb),
                         reads=kv_reads + ["cb"], writes=[PB(bT)])
                for h in range(4):
                    c.op("pe", lambda e, h=h: e.transpose(out=pbb[bT][:, hb(4 + h)], in_=cqT[:, 8 + h, c0:c0 + 128], identity=identb),
                         reads=kv_reads + ["cb"], writes=[PB(bT)])
                if 'subb' in SK:
                    continue
                kg = rB.get()
                c.op("dve", lambda e: e.tensor_tensor(out=v4(kg.t[:, :]), in0=v4(pbb[bT][:, 0:512]),
                                                      in1=ex.t[:, 0:4].unsqueeze(2).broadcast_to([128, 4, 128]), op=ALU.mult),
                     reads=[PB(bT), ex], writes=[kg])
                if 'subc' in SK:
                    continue
                kd = rB.get()
                c.op("dve", lambda e: e.tensor_tensor(out=v4(kd.t[:, :]), in0=v4(pbb[bT][:, 0:512]),
                                                      in1=ex.t[:, 4:8].unsqueeze(2).broadcast_to([128, 4, 128]), op=ALU.mult),
                     reads=[PB(bT), ex], writes=[kd])
                if 'subd' in SK:
                    continue
                vt = rB.get()
                c.op("act", lambda e: e.activation(out=vt.t[:, :], in_=pbb[bT][:, 512:1024], func=AF.Copy), reads=[PB(bT)], writes=[vt])
                if G2CUT <= 6:
                    continue
                bW = bank()
                for h in range(4):
                    c.op("pe", lambda e, h=h: e.matmul(pb[bW][:, hb(h)], lhsT=kg.t[:, hb(h)], rhs=PT.t[:, hb(h)],
                                                       start=True, stop=True), reads=[kg, PT], writes=[PB(bW)])
                NW = rB.get()
                c.op("act", lambda e: e.activation(out=NW.t[:, :], in_=pb[bW][:, :], func=AF.Copy, scale=-1.0),
                     reads=[PB(bW)], writes=[NW])
                if G2CUT <= 7:
                    continue
                bU = bank()
                for h in range(4):
                    c.op("pe", lambda e, h=h: e.matmul(pb[bU][:, hb(h)], lhsT=PT.t[:, hb(h)], rhs=vt.t[:, hb(h)],
                                                       start=True, stop=False), reads=[PT, vt], writes=[PB(bU)])
                    c.op("pe", lambda e, h=h: e.matmul(pb[bU][:, hb(h)], lhsT=NW.t[:, hb(h)], rhs=Sgdnb[:, h, :],
                                                       start=False, stop=True), reads=[NW, "Sgdnb"], writes=[PB(bU)])
                U = rB.get()
                c.op("dve", lambda e: e.tensor_tensor(out=v4(U.t[:, :]), in0=v4(pb[bU][:, :]),
                                                      in1=gsc[:, i, 4:8].unsqueeze(2).broadcast_to([128, 4, 128]), op=ALU.mult),
                     reads=[PB(bU), gk], writes=[U])
                if G2CUT <= 8:
                    continue
                gbc = rF.get()
                c.op("pool", lambda e: e.tensor_copy(out=v4(gbc.t[:, 0:512]), in_=gsc[:, i, 0:4].unsqueeze(2).broadcast_to([128, 4, 128])),
                     reads=[gk], writes=[gbc])
                bR = bank()
                for h in range(4):
                    c.op("pe", lambda e, h=h: e.matmul(pb[bR][:, hb(h)], lhsT=gbc.t[:, hb(h)],
                                                       rhs=cf[:, TRI, :], start=True, stop=True), reads=[gbc, "cf"], writes=[PB(bR)])
                EG = rF.get()
                c.op("act", lambda e: e.activation(out=EG.t[:, 0:512], in_=pb[bR][:, :], func=AF.Exp), reads=[PB(bR)], writes=[EG])
                qg = rB.get()
                c.op("pool", lambda e: e.tensor_tensor(out=v4(qg.t[:, :]), in0=cqT[:, 0:4, c0:c0 + 128], in1=v4(EG.t[:, 0:512]),
                                                       op=ALU.mult), reads=kq_reads + [EG], writes=[qg])
                if G2CUT <= 9:
                    continue
                bO = bank()
                for h in range(4):
                    c.op("pe", lambda e, h=h: e.matmul(pb[bO][:, hb(h)], lhsT=qg.t[:, hb(h)], rhs=Sgdnb[:, h, :],
                                                       start=True, stop=False), reads=[qg, "Sgdnb"], writes=[PB(bO)])
                    c.op("pe", lambda e, h=h: e.matmul(pb[bO][:, hb(h)], lhsT=qkm.t[:, hb(h)], rhs=U.t[:, hb(h)],
                                                       start=False, stop=True), reads=[qkm, U], writes=[PB(bO)])
                if G2CUT <= 10:
                    continue
                bS = bank()
                for h in range(4):
                    c.op("pe", lambda e, h=h: e.matmul(pb[bS][:, hb(h)], lhsT=kd.t[:, hb(h)], rhs=U.t[:, hb(h)],
                                                       start=True, stop=True), reads=[kd, U], writes=[PB(bS)])
                for h in range(4):
                    c.op("dve", lambda e, h=h: e.scalar_tensor_tensor(out=Sgdn[:, h, :], in0=Sgdn[:, h, :], scalar=ex.t[:, 8 + h:9 + h],
                                                                      op0=ALU.mult, in1=pb[bS][:, hb(h)], op1=ALU.add),
                         reads=[PB(bS), "Sgdn", ex], writes=["Sgdn"])
                c.op("act", lambda e: e.activation(out=Sgdnb[:], in_=Sgdn[:], func=AF.Copy), reads=["Sgdn"], writes=["Sgdnb"])
                if G2CUT <= 11:
                    continue
                bz = bank()
                mm8(bz, 128, 512, lambda k: hT[:, k, c0:c0 + 128], lambda k: T7[:, k, :], [k7, ("hT", i)])
                gzs = rB.get()
                c.op("act", lambda e: e.activation(out=gzs.t[:, :], in_=pb[bz][:, :], func=AF.Silu), reads=[PB(bz)], writes=[gzs])
                on = rmsnorm_gdn(bO, 128)
                pendG = onorm_tail(i, 128, c0, on, gzs, o_gT, "ogT", gng_t, "gng", True, defer=True)
            if pendG is not None:
                pendG()
            if with_sample:
                sample_gdn(T7, k7)

        cur_cgs_p = [None]

        def cq_cg0(c0):
            return [cc for (cc, n, tl) in cur_cgs_p[0] if cc <= c0 < cc + n]

        not_sample_block = [False]

        def sample_gdn(T7, k7):
            i, c0 = SI, SC0
            gk, _ba = gdn_scalars(i, NS, c0)
            sa = rS.get()
            c.op("act", lambda e: e.activation(out=sa.t[:NS, 0:4], in_=gsc[:NS, i, 0:4], func=AF.Exp), reads=[gk], writes=[sa])
            ad = rF.get()
            c.op("pool", lambda e: e.memset(ad.t[:, 0:NS * 4], 0.0), writes=[ad])
            c.op("dve", lambda e: e.tensor_tensor(out=ad.t[:NS, 0:NS * 4].rearrange("p (t h) -> p t h", h=4),
                                                  in0=sa.t[:NS, 0:4].unsqueeze(1).broadcast_to([NS, NS, 4]),
                                                  in1=cf[:NS, IDF, :NS].unsqueeze(2).broadcast_to([NS, NS, 4]), op=ALU.mult),
                 reads=[sa, "cf", ad], writes=[ad])
            ba = bank()
            c.op("pe", lambda e: e.matmul(pb[ba][:, 0:NS * 4], lhsT=cf[:, ONES, :], rhs=ad.t[:, 0:NS * 4], start=True, stop=True),
                 reads=[ad, "cf"], writes=[PB(ba)])
            c.op("act", lambda e: e.activation(out=s_abc[:].rearrange("p t h -> p (t h)"), in_=pb[ba][:, 0:NS * 4], func=AF.Copy),
                 reads=[PB(ba)], writes=["s_abc"])
            cq_reads = [("cqT", s, SC0) for s in range(12)]
            bT = bank()
            for h in range(4):
                c.op("pe", lambda e, h=h: e.transpose(out=pbb[bT][:NS, hb(h)], in_=cqT[:, 4 + h, c0:c0 + NS], identity=identb),
                     reads=cq_reads + ["cb"], writes=[PB(bT)])
            bT2 = bank()
            for h in range(4):
                c.op("pe", lambda e, h=h: e.transpose(out=pbb[bT2][:NS, hb(h)], in_=cqT[:, 8 + h, c0:c0 + NS], identity=identb),
                     reads=cq_reads + ["cb"], writes=[PB(bT2)])
            c.op("act", lambda e: e.activation(out=s_ks[:NS, :], in_=pbb[bT][:NS, 0:512], func=AF.Copy), reads=[PB(bT)], writes=["s_ks"])
            c.op("act", lambda e: e.activation(out=s_vs[:NS, :], in_=pbb[bT2][:NS, 0:512], func=AF.Copy), reads=[PB(bT2)], writes=["s_vs"])
            build_masked(s_km, "s_km", lambda h: cqT[:, 4 + h, c0:c0 + NS], cq_reads)
            build_masked(s_qm, "s_qm", lambda h: cqT[:, h, c0:c0 + NS], cq_reads)
            for h in range(4):
                S0gs, S0bs = [], []
                for tg in range(4):
                    S0g = rF.get()
                    c.dma("sp", lambda e, S0g=S0g, tg=tg: e.dma_start(
                        out=v4(S0g.t[:, 0:512]), in_=sgdn_in[tg * 4:(tg + 1) * 4, h].rearrange("t d e -> d t e")),
                        writes=[S0g])
                    S0b = rB.get()
                    c.op("act", lambda e, S0b=S0b, S0g=S0g: e.activation(out=S0b.t[:, :], in_=S0g.t[:, 0:512], func=AF.Copy),
                         reads=[S0g], writes=[S0b])
                    S0gs.append(S0g)
                    S0bs.append(S0b)
                bK = bank()
                for t in range(NS):
                    c.op("pe", lambda e, t=t: e.matmul(pb[bK][:NS, 0:128], lhsT=s_km[:, h, t, :], rhs=S0bs[t // 4].t[:, hb(t % 4)],
                                                       start=(t == 0), stop=(t == NS - 1)), reads=["s_km", S0bs[t // 4]], writes=[PB(bK)])
                uu = rF.get()
                c.op("dve", lambda e: e.scalar_tensor_tensor(out=uu.t[:NS, 0:128], in0=pb[bK][:NS, 0:128], scalar=sa.t[:NS, h:h + 1],
                                                             op0=ALU.mult, in1=s_vs[:NS, hb(h)], op1=ALU.subtract),
                     reads=[PB(bK), sa, "s_vs"], writes=[uu])
                c.op("dve", lambda e: e.tensor_scalar(out=uu.t[:NS, 0:128], in0=uu.t[:NS, 0:128], scalar1=gsc[:NS, i, 8 + h:9 + h],
                                                      scalar2=None, op0=ALU.mult), reads=[uu, gk], writes=[uu])
                snbs = []
                for tg in range(4):
                    ub = make_ublk(uu.t[:NS, 0:128], [uu], tg)
                    a_bc = s_abc[:, tg * 4:(tg + 1) * 4, h].unsqueeze(2).broadcast_to([128, 4, 128])
                    snbs.append(sample_state_update(h, tg, S0gs[tg], s_ks[:NS, hb(h)], ub, None, a_bc, nsg_s))
                bQ = bank()
                for t in range(NS):
                    c.op("pe", lambda e, t=t: e.matmul(pb[bQ][:NS, 0:128], lhsT=s_qm[:, h, t, :], rhs=snbs[t // 4].t[:, hb(t % 4)],
                                                       start=(t == 0), stop=(t == NS - 1)), reads=["s_qm", snbs[t // 4]], writes=[PB(bQ)])
                c.op("act", lambda e, h=h: e.activation(out=s_os[:NS, hb(h)], in_=pb[bQ][:NS, 0:128], func=AF.Copy),
                     reads=[PB(bQ), "s_os"], writes=["s_os"])
            bz = bank()
            mm8(bz, NS, 512, lambda k: hT[:, k, c0:c0 + NS], lambda k: T7[:, k, :], [k7, ("hT", i)])
            c.op("act", lambda e: e.activation(out=s_gate[:NS, :], in_=pb[bz][:NS, :], func=AF.Silu), reads=[PB(bz)], writes=["s_gate"])
            on = rmsnorm_gdn(None, NS, s_os[:NS, :], "s_os")
            onorm_tail(i, NS, c0, on, None, o_gT, "ogT", gng_t, "gng", True)

        def phase_M(tiles, cgs, after_tile=None):
            load_gp(1)
            yk = []
            for half in range(2):
                Tgr = wload(w_in, 0, 8, 4104 + half * 512, 512)
                Tgg = wload(w_in, 0, 8, 5128 + half * 512, 512)
                Trb = wload(W["w_ret_branch"], 0, 4, half * 512, 512)
                Tgb = wload(W["w_gdn_branch"], 0, 4, half * 512, 512)
                for cc in range(4):
                    ch = half * 4 + cc
                    for (c0, n, tl) in cgs:
                        hk = [("hT", t) for t in tl]
                        b1, b2, b3, b4 = bank(), bank(), bank(), bank()
                        mm8(b1, 128, n, lambda k: Tgr[0][:, k, cc * 128:(cc + 1) * 128], lambda k: hT[:, k, c0:c0 + n], [Tgr[1]] + hk)
                        mm8(b2, 128, n, lambda k: Tgg[0][:, k, cc * 128:(cc + 1) * 128], lambda k: hT[:, k, c0:c0 + n], [Tgg[1]] + hk)
                        mm8(b3, 128, n, lambda k: Trb[0][:, k, cc * 128:(cc + 1) * 128], lambda k: o_rT[:, k, c0:c0 + n],
                            [Trb[1]] + [("orT", t) for t in tl], nk=4)
                        mm8(b4, 128, n, lambda k: Tgb[0][:, k, cc * 128:(cc + 1) * 128], lambda k: o_gT[:, k, c0:c0 + n],
                            [Tgb[1]] + [("ogT", t) for t in tl], nk=4)
                        s1 = rF.get()
                        c.op("act", lambda e: e.activation(out=s1.t[:, 0:n], in_=pb[b1][:, 0:n], func=AF.Sigmoid), reads=[PB(b1)], writes=[s1])
                        s2 = rF.get()
                        c.op("act", lambda e: e.activation(out=s2.t[:, 0:n], in_=pb[b2][:, 0:n], func=AF.Sigmoid), reads=[PB(b2)], writes=[s2])
                        c.op("dve", lambda e: e.tensor_tensor(out=s1.t[:, 0:n], in0=s1.t[:, 0:n], in1=pb[b3][:, 0:n], op=ALU.mult),
                             reads=[s1, PB(b3)], writes=[s1])
                        c.op("dve", lambda e: e.tensor_tensor(out=s2.t[:, 0:n], in0=s2.t[:, 0:n], in1=pb[b4][:, 0:n], op=ALU.mult),
                             reads=[s2, PB(b4)], writes=[s2])
                        ykey = ("yT", ch, c0)
                        s0_mix_keys.add(ykey)
                        c.op("pool", lambda e: e.tensor_tensor(out=yT[:, ch, c0:c0 + n], in0=s1.t[:, 0:n], in1=s2.t[:, 0:n], op=ALU.add),
                             reads=[s1, s2], writes=[ykey])
            Wo = [wload(W["w_out"], 0, 8, hf * 512, 512) for hf in range(2)]
            pend = [None]
            for (i, nrows, c0, row0) in tiles:
                cgc0 = [cc for (cc, n, tl) in cgs if cc <= c0 < cc + n][0]
                bs = []
                for hf in range(2):
                    b = bank()
                    bs.append(b)
                    mm8(b, nrows, 512, lambda k: yT[:, k, c0:c0 + nrows], lambda k: Wo[hf][0][:, k, :],
                        [Wo[hf][1]] + [("yT", k_, cgc0) for k_ in range(8)])
                if pend[0] is not None:
                    pend[0]()
                    pend[0] = None
                post(i, nrows, bs[0], bs[1], 1.0)
                if after_tile is not None:
                    pend[0] = after_tile(i, nrows, c0)
            if pend[0] is not None:
                pend[0]()

        def mk_tiles(g0, n, sample):
            t = [(i, 128, i * 128, (g0 + i) * 128) for i in range(n)]
            if sample:
                t.append((SI, NS, SC0, SEQ))
            return t

        blocks = [
            dict(g0=0, n=6, sample=True, cgs=[(0, 384, [0, 1, 2]), (384, 384 + NS, [3, 4, 5, SI])],
                 cgs_p=[(0, 384, [0, 1, 2]), (384, 384, [3, 4, 5])]),
            dict(g0=6, n=6, sample=False, cgs=[(0, 384, [0, 1, 2]), (384, 384, [3, 4, 5])],
                 cgs_p=[(0, 384, [0, 1, 2]), (384, 384, [3, 4, 5])]),
            dict(g0=12, n=4, sample=False, cgs=[(0, 512, [0, 1, 2, 3])], cgs_p=[(0, 512, [0, 1, 2, 3])]),
        ]
        if "oneblock" in SK:
            blocks = blocks[:1]
        for bi, blk in enumerate(blocks):
            tiles = mk_tiles(blk["g0"], blk["n"], blk["sample"])
            ptiles = [t for t in tiles if t[1] == 128]
            cgs = blk["cgs"]
            cgs_p = blk["cgs_p"]
            CQC = tuple(g[0] for g in cgs_p) + (SC0,)
            cur_cgs_p[0] = cgs_p
            not_sample_block[0] = not blk["sample"]
            if bi == 0 or stage < 99:
                for (i, nrows, c0, row0) in tiles:
                    c.dma("sp", lambda e, i=i, nrows=nrows, row0=row0: e.dma_start(out=x1[:nrows, i, :],
                                                                              in_=xin[row0:row0 + nrows, :]),
                          writes=[("x1", i)])
            c.dma("sp", lambda e: e.dma_start(out=cs_t[:, 0:blk["n"]], in_=cs_in[:, blk["g0"]:blk["g0"] + blk["n"]]), writes=["cs_t"])
            if blk["sample"]:
                c.dma("sp", lambda e: e.dma_start(out=cs_t[:, SI], in_=cs_in[:, 16]), reads=["cs_t"], writes=["cs_t"])
            OVL = stage == 99
            c.region("ffn1")
            if "noffn1" not in SK:
                if bi == 0 or not OVL:
                    for (i, nrows, c0, row0) in tiles:
                        prenorm(i, nrows, c0, 0)
                if bi == 0:
                    wload_after[0] = [("x1", t_[0]) for t_ in tiles]
                ffn(tiles, cgs, W["ffn1_w_gate"], W["ffn1_w_up"], W["ffn1_w_down"], 0, final=(stage == 1),
                    after_tile=(lambda i, nrows, c0: prenormA(i, nrows, c0, 1)) if OVL else None)
                wload_after[0] = ()
            if stage == 1:
                continue
            c.fence(list(s0_mix_keys) + [("orT", t[0]) for t in tiles] + [("ogT", t[0]) for t in tiles]
                    + [("cqT", s, cc_) for s in range(12) for cc_ in CQC], list(s0_ffn_keys))
            if not OVL:
                for (i, nrows, c0, row0) in tiles:
                    prenorm(i, nrows, c0, 1)
            if blk["sample"] and "sample" not in SK and "nsconv" not in SK:
                for r in range(3):
                    stg = [rF.get() for _ in range(3)]
                    for q_ in range(3):
                        c.dma("sp", lambda e, r=r, q_=q_: e.dma_start(out=stg[q_].t[:NS, 0:512], in_=sconv_in[:, r, q_ * 512:(q_ + 1) * 512]),
                              writes=[stg[q_]])
                    bt = bank()
                    for s in range(12):
                        c.op("pe", lambda e, s=s: e.transpose(out=pb[bt][:, s * NS:(s + 1) * NS],
                                                              in_=stg[s // 4].t[:NS, (s % 4) * 128:(s % 4 + 1) * 128],
                                                              identity=cf[:NS, IDF, :NS]), reads=[stg[s // 4], "cf"], writes=[PB(bt)])
                    c.op("act", lambda e, r=r: e.activation(out=scb[:, :, r, :], in_=pb[bt][:, 0:12 * NS].rearrange("p (s t) -> p s t", s=12),
                                                            func=AF.Copy), reads=[PB(bt), "scb"] if r else [PB(bt)], writes=["scb"])
                c.dma("sp", lambda e: e.dma_start(out=nsc_s[:, 0:2, :], in_=sconv_in[:, 1:3, :]), is_output=True)
            smp = blk["sample"] and "sample" not in SK
            c.region("R")
            g1 = phase_G1(cgs_p, smp and "nsg1" not in SK, bi == len(blocks) - 1) if "G1" not in SK else iter(())
            if "R" not in SK:
                phase_R(ptiles, smp and "nsret" not in SK, g1 if OVL else None)
            for _ in g1:
                pass
            c.region("G2")
            if "G2" not in SK:
                phase_G2(ptiles[:1] if "G2one" in SK else ptiles, smp and "nsgdn" not in SK)
            c.fence([("yT", ch, g[0]) for ch in range(8) for g in cgs], [("cqT", s, cc_) for s in range(12) for cc_ in CQC])
            c.region("M")
            if "M" not in SK:
                phase_M(tiles, cgs, after_tile=(lambda i, nrows, c0: prenormA(i, nrows, c0, 2)) if OVL else None)
            if bi == len(blocks) - 1:
                c.dma("sp", lambda e: e.dma_start(out=nsr_p.rearrange("h d e -> d h e"), in_=Sret[:]), reads=["Sret"], is_output=True)
                c.dma("sp", lambda e: e.dma_start(out=nsg_p.rearrange("h d e -> d h e"), in_=Sgdn[:]), reads=["Sgdn"], is_output=True)
            if stage == 2:
                for (i, nrows, c0, row0) in tiles:
                    c.dma("sp", lambda e, i=i, nrows=nrows, row0=row0: e.dma_start(out=y_out[row0:row0 + nrows, :], in_=x1[:nrows, i, :]),
                          reads=[("x1", i)], is_output=True)
                c.fence(list(s0_ffn_keys), list(s0_mix_keys))
                continue
            c.region("ffn2")
            c.fence(list(s0_ffn_keys), list(s0_mix_keys))
            if not OVL:
                for (i, nrows, c0, row0) in tiles:
                    prenorm(i, nrows, c0, 2)
            nxt = blocks[bi + 1] if bi + 1 < len(blocks) else None
            ntiles = {t[0]: t for t in mk_tiles(nxt["g0"], nxt["n"], nxt["sample"])} if nxt else {}

            def next_block_prefetch(i, nrows, c0):
                if i in ntiles:
                    (i2, nrows2, c02, row02) = ntiles[i]
                    c.dma("sp", lambda e: e.dma_start(out=x1[:nrows2, i2, :], in_=xin[row02:row02 + nrows2, :]),
                          writes=[("x1", i2)])
                    return prenormA(i2, nrows2, c02, 0)
                return None

            ffn(tiles, cgs, W["ffn2_w_gate"], W["ffn2_w_up"], W["ffn2_w_down"], 2, final=True,
                after_tile=next_block_prefetch if OVL else None)

        c.finish("sp")
        c.run_block()
    return nc


def host_consts():
    f32 = np.float32
    p = np.arange(128)
    cf = np.zeros((128, NCF, 128), f32)
    cf[:, 0, :] = np.eye(128)
    cf[:, 1, :] = (p[:, None] <= p[None, :])
    cf[:, 2, :] = (p[:, None] > p[None, :])
    cf[:, 3, :] = (p[:, None] < p[None, :])
    cf[:, 4, :] = 1.0
    cf[:, 5:7, :] = np.eye(NS, dtype=f32).reshape(1, 2, 128)
    lg = np.log(np.array(GAMMA, np.float64))
    rt = np.zeros((128, 2, 4, 128), np.float64)
    for h in range(4):
        dmt = np.exp((p[None, :] - p[:, None]) * lg[h]) * (p[:, None] <= p[None, :]) * DK ** -0.5
        rt[:, 0, h, :] = dmt
        rt[:, 1, h, :] = np.exp((p[None, :] + 1) * lg[h])
    kdec = np.exp((127 - p[:, None]) * lg[None, :]) * DK ** -0.5
    inv = (10000.0 ** (-(np.arange(0, 128, 2, dtype=f32)) / f32(128))).astype(f32)
    pos = np.concatenate([np.arange(SEQ, dtype=f32), np.full((128,), 16384.0, f32)])
    ang = (pos[:, None] * inv[None, :]).astype(f32)
    cs = np.stack([np.cos(ang), np.sin(ang), -np.sin(ang)], axis=1).astype(f32)
    cs = cs.reshape(17, 128, 3, 64).transpose(1, 0, 2, 3)
    return dict(constf=cf, rettab=rt.astype(f32), kdec=kdec.astype(f32), cossin=np.ascontiguousarray(cs))


_CACHE = {}


def kernel(**inp):
    f32 = np.float32
    stage = inp.pop("_stage", 99)
    debug = inp.pop("_debug", False)
    ncores = inp.pop("_cores", 8)
    key = (stage, debug, tuple(sorted(DBG_SKIP)))
    if key not in _CACHE:
        _CACHE[key] = build(stage, debug)
    nc = _CACHE[key]
    hc = host_consts()
    g = lambda n: np.asarray(inp[n], f32)[0]
    shared = {}
    for nm in ["ffn1_w_gate", "ffn1_w_up", "ffn1_w_down", "w_in", "w_ret_branch", "w_gdn_branch", "w_out",
               "ffn2_w_gate", "ffn2_w_up", "ffn2_w_down"]:
        shared[nm] = np.ascontiguousarray(g(nm))
    gpre = np.concatenate([g(n).reshape(8, 128).T for n in ("ffn1_pre_g", "mix_pre_g", "ffn2_pre_g")], axis=1)
    shared["gpre"] = np.ascontiguousarray(gpre)
    shared["gpost"] = np.ascontiguousarray(np.stack([g("ffn1_post_g"), g("mix_post_g"), g("ffn2_post_g")]))
    shared["ret_norm_g"] = g("ret_norm_g").reshape(1, 512)
    shared["gdn_norm_g"] = g("gdn_norm_g").reshape(1, 128)
    cw = g("gdn_conv_w")
    shared["convw"] = np.ascontiguousarray(cw.reshape(4, 12, 128).transpose(2, 1, 0))
    shared["a_log"] = g("gdn_a_log").reshape(1, 4)
    shared["dt_bias"] = g("gdn_dt_bias").reshape(1, 4)
    shared.update(hc)
    xp = np.asarray(inp["x_prompt"], f32)
    xs = np.asarray(inp["x_sample"], f32)
    sr = np.asarray(inp["state_ret"], f32)[0]
    sg = np.asarray(inp["state_gdn"], f32)[0]
    sc = np.asarray(inp["state_conv"], f32)[0]
    in_maps = []
    for b in range(ncores):
        m = dict(shared)
        m["xin"] = np.ascontiguousarray(np.concatenate([xp[b], xs[b * NS:(b + 1) * NS, 0, :]], axis=0))
        m["sret"] = np.ascontiguousarray(sr[b * NS:(b + 1) * NS])
        m["sgdn"] = np.ascontiguousarray(sg[b * NS:(b + 1) * NS])
        m["sconv"] = np.ascontiguousarray(sc[b * NS:(b + 1) * NS])
        in_maps.append(m)
    res = run_bass_kernel_spmd(nc, in_maps, core_ids=list(range(ncores)))
    R = list(res.results) + [res.results[0]] * (8 - ncores)
    yp = np.stack([R[b]["y"][:SEQ] for b in range(8)])
    ys = np.concatenate([R[b]["y"][SEQ:] for b in range(8)])[:, None, :]
    nrp = np.stack([R[b]["nsr_p"] for b in range(8)])[None]
    ngp = np.stack([R[b]["nsg_p"] for b in range(8)])[None]
    ncp = np.stack([R[b]["nsc_p"] for b in range(8)])[None]
    nrs = np.concatenate([R[b]["nsr_s"] for b in range(8)])[None]
    ngs = np.concatenate([R[b]["nsg_s"] for b in range(8)])[None]
    ncs = np.concatenate([R[b]["nsc_s"] for b in range(8)])[None]
    out = (yp, ys, nrp, ngp, ncp, nrs, ngs, ncs)
    if debug:
        return out, [R[b]["dbg"] for b in range(8)]
    return tuple(np.ascontiguousarray(o, dtype=f32) for o in out)
```

```python
import numpy as np
from contextlib import ExitStack
import concourse.bass as bass
import concourse.mybir as mybir
from concourse.bass_utils import run_bass_kernel_spmd

F32 = mybir.dt.float32
BF16 = mybir.dt.bfloat16
AF = mybir.ActivationFunctionType
ALU = mybir.AluOpType
AX = mybir.AxisListType

ENGS = ("pe", "act", "dve", "pool", "sp")

D = 1024
DFF = 2816
NJ = 22
DIN = 6152
SEQ = 2048
NS = 16
EPS = 1e-6
GAMMA = [1.0 - 2.0 ** (-5.0 - h) for h in range(4)]
DK = 128
NCF = 7


class H:
    __slots__ = ("t", "key", "rot", "idx", "gen")

    def __init__(self, t, key, rot, idx, gen):
        self.t, self.key, self.rot, self.idx, self.gen = t, key, rot, idx, gen


def _res(keys):
    out = []
    for k in keys:
        if isinstance(k, H):
            assert k.rot.gen[k.idx] == k.gen, "stale scratch tile %s" % (k.key,)
            out.append(k.key)
        else:
            out.append(k)
    return out


class _Rec:
    def __init__(self):
        self.call = None

    def __getattr__(self, name):
        if name.startswith("__"):
            raise AttributeError(name)

        def f(*a, **kw):
            self.call = (name, a, kw)
            return self
        return f

    def then_inc(self, *a, **k):
        return self


def _free_size(ap):
    try:
        n = 1
        for d in ap.shape[1:]:
            n *= int(d)
        return n
    except Exception:
        return 256


SCHED = True
PSUM_EXCL = True
HYBRID_K = None
TABLE_AWARE = False
REGION_FLUSH = False
KEEP_ORDER = set()
SCHED_REGIONS = {"ffn1", "ffn2", "G2", "M", "R"}


class Ctx:
    NDSEM = 28

    def __init__(self, nc, es, same_eng_sync=("act", "dve", "pool")):
        self.nc = nc
        self.es = es
        self.same_eng_sync = same_eng_sync
        self.q = {e: [] for e in ENGS}
        self.cnt = {e: 0 for e in ENGS}
        self.waited = {e: {} for e in ENGS}
        self.sems = {}
        self.EPOCH = 2000
        for i in range(self.NDSEM):
            self.sems[("d", i)] = es.enter_context(nc.semaphore("s_d%d" % i))
        self.ndma = 0
        self.res = {}
        self.out_events = []
        self.rec = SCHED
        self.nodes = []

    def sb(self, name, shape, dt):
        return self.es.enter_context(self.nc.sbuf_tensor(name, list(shape), dt))

    def ps(self, name, shape, dt):
        return self.es.enter_context(self.nc.psum_tensor(name, list(shape), dt))

    def _deps(self, reads, writes, eng=None):
        deps = []
        for r in reads:
            st = self.res.get(r)
            if st and st["w"]:
                deps.append(st["w"])
            if st and PSUM_EXCL and isinstance(r, tuple) and r[0] == "pb":
                for ev in st["r"].items():
                    if ev[0][0] != eng:
                        deps.append(ev)
        for w in writes:
            st = self.res.get(w)
            if st:
                if st["w"]:
                    deps.append(st["w"])
                deps.extend(st["r"].items())
        return deps

    def _emit_waits(self, eng, deps):
        wd = self.waited[eng]
        best = {}
        for (k, v) in deps:
            if k[0] == eng and (eng == "pe" or eng not in self.same_eng_sync):
                continue
            if wd.get(k, 0) < v and best.get(k, 0) < v:
                best[k] = v
        for k, v in best.items():
            wd[k] = v
            self._eo(eng).wait_ge(self.sems[k], v)

    def _update(self, ev, reads, writes):
        for r in reads:
            st = self.res.setdefault(r, {"w": None, "r": {}})
            if st["r"].get(ev[0], 0) < ev[1]:
                st["r"][ev[0]] = ev[1]
        for w in writes:
            self.res[w] = {"w": ev, "r": {}}

    def fence(self, new_keys, old_keys):
        if self.rec:
            self.nodes.append(("fence", list(new_keys), list(old_keys)))
            return
        self._fence(new_keys, old_keys)

    def _fence(self, new_keys, old_keys):
        acc = {}
        for ok in old_keys:
            st = self.res.get(ok)
            if not st:
                continue
            evs = list(st["r"].items())
            if st["w"]:
                evs.append(st["w"])
            for k, v in evs:
                if acc.get(k, 0) < v:
                    acc[k] = v
        for nk in new_keys:
            st = self.res.setdefault(nk, {"w": None, "r": {}})
            for k, v in acc.items():
                if st["r"].get(k, 0) < v:
                    st["r"][k] = v

    def op(self, eng, fn, reads=(), writes=()):
        reads, writes = _res(reads), _res(writes)
        if self.rec:
            r = _Rec()
            fn(r)
            self.nodes.append(("op", eng, r.call, reads, writes))
            return None
        return self._op(eng, fn, reads, writes)

    def _op(self, eng, fn, reads, writes):
        deps = self._deps(reads, writes, eng)
        self._emit_waits(eng, deps)
        self.cnt[eng] += 1
        n = self.cnt[eng] - 1
        sk = (eng, n // self.EPOCH)
        if sk not in self.sems:
            self.sems[sk] = self.es.enter_context(self.nc.semaphore("s_%s_%d" % sk))
        ev = (sk, n % self.EPOCH + 1)
        fn(self._eo(eng)).then_inc(self.sems[sk], 1)
        self._update(ev, reads, writes)
        return ev

    def dma(self, queue, fn, reads=(), writes=(), is_output=False):
        reads, writes = _res(reads), _res(writes)
        if self.rec:
            r = _Rec()
            fn(r)
            self.nodes.append(("dma", queue, r.call, reads, writes, is_output))
            return None
        return self._dma(queue, fn, reads, writes, is_output)

    def _dma(self, queue, fn, reads, writes, is_output):
        n = self.ndma
        self.ndma += 1
        k = ("d", n % self.NDSEM)
        rnd = n // self.NDSEM
        deps = self._deps(reads, writes)
        if rnd > 0:
            deps.append((k, 16 * rnd))
        self._emit_waits(queue, deps)
        ev = (k, 16 * (rnd + 1))
        fn(self._eo(queue)).then_inc(self.sems[k], 16)
        self._update(ev, reads, writes)
        if is_output:
            self.out_events.append(ev)
        return ev

    def _eo(self, e):
        nc = self.nc
        return {"pe": nc.tensor, "act": nc.scalar, "dve": nc.vector, "pool": nc.gpsimd, "sp": nc.sync}[e]

    def region(self, name):
        if not REGION_FLUSH:
            return
        self.flush()
        self.rec = SCHED and (SCHED_REGIONS is None or name in SCHED_REGIONS)

    def finish(self, eng="sp"):
        self.flush()
        self._emit_waits(eng, self.out_events)

    def _est(self, nd):
        kind, eng, call = nd[0], nd[1], nd[2]
        name, a, kw = call
        if kind == "dma":
            out = kw.get("out", a[0] if a else None)
            nbytes = _free_size(out) * int(out.shape[0]) * 4 if out is not None else 65536
            lat = 2.0 + nbytes / 250e3
            return (1.0 if eng == "pool" else 0.06), lat
        if eng == "pe":
            if name == "transpose":
                return 0.1, 0.1
            rhs = kw.get("rhs")
            n = _free_size(rhs) if rhs is not None else 128
            d = 0.04 + n / 2600.0
            if rhs is not None and rhs.dtype == F32:
                d *= 4
            return d, d
        out = kw.get("out", a[0] if a else None)
        n = _free_size(out) if out is not None else 256
        if eng == "act":
            d = 0.30 + n / 1200.0
        elif eng == "dve":
            d = 0.16 + n / 1200.0
        else:
            d = 0.25 + n / 550.0
        return d, d

    def flush(self):
        import heapq
        nodes, self.nodes = self.nodes, []
        if not nodes:
            return
        self.rec = False
        N = len(nodes)
        lastw, readers = {}, {}
        preds = [None] * N
        succs = [[] for _ in range(N)]
        for idx, nd in enumerate(nodes):
            if nd[0] == "fence":
                rd, wr = (), list(nd[1]) + list(nd[2])
            else:
                rd, wr = nd[3], nd[4]
            p = set()
            for r in rd:
                w = lastw.get(r)
                if w is not None:
                    p.add(w)
                if PSUM_EXCL and isinstance(r, tuple) and r[0] == "pb":
                    for q in readers.get(r, ()):
                        if nodes[q][1] != nd[1]:
                            p.add(q)
            for w_ in wr:
                w = lastw.get(w_)
                if w is not None:
                    p.add(w)
                rs = readers.get(w_)
                if rs:
                    p.update(rs)
            p.discard(idx)
            preds[idx] = p
            for q in p:
                succs[q].append(idx)
            for r in rd:
                readers.setdefault(r, set()).add(idx)
            for w_ in wr:
                lastw[w_] = idx
                readers[w_] = set()
        if KEEP_ORDER:
            last_e = {}
            for idx, nd in enumerate(nodes):
                if nd[0] == "fence":
                    continue
                e = nd[1] if nd[0] == "op" else "dma_" + nd[1]
                if e in KEEP_ORDER:
                    q = last_e.get(e)
                    if q is not None and q not in preds[idx]:
                        preds[idx].add(q)
                        succs[q].append(idx)
                    last_e[e] = idx
        occ = [0.0] * N
        lat = [0.0] * N
        engs = [None] * N
        aset = [0] * N
        for i, nd in enumerate(nodes):
            if nd[0] != "fence":
                occ[i], lat[i] = self._est(nd)
                engs[i] = nd[1]
                if nd[0] == "op" and nd[1] == "act":
                    f_ = nd[2][2].get("func")
                    if f_ == AF.Silu:
                        aset[i] = 1
                    elif f_ == AF.Exp or f_ == AF.Ln:
                        aset[i] = 2
                    elif f_ == AF.Sigmoid:
                        aset[i] = 3
        cur_set = [0]
        TBL = 1.3
        blev = [0.0] * N
        for i in range(N - 1, -1, -1):
            b = 0.0
            for q in succs[i]:
                if blev[q] > b:
                    b = blev[q]
            blev[i] = b + lat[i]
        LAT = 0.15
        indeg = [len(p) for p in preds]
        finish = [0.0] * N
        eng_free = {e: 0.0 for e in ENGS}
        pending = {e: [] for e in ENGS}
        avail = {e: [] for e in ENGS}
        order = []

        def ready_time(i):
            t = 0.0
            for q in preds[i]:
                l = 0.0 if (engs[q] == "pe" and engs[i] == "pe") or engs[q] is None else LAT
                if finish[q] + l > t:
                    t = finish[q] + l
            return t

        def release(i):
            stack = [i]
            while stack:
                j = stack.pop()
                for q in succs[j]:
                    indeg[q] -= 1
                    if indeg[q] == 0:
                        rt = ready_time(q)
                        if engs[q] is None:
                            finish[q] = rt
                            order.append(q)
                            stack.append(q)
                        else:
                            heapq.heappush(pending[engs[q]], (rt, q))

        for i in range(N):
            if indeg[i] == 0:
                if engs[i] is None:
                    finish[i] = 0.0
                    order.append(i)
                    release(i)
                else:
                    heapq.heappush(pending[engs[i]], (0.0, i))
        nsched = sum(1 for e in engs if e is not None)
        done = 0
        while done < nsched:
            best = None
            for e in ENGS:
                pe_, av = pending[e], avail[e]
                while pe_ and pe_[0][0] <= eng_free[e]:
                    rt, q = heapq.heappop(pe_)
                    heapq.heappush(av, (-blev[q], q))
                if av:
                    if e == "act" and TABLE_AWARE:
                        pick = None
                        for it in heapq.nsmallest(8, av):
                            if aset[it[1]] == 0 or aset[it[1]] == cur_set[0]:
                                pick = it
                                break
                        if pick is None:
                            pick = av[0]
                            cand = (eng_free[e] + TBL, pick[0], pick[1], e, pick)
                        else:
                            cand = (eng_free[e], pick[0], pick[1], e, pick)
                    else:
                        cand = (eng_free[e], av[0][0], av[0][1], e, True)
                elif pe_:
                    cand = (pe_[0][0], -blev[pe_[0][1]], pe_[0][1], e, False)
                else:
                    continue
                if best is None or cand[:3] < best[:3]:
                    best = cand
            st, _, i, e, from_av = best
            if from_av is True:
                heapq.heappop(avail[e])
            elif from_av is False:
                heapq.heappop(pending[e])
            else:
                avail[e].remove(from_av)
                heapq.heapify(avail[e])
            if e == "act" and aset[i] != 0:
                if not from_av and aset[i] != cur_set[0] and TABLE_AWARE:
                    st += TBL
                cur_set[0] = aset[i]
            finish[i] = st + lat[i]
            eng_free[e] = st + occ[i]
            order.append(i)
            done += 1
            release(i)
        assert len(order) == N, (len(order), N)
        if HYBRID_K is not None:
            pre = order[:HYBRID_K]
            ps_ = set(pre)
            order = pre + [i for i in range(N) if i not in ps_]
            self.hyb_info = (N, [nodes[i][:2] + (nodes[i][2][0],) if nodes[i][0] != "fence" else ("fence",) for i in order[max(0, HYBRID_K - 3):HYBRID_K + 1]])
        self.sched_makespan = getattr(self, "sched_makespan", 0.0) + (max(finish) if finish else 0.0)
        for i in order:
            nd = nodes[i]
            if nd[0] == "fence":
                self._fence(nd[1], nd[2])
            elif nd[0] == "op":
                name, a, kw = nd[2]
                self._op(nd[1], lambda e, name=name, a=a, kw=kw: getattr(e, name)(*a, **kw), nd[3], nd[4])
            else:
                name, a, kw = nd[2]
                self._dma(nd[1], lambda e, name=name, a=a, kw=kw: getattr(e, name)(*a, **kw), nd[3], nd[4], nd[5])
        self.rec = True

    def replay(self, e, engobj):
        for item in self.q[e]:
            if item[0] == "wait":
                engobj.wait_ge(self.sems[item[1]], item[2])
            elif item[0] == "op":
                item[1](engobj).then_inc(self.sems[e], 1)
            else:
                item[1](engobj).then_inc(self.sems[item[2]], 16)

    def run_block(self):
        return
        nc = self.nc
        with nc.Block() as block:
            @block.sync
            def _(e):
                self.replay("sp", e)

            @block.tensor
            def _(e):
                self.replay("pe", e)

            @block.scalar
            def _(e):
                self.replay("act", e)

            @block.vector
            def _(e):
                self.replay("dve", e)

            @block.gpsimd
            def _(e):
                self.replay("pool", e)


class Rot:
    def __init__(self, c, name, n, shape, dt):
        self.t = [c.sb("%s%d" % (name, i), shape, dt) for i in range(n)]
        self.name = name
        self.gen = [0] * n
        self.i = 0

    def get(self):
        i = self.i
        self.i = (i + 1) % len(self.t)
        self.gen[i] += 1
        return H(self.t[i], (self.name, i), self, i, self.gen[i])


DBG_SKIP = set()


def build(stage=99, debug=False):
    nc = bass.Bass("TRN2", target_bir_lowering=False)
    SK = DBG_SKIP
    G2CUT = 99
    for f_ in SK:
        if f_.startswith('cut'):
            G2CUT = int(f_[3:])

    def din(name, shape):
        return nc.dram_tensor(name, list(shape), F32, kind="ExternalInput").ap()

    def dout(name, shape):
        return nc.dram_tensor(name, list(shape), F32, kind="ExternalOutput").ap()

    xin = din("xin", [SEQ + NS, D])
    sret_in = din("sret", [NS, 4, 128, 128])
    sgdn_in = din("sgdn", [NS, 4, 128, 128])
    sconv_in = din("sconv", [NS, 3, 1536])
    W = {}
    for nm, shp in [("ffn1_w_gate", [D, DFF]), ("ffn1_w_up", [D, DFF]), ("ffn1_w_down", [DFF, D]),
                    ("w_in", [D, DIN]), ("w_ret_branch", [512, D]), ("w_gdn_branch", [512, D]),
                    ("w_out", [D, D]),
                    ("ffn2_w_gate", [D, DFF]), ("ffn2_w_up", [D, DFF]), ("ffn2_w_down", [DFF, D])]:
        W[nm] = din(nm, shp)
    w_in = W["w_in"]
    gpre_in = din("gpre", [128, 24])
    gpost_in = din("gpost", [3, D])
    rng_in = din("ret_norm_g", [1, 512])
    gng_in = din("gdn_norm_g", [1, 128])
    convw_in = din("convw", [128, 12, 4])
    alog_in = din("a_log", [1, 4])
    dtb_in = din("dt_bias", [1, 4])
    cs_in = din("cossin", [128, 17, 3, 64])
    cf_in = din("constf", [128, NCF, 128])
    rt_in = din("rettab", [128, 2, 4, 128])
    kdec_in = din("kdec", [128, 4])

    y_out = dout("y", [SEQ + NS, D])
    nsr_p = dout("nsr_p", [4, 128, 128])
    nsg_p = dout("nsg_p", [4, 128, 128])
    nsc_p = dout("nsc_p", [3, 1536])
    nsr_s = dout("nsr_s", [NS, 4, 128, 128])
    nsg_s = dout("nsg_s", [NS, 4, 128, 128])
    nsc_s = dout("nsc_s", [NS, 3, 1536])

    with ExitStack() as es:
        c = Ctx(nc, es)
        es.enter_context(nc.allow_non_contiguous_dma(reason="tiny strided state outputs"))
        NTB = 7
        CB = 784
        SC0 = 768
        SI = 6
        x1 = c.sb("x1", [128, NTB, D], F32)
        hT = c.sb("hT", [128, 8, CB], BF16)
        S0 = c.sb("S0", [128, NJ * CB], BF16)
        aT = S0[:, :].rearrange("p (j t) -> p j t", j=NJ)
        o_rT = S0[:, 0:4 * CB].rearrange("p (h t) -> p h t", h=4)
        o_gT = S0[:, 4 * CB:8 * CB].rearrange("p (h t) -> p h t", h=4)
        cqT = S0[:, 8 * CB:20 * CB].rearrange("p (s t) -> p s t", s=12)
        yT = S0[:, 8 * CB:16 * CB].rearrange("p (k t) -> p k t", k=8)
        NSLOT = 7
        ring = [c.sb("ring%d" % i, [128, 8, 512], BF16) for i in range(NSLOT)]
        wsmall = c.sb("wsmall", [128, 8, 8], BF16)
        gp = c.sb("gp", [128, D], F32)
        gpre = c.sb("gpre_sb", [128, 24], F32)
        cs_t = c.sb("cs_t", [128, NTB, 3, 64], F32)
        cf = c.sb("cf", [128, NCF, 128], F32)
        cb = c.sb("cb", [128, NCF, 128], BF16)
        rettab = c.sb("rettab_sb", [128, 2, 4, 128], F32)
        kdec = c.sb("kdec_sb", [128, 4], F32)
        rng_t = c.sb("rng_t", [128, 512], F32)
        gng_t = c.sb("gng_t", [128, 128], F32)
        convw = c.sb("convw_sb", [128, 12, 4], F32)
        alog_t = c.sb("alog_t", [128, 4], F32)
        dtb_t = c.sb("dtb_t", [128, 4], F32)
        nega_t = c.sb("nega_t", [128, 4], F32)
        nhalf = c.sb("nhalf", [128, 4], F32)
        epst = c.sb("epst", [128, 4], F32)
        Sret = c.sb("Sret", [128, 4, 128], F32)
        Sretb = c.sb("Sretb", [128, 4, 128], BF16)
        Sgdn = c.sb("Sgdn", [128, 4, 128], F32)
        Sgdnb = c.sb("Sgdnb", [128, 4, 128], BF16)
        car = c.sb("car", [128, 12, 3], F32)
        scb = c.sb("scb", [128, 12, 3, NS], F32)
        gsc = c.sb("gsc", [128, NTB, 12], F32)
        s_qm = c.sb("s_qm", [128, 4, NS, NS], BF16)
        s_km = c.sb("s_km", [128, 4, NS, NS], BF16)
        s_os = c.sb("s_os", [128, 512], F32)
        s_vs = c.sb("s_vs", [128, 512], F32)
        s_ks = c.sb("s_ks", [128, 512], BF16)
        s_abc = c.sb("s_abc", [128, NS, 4], F32)
        s_gate = c.sb("s_gate", [128, 512], BF16)
        rF = Rot(c, "rF", 9, [128, 520], F32)
        rB = Rot(c, "rB", 12, [128, 512], BF16)
        rBB = Rot(c, "rBB", 2, [128, 1024], BF16)
        rS = Rot(c, "rS", 24, [128, 16], F32)
        pb = [c.ps("pb%d" % i, [128, 512], F32) for i in range(8)]
        pbb = [p.bitcast(BF16) for p in pb]
        bank_i = [0]
        bank_gen = [0] * 8

        class BK(int):
            pass

        def bank():
            b = bank_i[0]
            bank_i[0] = (b + 1) % 8
            bank_gen[b] += 1
            r = BK(b)
            r.gen = bank_gen[b]
            return r

        IDF, TRI, SU, MASKS, ONES, DI0, DI1 = 0, 1, 2, 3, 4, 5, 6
        def PB(b):
            assert bank_gen[int(b)] == b.gen, "stale psum bank %d" % int(b)
            return ("pb", int(b))

        def v4(ap):
            return ap.rearrange("p (h e) -> p h e", h=4)

        def hb(h):
            return slice(h * 128, (h + 1) * 128)

        def ld(dst, src, key):
            c.dma("sp", lambda e: e.dma_start(out=dst, in_=src), writes=[key])

        ld(gpre[:], gpre_in, "gpre")
        ld(cf[:], cf_in, "cf")
        ld(rettab[:], rt_in, "rettab")
        ld(kdec[:], kdec_in, "kdec")
        ld(convw[:], convw_in, "convw")
        ld(rng_t[:], rng_in.partition_broadcast(128), "rng")
        ld(gng_t[:], gng_in.partition_broadcast(128), "gng")
        ld(alog_t[:], alog_in.partition_broadcast(128), "alog")
        ld(dtb_t[:], dtb_in.partition_broadcast(128), "dtb")
        c.op("dve", lambda e: e.tensor_copy(out=cb[:], in_=cf[:]), reads=["cf"], writes=["cb"])
        c.op("pool", lambda e: e.memset(nhalf[:], -0.5), writes=["nhalf"])
        c.op("pool", lambda e: e.memset(epst[:], EPS), writes=["epst"])
        for t_, k_ in ((Sret, "Sret"), (Sgdn, "Sgdn"), (Sretb, "Sretb"), (Sgdnb, "Sgdnb"), (car, "car")):
            c.op("pool", lambda e, t_=t_: e.memset(t_[:], 0.0), writes=[k_])
        identb = cb[:, IDF, :]

        wstate = {"n": 0}

        wload_after = [()]

        def wload(src, r0, kc, c0, ncols):
            s = wstate["n"] % NSLOT
            wstate["n"] += 1
            t, key = ring[s], ("ring", s)
            v = src[r0 * 128:(r0 + kc) * 128, c0:c0 + ncols].rearrange("(c p) n -> p c n", p=128)
            c.dma("pool", lambda e: e.dma_start(out=t[:, 0:kc, 0:ncols], in_=v), reads=list(wload_after[0]), writes=[key])
            return t, key

        def mm8(b, nrows_out, ncols, lhsT_fn, rhs_fn, reads, nk=8):
            for k in range(nk):
                lt, rt_ = lhsT_fn(k), rhs_fn(k)
                c.op("pe", lambda e, k=k, lt=lt, rt_=rt_: e.matmul(pb[b][:nrows_out, 0:ncols], lhsT=lt, rhs=rt_,
                                                                  start=(k == 0), stop=(k == nk - 1)),
                     reads=reads, writes=[PB(b)])

        def prenormA(i, nrows, c0, gidx):
            kx = ("x1", i)
            jt = rBB.get()
            ss = rS.get()
            c.op("act", lambda e: e.activation(out=jt.t[:nrows, :], in_=x1[:nrows, i, :], func=AF.Square,
                                               accum_out=ss.t[:nrows, 0:1]), reads=[kx], writes=[jt, ss])
            c.op("dve", lambda e: e.tensor_scalar(out=ss.t[:nrows, 1:2], in0=ss.t[:nrows, 0:1], scalar1=1.0 / D,
                                                  scalar2=EPS, op0=ALU.mult, op1=ALU.add), reads=[ss], writes=[ss])
            c.op("pool", lambda e: e.tensor_tensor(out=ss.t[:nrows, 2:3], in0=ss.t[:nrows, 1:2], in1=nhalf[:nrows, 0:1],
                                                   op=ALU.pow), reads=[ss, "nhalf"], writes=[ss])
            hn = rBB.get()
            c.op("dve", lambda e: e.tensor_scalar(out=hn.t[:nrows, :], in0=x1[:nrows, i, :], scalar1=ss.t[:nrows, 2:3],
                                                  scalar2=None, op0=ALU.mult), reads=[kx, ss], writes=[hn])

            def partB():
                b = bank()
                for k in range(8):
                    c.op("pe", lambda e, k=k: e.transpose(out=pbb[b][:, k * 128:k * 128 + nrows],
                                                          in_=hn.t[:nrows, k * 128:(k + 1) * 128],
                                                          identity=cb[:nrows, IDF, :nrows]),
                         reads=[hn, "cb"], writes=[PB(b)])
                src = pbb[b][:, :].rearrange("p (k t) -> p k t", k=8)[:, :, 0:nrows]
                gb = gpre[:, gidx * 8:(gidx + 1) * 8].unsqueeze(2).broadcast_to([128, 8, nrows])
                c.op("dve", lambda e: e.tensor_tensor(out=hT[:, :, c0:c0 + nrows], in0=src, in1=gb, op=ALU.mult),
                     reads=[PB(b), "gpre"], writes=[("hT", i)])
            return partB

        def prenorm(i, nrows, c0, gidx):
            prenormA(i, nrows, c0, gidx)()

        def load_gp(row):
            c.dma("sp", lambda e: e.dma_start(out=gp[:], in_=gpost_in[row:row + 1, :].partition_broadcast(128)),
                  writes=["gp"])

        def post(i, nrows, b0, b1, scale, final_row0=None):
            ss = rS.get()
            jt = rBB.get()
            c.op("act", lambda e: e.activation(out=jt.t[:nrows, 0:512], in_=pb[b0][:nrows, :], func=AF.Square,
                                               accum_out=ss.t[:nrows, 0:1]), reads=[PB(b0)], writes=[jt, ss])
            c.op("act", lambda e: e.activation(out=jt.t[:nrows, 512:1024], in_=pb[b1][:nrows, :], func=AF.Square,
                                               accum_out=ss.t[:nrows, 1:2]), reads=[PB(b1)], writes=[jt, ss])
            c.op("dve", lambda e: e.tensor_tensor(out=ss.t[:nrows, 2:3], in0=ss.t[:nrows, 0:1], in1=ss.t[:nrows, 1:2],
                                                  op=ALU.add), reads=[ss], writes=[ss])
            m = 1.0 / (scale * scale)
            c.op("dve", lambda e: e.tensor_scalar(out=ss.t[:nrows, 3:4], in0=ss.t[:nrows, 2:3], scalar1=m / D,
                                                  scalar2=EPS * m, op0=ALU.mult, op1=ALU.add), reads=[ss], writes=[ss])
            c.op("pool", lambda e: e.tensor_tensor(out=ss.t[:nrows, 4:5], in0=ss.t[:nrows, 3:4], in1=nhalf[:nrows, 0:1],
                                                   op=ALU.pow), reads=[ss, "nhalf"], writes=[ss])
            for hf, bb in ((0, b0), (1, b1)):
                t = rF.get()
                c.op("dve", lambda e, t=t, bb=bb, hf=hf: e.scalar_tensor_tensor(
                    out=t.t[:nrows, 0:512], in0=pb[bb][:nrows, :], scalar=ss.t[:nrows, 4:5], op0=ALU.mult,
                    in1=gp[:nrows, hf * 512:(hf + 1) * 512], op1=ALU.mult),
                    reads=[PB(bb), ss, "gp"], writes=[t])
                c.op("pool", lambda e, t=t, hf=hf: e.tensor_tensor(
                    out=x1[:nrows, i, hf * 512:(hf + 1) * 512], in0=x1[:nrows, i, hf * 512:(hf + 1) * 512],
                    in1=t.t[:nrows, 0:512], op=ALU.add), reads=[t, ("x1", i)], writes=[("x1", i)])
            if final_row0 is not None:
                c.dma("sp", lambda e: e.dma_start(out=y_out[final_row0:final_row0 + nrows, :], in_=x1[:nrows, i, :]),
                      reads=[("x1", i)], is_output=True)

        s0_ffn_keys = set()
        s0_mix_keys = set()

        def ffn(tiles, cgs, wg, wu, wd, gpost_row, final, after_tile=None):
            load_gp(gpost_row)
            ntl = (NJ + 3) // 4
            loaded = {}

            def load_gu(g):
                nj = min(4, NJ - g * 4)
                loaded[g] = (wload(wg, 0, 8, g * 512, nj * 128), wload(wu, 0, 8, g * 512, nj * 128), nj)

            load_gu(0)
            dts = []
            dspecs = [(hf, r0, kc) for hf in range(2) for (r0, kc) in ((0, 8), (8, 8), (16, 6))]
            for g in range(ntl):
                if g + 1 < ntl:
                    load_gu(g + 1)
                else:
                    for (hf, r0, kc) in dspecs[:NSLOT - 2]:
                        dts.append(wload(wd, r0, kc, hf * 512, 512))
                (gt, gk), (ut, uk), nj = loaded.pop(g)
                for jj in range(nj):
                    j = g * 4 + jj
                    bgs = [bank() for _ in cgs]
                    bus = [bank() for _ in cgs]
                    for (wt, wk, bks) in ((gt, gk, bgs), (ut, uk, bus)):
                        for k in range(8):
                            for ci, (c0, n, tl) in enumerate(cgs):
                                b = bks[ci]
                                c.op("pe", lambda e, k=k, b=b, c0=c0, n=n, wt=wt: e.matmul(
                                    pb[b][:, 0:n], lhsT=wt[:, k, jj * 128:(jj + 1) * 128], rhs=hT[:, k, c0:c0 + n],
                                    start=(k == 0), stop=(k == 7)),
                                    reads=[wk] + [("hT", t) for t in tl], writes=[PB(b)])
                    for ci, (c0, n, tl) in enumerate(cgs):
                        bg, bu = bgs[ci], bus[ci]
                        sg = rF.get()
                        c.op("act", lambda e, sg=sg, bg=bg, n=n: e.activation(out=sg.t[:, 0:n], in_=pb[bg][:, 0:n],
                                                                          func=AF.Silu),
                             reads=[PB(bg)], writes=[sg])
                        s0_ffn_keys.add(("aT", j, c0))
                        c.op("dve", lambda e, sg=sg, bu=bu, n=n, c0=c0, j=j: e.tensor_tensor(
                            out=aT[:, j, c0:c0 + n], in0=sg.t[:, 0:n], in1=pb[bu][:, 0:n], op=ALU.mult),
                            reads=[sg, PB(bu)], writes=[("aT", j, c0)])
            for (hf, r0, kc) in dspecs[NSLOT - 2:]:
                dts.append(wload(wd, r0, kc, hf * 512, 512))
            pend = [None]
            for (i, nrows, c0, row0) in tiles:
                bs = []
                cgc0 = [cc for (cc, n, tl) in cgs if cc <= c0 < cc + n][0]
                for hf in range(2):
                    b = bank()
                    bs.append(b)
                    for j in range(NJ):
                        dt_, dk_ = dts[hf * 3 + j // 8]
                        c.op("pe", lambda e, j=j, b=b, dt_=dt_: e.matmul(
                            pb[b][:nrows, :], lhsT=aT[:, j, c0:c0 + nrows], rhs=dt_[:, j % 8, :],
                            start=(j == 0), stop=(j == NJ - 1)),
                            reads=[dk_, ("aT", j, cgc0)], writes=[PB(b)])
                if pend[0] is not None:
                    pend[0]()
                    pend[0] = None
                post(i, nrows, bs[0], bs[1], 0.5, final_row0=(row0 if final else None))
                if after_tile is not None:
                    pend[0] = after_tile(i, nrows, c0)
            if pend[0] is not None:
                pend[0]()

        def rope(b, i, nrows, out_ap, out_key):
            q4 = pb[b][:nrows, :].rearrange("p (h two d) -> p h two d", h=4, two=2)
            t1 = rF.get()
            t1v = t1.t[:nrows, 0:512].rearrange("p (h two d) -> p h two d", h=4, two=2)
            cosb = cs_t[:nrows, i, 0, :].unsqueeze(1).unsqueeze(1).broadcast_to([nrows, 4, 2, 64])
            c.op("dve", lambda e: e.tensor_tensor(out=t1v, in0=q4, in1=cosb, op=ALU.mult),
                 reads=[PB(b), "cs_t"], writes=[t1])
            u = rF.get()
            uv = u.t[:nrows, 0:512].rearrange("p (h two d) -> p h two d", h=4, two=2)
            nsb = cs_t[:nrows, i, 2, :].unsqueeze(1).broadcast_to([nrows, 4, 64])
            sb_ = cs_t[:nrows, i, 1, :].unsqueeze(1).broadcast_to([nrows, 4, 64])
            c.op("dve", lambda e: e.tensor_tensor(out=uv[:, :, 0, :], in0=q4[:, :, 1, :], in1=nsb, op=ALU.mult),
                 reads=[PB(b), "cs_t"], writes=[u])
            c.op("dve", lambda e: e.tensor_tensor(out=uv[:, :, 1, :], in0=q4[:, :, 0, :], in1=sb_, op=ALU.mult),
                 reads=[PB(b), "cs_t", u], writes=[u])
            c.op("pool", lambda e: e.tensor_tensor(out=out_ap, in0=t1.t[:nrows, 0:512], in1=u.t[:nrows, 0:512],
                                                   op=ALU.add), reads=[t1, u], writes=[out_key])

        def ret_proj(i, nrows, c0, Ts):
            bs = []
            for (T, kT) in Ts:
                b = bank()
                bs.append(b)
                mm8(b, nrows, 512, lambda k: hT[:, k, c0:c0 + nrows], lambda k, T=T: T[:, k, :], [kT, ("hT", i)])
            return bs

        def onorm_tail(i, nrows, c0, on, gate, dstT, dkey, gtab, gkey, gbc, defer=False):
            if gbc:
                gi = gtab[:nrows, :].unsqueeze(1).broadcast_to([nrows, 4, 128])
                c.op("pool", lambda e: e.tensor_tensor(out=v4(on.t[:nrows, 0:512]), in0=v4(on.t[:nrows, 0:512]), in1=gi,
                                                       op=ALU.mult), reads=[on, gkey], writes=[on])
            else:
                c.op("pool", lambda e: e.tensor_tensor(out=on.t[:nrows, 0:512], in0=on.t[:nrows, 0:512],
                                                       in1=gtab[:nrows, :], op=ALU.mult), reads=[on, gkey], writes=[on])
            orn = rB.get()
            g_ap = s_gate[:nrows, :] if gate is None else gate.t[:nrows, :]
            g_k = "s_gate" if gate is None else gate
            c.op("pool", lambda e: e.tensor_tensor(out=orn.t[:nrows, :], in0=on.t[:nrows, 0:512], in1=g_ap,
                                                   op=ALU.mult), reads=[on, g_k], writes=[orn])
            s0_mix_keys.add((dkey, i))

            def partB():
                bt = bank()
                for h in range(4):
                    c.op("pe", lambda e, h=h: e.transpose(out=pbb[bt][:, h * 128:h * 128 + nrows], in_=orn.t[:nrows, hb(h)],
                                                          identity=cb[:nrows, IDF, :nrows]),
                         reads=[orn, "cb"], writes=[PB(bt)])
                src = pbb[bt][:, 0:512].rearrange("p (h t) -> p h t", h=4)[:, :, 0:nrows]
                c.op("act", lambda e: e.activation(out=dstT[:, :, c0:c0 + nrows], in_=src, func=AF.Copy),
                     reads=[PB(bt)], writes=[(dkey, i)])
            if defer:
                return partB
            partB()
            return None

        def groupnorm_ret(bo, nrows, src=None, skey=None):
            sm = rS.get()
            sm2 = rS.get()
            if src is None:
                src, skey = pb[bo][:nrows, :], PB(bo)
            c.op("dve", lambda e: e.tensor_reduce(out=sm.t[:nrows, 0:4], in_=v4(src), axis=AX.X, op=ALU.add),
                 reads=[skey], writes=[sm])
            sq = rF.get()
            c.op("act", lambda e: e.activation(out=sq.t[:nrows, 0:512], in_=src, func=AF.Square),
                 reads=[skey], writes=[sq])
            c.op("dve", lambda e: e.tensor_reduce(out=sm.t[:nrows, 4:8], in_=v4(sq.t[:nrows, 0:512]), axis=AX.X, op=ALU.add),
                 reads=[sq, sm], writes=[sm])
            c.op("dve", lambda e: e.tensor_scalar(out=sm.t[:nrows, 8:12], in0=sm.t[:nrows, 0:4], scalar1=1.0 / 128,
                                                  scalar2=None, op0=ALU.mult), reads=[sm], writes=[sm])
            c.op("dve", lambda e: e.tensor_tensor(out=sm.t[:nrows, 12:16], in0=sm.t[:nrows, 8:12], in1=sm.t[:nrows, 8:12],
                                                  op=ALU.mult), reads=[sm], writes=[sm])
            c.op("dve", lambda e: e.scalar_tensor_tensor(out=sm2.t[:nrows, 0:4], in0=sm.t[:nrows, 4:8], scalar=1.0 / 128,
                                                         op0=ALU.mult, in1=sm.t[:nrows, 12:16], op1=ALU.subtract),
                 reads=[sm], writes=[sm2])
            c.op("dve", lambda e: e.tensor_scalar(out=sm2.t[:nrows, 4:8], in0=sm2.t[:nrows, 0:4], scalar1=EPS, scalar2=None,
                                                  op0=ALU.add), reads=[sm2], writes=[sm2])
            c.op("pool", lambda e: e.tensor_tensor(out=sm2.t[:nrows, 8:12], in0=sm2.t[:nrows, 4:8], in1=nhalf[:nrows, 0:4],
                                                   op=ALU.pow), reads=[sm2, "nhalf"], writes=[sm2])
            on = rF.get()
            for h in range(4):
                c.op("dve", lambda e, h=h: e.tensor_scalar(out=on.t[:nrows, hb(h)], in0=src[:, hb(h)],
                                                           scalar1=sm.t[:nrows, 8 + h:9 + h], scalar2=sm2.t[:nrows, 8 + h:9 + h],
                                                           op0=ALU.subtract, op1=ALU.mult),
                     reads=[skey, sm, sm2, on] if h else [skey, sm, sm2], writes=[on])
            return on

        def rmsnorm_gdn(bo, nrows, src=None, skey=None):
            sq = rF.get()
            sm = rS.get()
            if src is None:
                src, skey = pb[bo][:nrows, :], PB(bo)
            c.op("act", lambda e: e.activation(out=sq.t[:nrows, 0:512], in_=src, func=AF.Square),
                 reads=[skey], writes=[sq])
            c.op("dve", lambda e: e.tensor_reduce(out=sm.t[:nrows, 0:4], in_=v4(sq.t[:nrows, 0:512]), axis=AX.X, op=ALU.add),
                 reads=[sq], writes=[sm])
            c.op("dve", lambda e: e.tensor_scalar(out=sm.t[:nrows, 4:8], in0=sm.t[:nrows, 0:4], scalar1=1.0 / 128,
                                                  scalar2=EPS, op0=ALU.mult, op1=ALU.add), reads=[sm], writes=[sm])
            c.op("pool", lambda e: e.tensor_tensor(out=sm.t[:nrows, 8:12], in0=sm.t[:nrows, 4:8], in1=nhalf[:nrows, 0:4],
                                                   op=ALU.pow), reads=[sm, "nhalf"], writes=[sm])
            on = rF.get()
            c.op("dve", lambda e: e.tensor_tensor(out=v4(on.t[:nrows, 0:512]), in0=v4(src),
                                                  in1=sm.t[:nrows, 8:12].unsqueeze(2).broadcast_to([nrows, 4, 128]),
                                                  op=ALU.mult), reads=[skey, sm], writes=[on])
            return on

        def phase_R(ptiles, with_sample, g1=None):
            Ts = [wload(w_in, 0, 8, cc, 512) for cc in (0, 512, 1024, 1536)]

            def pump(n, site="x"):
                if ("np_" + site) in SK:
                    return
                if g1 is not None:
                    for _ in range(n):
                        next(g1, None)
            pump(1)
            pendR = None
            for (i, nrows, c0, row0) in ptiles:
                bq, bk, bv, bg = ret_proj(i, 128, c0, Ts)
                if pendR is not None:
                    pendR()
                    pendR = None
                pump(1, "a")
                qr = rB.get()
                rope(bq, i, 128, qr.t[:, :], qr)
                kf = rF.get()
                rope(bk, i, 128, kf.t[:, 0:512], kf)
                krb = rB.get()
                c.op("act", lambda e: e.activation(out=krb.t[:, :], in_=kf.t[:, 0:512], func=AF.Copy), reads=[kf], writes=[krb])
                kd = rB.get()
                c.op("pool", lambda e: e.tensor_tensor(out=v4(kd.t[:, :]), in0=v4(kf.t[:, 0:512]),
                                                       in1=kdec[:, :].unsqueeze(2).broadcast_to([128, 4, 128]), op=ALU.mult),
                     reads=[kf, "kdec"], writes=[kd])
                vb = rB.get()
                c.op("act", lambda e: e.activation(out=vb.t[:, :], in_=pb[bv][:, :], func=AF.Copy), reads=[PB(bv)], writes=[vb])
                rgs = rB.get()
                c.op("act", lambda e: e.activation(out=rgs.t[:, :], in_=pb[bg][:, :], func=AF.Silu), reads=[PB(bg)], writes=[rgs])
                bt = bank()
                for h in range(4):
                    c.op("pe", lambda e, h=h: e.transpose(out=pbb[bt][:, hb(h)], in_=qr.t[:, hb(h)], identity=identb),
                         reads=[qr, "cb"], writes=[PB(bt)])
                for h in range(4):
                    c.op("pe", lambda e, h=h: e.transpose(out=pbb[bt][:, hb(4 + h)], in_=krb.t[:, hb(h)], identity=identb),
                         reads=[krb, "cb"], writes=[PB(bt)])
                qkT = rBB.get()
                c.op("act", lambda e: e.activation(out=qkT.t[:, :], in_=pbb[bt][:, 0:1024], func=AF.Copy),
                     reads=[PB(bt)], writes=[qkT])
                qg = rB.get()
                c.op("pool", lambda e: e.tensor_tensor(out=qg.t[:, :], in0=qkT.t[:, 0:512],
                                                       in1=rettab[:, 1, :, :].rearrange("p h i -> p (h i)"), op=ALU.mult),
                     reads=[qkT, "rettab"], writes=[qg])
                pump(5, "b")
                bsc = bank()
                for h in range(4):
                    c.op("pe", lambda e, h=h: e.matmul(pb[bsc][:, hb(h)], lhsT=qkT.t[:, hb(4 + h)], rhs=qkT.t[:, hb(h)],
                                                       start=True, stop=True), reads=[qkT], writes=[PB(bsc)])
                sT = rB.get()
                c.op("dve", lambda e: e.tensor_tensor(out=sT.t[:, :], in0=pb[bsc][:, :],
                                                      in1=rettab[:, 0, :, :].rearrange("p h i -> p (h i)"), op=ALU.mult),
                     reads=[PB(bsc), "rettab"], writes=[sT])
                bo = bank()
                for h in range(4):
                    c.op("pe", lambda e, h=h: e.matmul(pb[bo][:, hb(h)], lhsT=sT.t[:, hb(h)], rhs=vb.t[:, hb(h)],
                                                       start=True, stop=False), reads=[sT, vb], writes=[PB(bo)])
                    c.op("pe", lambda e, h=h: e.matmul(pb[bo][:, hb(h)], lhsT=qg.t[:, hb(h)], rhs=Sretb[:, h, :],
                                                       start=False, stop=True), reads=[qg, "Sretb"], writes=[PB(bo)])
                bS = bank()
                for h in range(4):
                    c.op("pe", lambda e, h=h: e.matmul(pb[bS][:, hb(h)], lhsT=kd.t[:, hb(h)], rhs=vb.t[:, hb(h)],
                                                       start=True, stop=True), reads=[kd, vb], writes=[PB(bS)])
                for h in range(4):
                    c.op("dve", lambda e, h=h: e.scalar_tensor_tensor(out=Sret[:, h, :], in0=Sret[:, h, :],
                                                                      scalar=float(GAMMA[h] ** 128), op0=ALU.mult,
                                                                      in1=pb[bS][:, hb(h)], op1=ALU.add),
                         reads=[PB(bS), "Sret"], writes=["Sret"])
                c.op("act", lambda e: e.activation(out=Sretb[:], in_=Sret[:], func=AF.Copy), reads=["Sret"], writes=["Sretb"])
                on = groupnorm_ret(bo, 128)
                pendR = onorm_tail(i, 128, c0, on, rgs, o_rT, "orT", rng_t, "rng", False, defer=True)
            if pendR is not None:
                pendR()
            if with_sample:
                c.region("Rs")
                sample_ret(Ts)
                c.region("R2")

        def build_masked(dst, dkey, srcT_fn, src_reads):
            di = cf[:, DI0:DI0 + 2, :].rearrange("p a (b m) -> p (a b) m", m=NS)
            for h in range(4):
                c.op("dve", lambda e, h=h: e.tensor_tensor(out=dst[:, h, :, :],
                                                           in0=srcT_fn(h).unsqueeze(1).broadcast_to([128, NS, NS]),
                                                           in1=di, op=ALU.mult),
                     reads=src_reads + ["cf"] + ([dkey] if h else []), writes=[dkey])

        def sample_state_update(h, tg, S0g, lhs_tok, ublk, a_scalar, a_bc, out_dram):
            bO = bank()
            c.op("pe", lambda e: e.matmul(pb[bO][:, :], lhsT=lhs_tok, rhs=ublk.t[:NS, :], start=True, stop=True),
                 reads=[ublk, "s_ks"], writes=[PB(bO)])
            if a_scalar is not None:
                c.op("dve", lambda e: e.scalar_tensor_tensor(out=S0g.t[:, 0:512], in0=S0g.t[:, 0:512], scalar=a_scalar,
                                                             op0=ALU.mult, in1=pb[bO][:, :], op1=ALU.add),
                     reads=[S0g, PB(bO)], writes=[S0g])
            else:
                c.op("dve", lambda e: e.tensor_tensor(out=v4(S0g.t[:, 0:512]), in0=v4(S0g.t[:, 0:512]), in1=a_bc,
                                                      op=ALU.mult), reads=[S0g, "s_abc"], writes=[S0g])
                c.op("dve", lambda e: e.tensor_tensor(out=S0g.t[:, 0:512], in0=S0g.t[:, 0:512], in1=pb[bO][:, :],
                                                      op=ALU.add), reads=[S0g, PB(bO)], writes=[S0g])
            c.dma("sp", lambda e: e.dma_start(out=out_dram[tg * 4:(tg + 1) * 4, h].rearrange("t d e -> d t e"),
                                              in_=v4(S0g.t[:, 0:512])), reads=[S0g], is_output=True)
            Snb = rB.get()
            c.op("act", lambda e: e.activation(out=Snb.t[:, :], in_=S0g.t[:, 0:512], func=AF.Copy), reads=[S0g], writes=[Snb])
            return Snb

        def make_ublk(u_ap, u_reads, tg):
            ub = rB.get()
            c.op("dve", lambda e: e.tensor_tensor(out=v4(ub.t[:NS, :]), in0=u_ap.unsqueeze(1).broadcast_to([NS, 4, 128]),
                                                  in1=cf[:NS, IDF, tg * 4:(tg + 1) * 4].unsqueeze(2).broadcast_to([NS, 4, 128]),
                                                  op=ALU.mult), reads=u_reads + ["cf"], writes=[ub])
            return ub

        def sample_ret(Ts):
            i, c0 = SI, SC0
            bq, bk, bv, bg = ret_proj(i, NS, c0, Ts)
            qr = rB.get()
            rope(bq, i, NS, qr.t[:NS, :], qr)
            kf = rF.get()
            rope(bk, i, NS, kf.t[:NS, 0:512], kf)
            c.op("act", lambda e: e.activation(out=s_ks[:NS, :], in_=kf.t[:NS, 0:512], func=AF.Copy, scale=float(DK ** -0.5)),
                 reads=[kf], writes=["s_ks"])
            c.op("act", lambda e: e.activation(out=s_vs[:NS, :], in_=pb[bv][:NS, :], func=AF.Copy), reads=[PB(bv)], writes=["s_vs"])
            c.op("act", lambda e: e.activation(out=s_gate[:NS, :], in_=pb[bg][:NS, :], func=AF.Silu), reads=[PB(bg)], writes=["s_gate"])
            bt = bank()
            for h in range(4):
                c.op("pe", lambda e, h=h: e.transpose(out=pbb[bt][:, h * NS:(h + 1) * NS], in_=qr.t[:NS, hb(h)],
                                                      identity=cb[:NS, IDF, :NS]), reads=[qr, "cb"], writes=[PB(bt)])
            qTs = rB.get()
            c.op("act", lambda e: e.activation(out=qTs.t[:, 0:4 * NS], in_=pbb[bt][:, 0:4 * NS], func=AF.Copy),
                 reads=[PB(bt)], writes=[qTs])
            build_masked(s_qm, "s_qm", lambda h: qTs.t[:, h * NS:(h + 1) * NS], [qTs])
            for h in range(4):
                snbs = []
                for tg in range(4):
                    S0g = rF.get()
                    c.dma("sp", lambda e, S0g=S0g, tg=tg: e.dma_start(
                        out=v4(S0g.t[:, 0:512]), in_=sret_in[tg * 4:(tg + 1) * 4, h].rearrange("t d e -> d t e")),
                        writes=[S0g])
                    ub = make_ublk(s_vs[:NS, hb(h)], ["s_vs"], tg)
                    snbs.append(sample_state_update(h, tg, S0g, s_ks[:NS, hb(h)], ub, float(GAMMA[h]), None, nsr_s))
                bQ = bank()
                for t in range(NS):
                    c.op("pe", lambda e, t=t: e.matmul(pb[bQ][:NS, 0:128], lhsT=s_qm[:, h, t, :],
                                                       rhs=snbs[t // 4].t[:, hb(t % 4)], start=(t == 0), stop=(t == NS - 1)),
                         reads=["s_qm", snbs[t // 4]], writes=[PB(bQ)])
                c.op("act", lambda e, h=h: e.activation(out=s_os[:NS, hb(h)], in_=pb[bQ][:NS, 0:128], func=AF.Copy),
                     reads=[PB(bQ), "s_os"], writes=["s_os"])
            on = groupnorm_ret(None, NS, s_os[:NS, :], "s_os")
            onorm_tail(i, NS, c0, on, None, o_rT, "orT", rng_t, "rng", False)

        def phase_G1(cgs_p, with_sample, last_block):
            g1w = [wload(w_in, 0, 8, 2048 + T * 512, 512) for T in range(3)]
            yield
            for T in range(3):
                Tt, kT = g1w[T]
                for cc in range(4):
                    s = T * 4 + cc
                    groups = [(c0, n, tl, False) for (c0, n, tl) in cgs_p]
                    if with_sample:
                        groups.append((SC0, NS, [SI], True))
                    for (c0, n, tl, is_s) in groups:
                        b = bank()
                        mm8(b, 128, n, lambda k: Tt[:, k, cc * 128:(cc + 1) * 128], lambda k: hT[:, k, c0:c0 + n],
                            [kT] + [("hT", t) for t in tl])
                        acc = rF.get()
                        if not is_s:
                            ub = rF.get()
                            c.op("pool", lambda e: e.tensor_copy(out=ub.t[:, 0:3], in_=car[:, s, :]), reads=["car"], writes=[ub])
                            c.op("act", lambda e: e.activation(out=ub.t[:, 3:3 + n], in_=pb[b][:, 0:n], func=AF.Copy),
                                 reads=[PB(b), ub], writes=[ub])
                            c.op("pool", lambda e: e.tensor_copy(out=car[:, s, :], in_=ub.t[:, n:n + 3]), reads=[ub, "car"],
                                 writes=["car"])
                            c.op("dve", lambda e: e.tensor_scalar(out=acc.t[:, 0:n], in0=ub.t[:, 3:3 + n], scalar1=convw[:, s, 3:4],
                                                                  scalar2=None, op0=ALU.mult), reads=[ub, "convw"], writes=[acc])
                            for tap in (2, 1, 0):
                                c.op("dve", lambda e, tap=tap: e.scalar_tensor_tensor(
                                    out=acc.t[:, 0:n], in0=ub.t[:, tap:tap + n], scalar=convw[:, s, tap:tap + 1], op0=ALU.mult,
                                    in1=acc.t[:, 0:n], op1=ALU.add), reads=[ub, "convw", acc], writes=[acc])
                        else:
                            us = rF.get()
                            c.op("pool", lambda e: e.memset(us.t[:, 0:128], 0.0), writes=[us])
                            c.op("act", lambda e: e.activation(out=us.t[:, 0:n], in_=pb[b][:, 0:n], func=AF.Copy),
                                 reads=[PB(b), us], writes=[us])
                            c.op("dve", lambda e: e.tensor_scalar(out=acc.t[:, 0:n], in0=us.t[:, 0:n], scalar1=convw[:, s, 3:4],
                                                                  scalar2=None, op0=ALU.mult), reads=[us, "convw"], writes=[acc])
                            for tap in (2, 1, 0):
                                c.op("dve", lambda e, tap=tap: e.scalar_tensor_tensor(
                                    out=acc.t[:, 0:n], in0=scb[:, s, tap, :], scalar=convw[:, s, tap:tap + 1], op0=ALU.mult,
                                    in1=acc.t[:, 0:n], op1=ALU.add), reads=["scb", "convw", acc], writes=[acc])
                            bt = bank()
                            c.op("pe", lambda e: e.transpose(out=pb[bt][:, 0:128], in_=us.t[:, 0:128], identity=cf[:, IDF, :]),
                                 reads=[us, "cf"], writes=[PB(bt)])
                            ut = rF.get()
                            c.op("act", lambda e: e.activation(out=ut.t[:NS, 0:128], in_=pb[bt][:NS, 0:128], func=AF.Copy),
                                 reads=[PB(bt)], writes=[ut])
                            c.dma("sp", lambda e: e.dma_start(out=nsc_s[:, 2, s * 128:(s + 1) * 128], in_=ut.t[:NS, 0:128]),
                                  reads=[ut], is_output=True)
                        ck = ("cqT", s, c0)
                        s0_mix_keys.add(ck)
                        if s >= 8:
                            c.op("act", lambda e: e.activation(out=cqT[:, s, c0:c0 + n], in_=acc.t[:, 0:n], func=AF.Silu),
                                 reads=[acc], writes=[ck])
                        else:
                            cs = rF.get()
                            c.op("act", lambda e: e.activation(out=cs.t[:, 0:n], in_=acc.t[:, 0:n], func=AF.Silu),
                                 reads=[acc], writes=[cs])
                            sqb = rB.get()
                            c.op("pool", lambda e: e.tensor_tensor(out=sqb.t[:, 0:n], in0=cs.t[:, 0:n], in1=cs.t[:, 0:n],
                                                                   op=ALU.mult), reads=[cs], writes=[sqb])
                            b2 = bank()
                            c.op("pe", lambda e: e.matmul(pb[b2][:, 0:n], lhsT=cb[:, ONES, :], rhs=sqb.t[:, 0:n],
                                                          start=True, stop=True), reads=[sqb, "cb"], writes=[PB(b2)])
                            rr = rF.get()
                            c.op("act", lambda e: e.activation(out=rr.t[:, 0:n], in_=pb[b2][:, 0:n], func=AF.Ln, bias=epst[:, 0:1]),
                                 reads=[PB(b2), "epst"], writes=[rr])
                            c.op("act", lambda e: e.activation(out=rr.t[:, 0:n], in_=rr.t[:, 0:n], func=AF.Exp, scale=-0.5),
                                 reads=[rr], writes=[rr])
                            sc_ = float(DK ** -0.5) if s < 4 else 1.0
                            c.op("dve", lambda e: e.scalar_tensor_tensor(out=cqT[:, s, c0:c0 + n], in0=cs.t[:, 0:n], scalar=sc_,
                                                                         op0=ALU.mult, in1=rr.t[:, 0:n], op1=ALU.mult),
                                 reads=[cs, rr], writes=[ck])
                        yield
            if last_block:
                for r in range(3):
                    cc_ = rF.get()
                    c.op("dve", lambda e, r=r: e.tensor_copy(out=cc_.t[:, 0:12], in_=car[:, :, r]), reads=["car"], writes=[cc_])
                    bt = bank()
                    c.op("pe", lambda e: e.transpose(out=pb[bt][:12, 0:128], in_=cc_.t[:, 0:12], identity=cf[:, IDF, :]),
                         reads=[cc_, "cf"], writes=[PB(bt)])
                    co = rF.get()
                    c.op("act", lambda e: e.activation(out=co.t[:12, 0:128], in_=pb[bt][:12, 0:128], func=AF.Copy),
                         reads=[PB(bt)], writes=[co])
                    c.dma("sp", lambda e, r=r: e.dma_start(out=nsc_p[r, :].rearrange("(s p) -> s p", p=128), in_=co.t[:12, 0:128]),
                          reads=[co], is_output=True)

        def gdn_scalars(i, nrows, c0):
            ba = bank()
            mm8(ba, nrows, 8, lambda k: hT[:, k, c0:c0 + nrows], lambda k: wsmall[:, k, :], ["wsmall", ("hT", i)])
            sA = rS.get()
            sB = rS.get()
            c.op("dve", lambda e: e.tensor_tensor(out=sA.t[:nrows, 0:4], in0=pb[ba][:nrows, 0:4], in1=dtb_t[:nrows, :],
                                                  op=ALU.add), reads=[PB(ba), "dtb"], writes=[sA])
            c.op("act", lambda e: e.activation(out=sA.t[:nrows, 4:8], in_=sA.t[:nrows, 0:4], func=AF.Abs), reads=[sA], writes=[sA])
            c.op("act", lambda e: e.activation(out=sA.t[:nrows, 8:12], in_=sA.t[:nrows, 4:8], func=AF.Exp, scale=-1.0),
                 reads=[sA], writes=[sA])
            c.op("dve", lambda e: e.tensor_scalar(out=sA.t[:nrows, 8:12], in0=sA.t[:nrows, 8:12], scalar1=1.0, scalar2=None,
                                                  op0=ALU.add), reads=[sA], writes=[sA])
            c.op("act", lambda e: e.activation(out=sA.t[:nrows, 8:12], in_=sA.t[:nrows, 8:12], func=AF.Ln),
                 reads=[sA], writes=[sA])
            c.op("dve", lambda e: e.tensor_scalar(out=sA.t[:nrows, 12:16], in0=sA.t[:nrows, 0:4], scalar1=0.0, scalar2=None,
                                                  op0=ALU.max), reads=[sA], writes=[sA])
            c.op("dve", lambda e: e.tensor_tensor(out=sB.t[:nrows, 0:4], in0=sA.t[:nrows, 12:16], in1=sA.t[:nrows, 8:12],
                                                  op=ALU.add), reads=[sA], writes=[sB])
            gk = ("gsc", i)
            c.op("dve", lambda e: e.tensor_tensor(out=gsc[:nrows, i, 0:4], in0=sB.t[:nrows, 0:4], in1=nega_t[:nrows, :],
                                                  op=ALU.mult), reads=[sB, "nega"], writes=[gk])
            c.op("act", lambda e: e.activation(out=sB.t[:nrows, 4:8], in_=pb[ba][:nrows, 4:8], func=AF.Exp, scale=-1.0),
                 reads=[PB(ba), sB], writes=[sB])
            c.op("dve", lambda e: e.tensor_scalar(out=sB.t[:nrows, 8:12], in0=sB.t[:nrows, 4:8], scalar1=1.0, scalar2=None,
                                                  op0=ALU.add), reads=[sB], writes=[sB])
            c.op("dve", lambda e: e.reciprocal(out=gsc[:nrows, i, 4:8], in_=sB.t[:nrows, 8:12]), reads=[sB, gk], writes=[gk])
            c.op("dve", lambda e: e.tensor_scalar(out=gsc[:nrows, i, 8:12], in0=gsc[:nrows, i, 4:8], scalar1=-1.0, scalar2=None,
                                                  op0=ALU.mult), reads=[gk], writes=[gk])
            return gk, ba

        def phase_G2(ptiles, with_sample):
            wsf = rF.get()
            c.dma("sp", lambda e: e.dma_start(out=wsf.t[:, 0:64].rearrange("p (c n) -> p c n", n=8),
                                              in_=w_in[:, 4096:4104].rearrange("(c p) n -> p c n", p=128)), writes=[wsf])
            c.op("dve", lambda e: e.tensor_copy(out=wsmall[:, :, :], in_=wsf.t[:, 0:64].rearrange("p (c n) -> p c n", n=8)),
                 reads=[wsf], writes=["wsmall"])
            T7, k7 = wload(w_in, 0, 8, 3584, 512)
            pendG = None
            for (i, nrows, c0, row0) in ptiles:
                gk, bG = gdn_scalars(i, 128, c0)
                if G2CUT <= 1:
                    continue
                for col, m in ((16, TRI), (20, SU), (24, ONES)):
                    c.op("pe", lambda e, col=col, m=m: e.matmul(pb[bG][:, col:col + 4], lhsT=cf[:, m, :], rhs=gsc[:, i, 0:4],
                                                                start=True, stop=True), reads=[gk, "cf"], writes=[PB(bG)])
                ex = rS.get()
                c.op("act", lambda e: e.activation(out=ex.t[:, 0:12], in_=pb[bG][:, 16:28], func=AF.Exp), reads=[PB(bG)], writes=[ex])
                gsu = rF.get()
                c.op("pool", lambda e: e.tensor_tensor(out=v4(gsu.t[:, 0:512]),
                                                       in0=cf[:, SU, :].unsqueeze(1).broadcast_to([128, 4, 128]),
                                                       in1=gsc[:, i, 0:4].unsqueeze(2).broadcast_to([128, 4, 128]), op=ALU.mult),
                     reads=[gk, "cf"], writes=[gsu])
                bD = bank()
                for h in range(4):
                    c.op("pe", lambda e, h=h: e.matmul(pb[bD][:, hb(h)], lhsT=gsu.t[:, hb(h)], rhs=cf[:, TRI, :],
                                                       start=True, stop=True), reads=[gsu, "cf"], writes=[PB(bD)])
                E = rF.get()
                c.op("act", lambda e: e.activation(out=E.t[:, 0:512], in_=pb[bD][:, :], func=AF.Exp), reads=[PB(bD)], writes=[E])
                EMS = rF.get()
                c.op("pool", lambda e: e.tensor_tensor(out=v4(EMS.t[:, 0:512]), in0=v4(E.t[:, 0:512]),
                                                       in1=cf[:, MASKS, :].unsqueeze(1).broadcast_to([128, 4, 128]), op=ALU.mult),
                     reads=[E, "cf"], writes=[EMS])
                c.op("pool", lambda e: e.tensor_tensor(out=v4(E.t[:, 0:512]), in0=v4(E.t[:, 0:512]),
                                                       in1=cf[:, TRI, :].unsqueeze(1).broadcast_to([128, 4, 128]), op=ALU.mult),
                     reads=[E, "cf"], writes=[E])
                kq_reads = [("cqT", s, cc) for s in range(8) for cc in [cg0 for cg0 in cq_cg0(c0)]]
                if G2CUT <= 2:
                    continue
                if pendG is not None:
                    pendG()
                    pendG = None
                bK = bank()
                for h in range(4):
                    c.op("pe", lambda e, h=h: e.matmul(pb[bK][:, hb(h)], lhsT=cqT[:, 4 + h, c0:c0 + 128], rhs=cqT[:, 4 + h, c0:c0 + 128],
                                                       start=True, stop=True), reads=kq_reads, writes=[PB(bK)])
                Y = rB.get()
                for h in range(4):
                    c.op("dve", lambda e, h=h: e.scalar_tensor_tensor(out=Y.t[:, hb(h)], in0=pb[bK][:, hb(h)],
                                                                      scalar=gsc[:, i, 8 + h:9 + h], op0=ALU.mult,
                                                                      in1=EMS.t[:, hb(h)], op1=ALU.mult),
                         reads=[PB(bK), gk, EMS] + ([Y] if h else []), writes=[Y])
                if G2CUT <= 3:
                    continue
                bX = bank()
                for h in range(4):
                    c.op("pe", lambda e, h=h: e.transpose(out=pbb[bX][:, hb(h)], in_=Y.t[:, hb(h)], identity=identb),
                         reads=[Y, "cb"], writes=[PB(bX)])
                X = rB.get()
                c.op("act", lambda e: e.activation(out=X.t[:, :], in_=pbb[bX][:, 0:512], func=AF.Copy), reads=[PB(bX)], writes=[X])
                PT = rB.get()
                c.op("pool", lambda e: e.tensor_tensor(out=v4(PT.t[:, :]), in0=v4(Y.t[:, :]),
                                                       in1=cb[:, IDF, :].unsqueeze(1).broadcast_to([128, 4, 128]), op=ALU.add),
                     reads=[Y, "cb"], writes=[PT])
                if G2CUT <= 4:
                    continue
                for step in range(6):
                    bXn = bank()
                    for h in range(4):
                        c.op("pe", lambda e, h=h, X=X, Y=Y: e.matmul(pb[bXn][:, hb(h)], lhsT=Y.t[:, hb(h)], rhs=X.t[:, hb(h)],
                                                                    start=True, stop=True), reads=[X, Y], writes=[PB(bXn)])
                    if step < 5:
                        bYn = bank()
                        for h in range(4):
                            c.op("pe", lambda e, h=h, X=X, Y=Y: e.matmul(pb[bYn][:, hb(h)], lhsT=X.t[:, hb(h)], rhs=Y.t[:, hb(h)],
                                                                        start=True, stop=True), reads=[X, Y], writes=[PB(bYn)])
                    Xn = rB.get()
                    c.op("act", lambda e, Xn=Xn: e.activation(out=Xn.t[:, :], in_=pb[bXn][:, :], func=AF.Copy),
                         reads=[PB(bXn)], writes=[Xn])
                    if step < 5:
                        Yn = rB.get()
                        c.op("dve", lambda e, Yn=Yn: e.tensor_copy(out=Yn.t[:, :], in_=pb[bYn][:, :]), reads=[PB(bYn)], writes=[Yn])
                    bP = bank()
                    for h in range(4):
                        c.op("pe", lambda e, h=h, Xn=Xn, PT=PT: e.matmul(pb[bP][:, hb(h)], lhsT=Xn.t[:, hb(h)], rhs=PT.t[:, hb(h)],
                                                                        start=True, stop=True), reads=[Xn, PT], writes=[PB(bP)])
                    PTn = rB.get()
                    c.op("dve", lambda e, PTn=PTn, PT=PT: e.tensor_tensor(out=PTn.t[:, :], in0=pb[bP][:, :], in1=PT.t[:, :],
                                                                         op=ALU.add), reads=[PB(bP), PT], writes=[PTn])
                    X, PT = Xn, PTn
                    if step < 5:
                        Y = Yn
                if G2CUT <= 5:
                    continue
                bQ = bank()
                for h in range(4):
                    c.op("pe", lambda e, h=h: e.matmul(pb[bQ][:, hb(h)], lhsT=cqT[:, 4 + h, c0:c0 + 128], rhs=cqT[:, h, c0:c0 + 128],
                                                       start=True, stop=True), reads=kq_reads, writes=[PB(bQ)])
                qkm = rB.get()
                c.op("dve", lambda e: e.tensor_tensor(out=qkm.t[:, :], in0=pb[bQ][:, :], in1=E.t[:, 0:512], op=ALU.mult),
                     reads=[PB(bQ), E], writes=[qkm])
                if 'suba' in SK:
                    continue
                bT = bank()
                kv_reads = [("cqT", s, cg0) for s in range(4, 12) for cg0 in cq_cg0(c0)]
                for h in range(4):
                    c.op("pe", lambda e, h=h: e.transpose(out=pbb[bT][:, hb(h)], in_=cqT[:, 4 + h, c0:c0 + 128], identity=identb),
                         reads=kv_reads + ["cb"], writes=[PB(bT)])
                for h in range(4):
                    c.op("pe", lambda e, h=h: e.transpose(out=pbb[bT][:, hb(4 + h)], in_=cqT[:, 8 + h, c0:c0 + 128], identity=identb),
                         reads=kv_reads + ["cb"], writes=[PB(bT)])
                if 'subb' in SK:
                    continue
                kg = rB.get()
                c.op("dve", lambda e: e.tensor_tensor(out=v4(kg.t[:, :]), in0=v4(pbb[bT][:, 0:512]),
                                                      in1=ex.t[:, 0:4].unsqueeze(2).broadcast_to([128, 4, 128]), op=ALU.mult),
                     reads=[PB(bT), ex], writes=[kg])
                if 'subc' in SK:
                    continue
                kd = rB.get()
                c.op("dve", lambda e: e.tensor_tensor(out=v4(kd.t[:, :]), in0=v4(pbb[bT][:, 0:512]),
                                                      in1=ex.t[:, 4:8].unsqueeze(2).broadcast_to([128, 4, 128]), op=ALU.mult),
                     reads=[PB(bT), ex], writes=[kd])
                if 'subd' in SK:
                    continue
                vt = rB.get()
                c.op("act", lambda e: e.activation(out=vt.t[:, :], in_=pbb[bT][:, 512:1024], func=AF.Copy), reads=[PB(bT)], writes=[vt])
                if G2CUT <= 6:
                    continue
                bW = bank()
                for h in range(4):
                    c.op("pe", lambda e, h=h: e.matmul(pb[bW][:, hb(h)], lhsT=kg.t[:, hb(h)], rhs=PT.t[:, hb(h)],
                                                       start=True, stop=True), reads=[kg, PT], writes=[PB(bW)])
                NW = rB.get()
                c.op("act", lambda e: e.activation(out=NW.t[:, :], in_=pb[bW][:, :], func=AF.Copy, scale=-1.0),
                     reads=[PB(bW)], writes=[NW])
                if G2CUT <= 7:
                    continue
                bU = bank()
                for h in range(4):
                    c.op("pe", lambda e, h=h: e.matmul(pb[bU][:, hb(h)], lhsT=PT.t[:, hb(h)], rhs=vt.t[:, hb(h)],
                                                       start=True, stop=False), reads=[PT, vt], writes=[PB(bU)])
                    c.op("pe", lambda e, h=h: e.matmul(pb[bU][:, hb(h)], lhsT=NW.t[:, hb(h)], rhs=Sgdnb[:, h, :],
                                                       start=False, stop=True), reads=[NW, "Sgdnb"], writes=[PB(bU)])
                U = rB.get()
                c.op("dve", lambda e: e.tensor_tensor(out=v4(U.t[:, :]), in0=v4(pb[bU][:, :]),
                                                      in1=gsc[:, i, 4:8].unsqueeze(2).broadcast_to([128, 4, 128]), op=ALU.mult),
                     reads=[PB(bU), gk], writes=[U])
                if G2CUT <= 8:
                    continue
                gbc = rF.get()
                c.op("pool", lambda e: e.tensor_copy(out=v4(gbc.t[:, 0:512]), in_=gsc[:, i, 0:4].unsqueeze(2).broadcast_to([128, 4, 128])),
                     reads=[gk], writes=[gbc])
                bR = bank()
                for h in range(4):
                    c.op("pe", lambda e, h=h: e.matmul(pb[bR][:, hb(h)], lhsT=gbc.t[:, hb(h)],
                                                       rhs=cf[:, TRI, :], start=True, stop=True), reads=[gbc, "cf"], writes=[PB(bR)])
                EG = rF.get()
                c.op("act", lambda e: e.activation(out=EG.t[:, 0:512], in_=pb[bR][:, :], func=AF.Exp), reads=[PB(bR)], writes=[EG])
                qg = rB.get()
                c.op("pool", lambda e: e.tensor_tensor(out=v4(qg.t[:, :]), in0=cqT[:, 0:4, c0:c0 + 128], in1=v4(EG.t[:, 0:512]),
                                                       op=ALU.mult), reads=kq_reads + [EG], writes=[qg])
                if G2CUT <= 9:
                    continue
                bO = bank()
                for h in range(4):
                    c.op("pe", lambda e, h=h: e.matmul(pb[bO][:, hb(h)], lhsT=qg.t[:, hb(h)], rhs=Sgdnb[:, h, :],
                                                       start=True, stop=False), reads=[qg, "Sgdnb"], writes=[PB(bO)])
                    c.op("pe", lambda e, h=h: e.matmul(pb[bO][:, hb(h)], lhsT=qkm.t[:, hb(h)], rhs=U.t[:, hb(h)],
                                                       start=False, stop=True), reads=[qkm, U], writes=[PB(bO)])
                if G2CUT <= 10:
                    continue
                bS = bank()
                for h in range(4):
                    c.op("pe", lambda e, h=h: e.matmul(pb[bS][:, hb(h)], lhsT=kd.t[:, hb(h)], rhs=U.t[:, hb(h)],
                                                       start=True, stop=True), reads=[kd, U], writes=[PB(bS)])
                for h in range(4):
                    c.op("dve", lambda e, h=h: e.scalar_tensor_tensor(out=Sgdn[:, h, :], in0=Sgdn[:, h, :], scalar=ex.t[:, 8 + h:9 + h],
                                                                      op0=ALU.mult, in1=pb[bS][:, hb(h)], op1=ALU.add),
                         reads=[PB(bS), "Sgdn", ex], writes=["Sgdn"])
                c.op("act", lambda e: e.activation(out=Sgdnb[:], in_=Sgdn[:], func=AF.Copy), reads=["Sgdn"], writes=["Sgdnb"])
                if G2CUT <= 11:
                    continue
                bz = bank()
                mm8(bz, 128, 512, lambda k: hT[:, k, c0:c0 + 128], lambda k: T7[:, k, :], [k7, ("hT", i)])
                gzs = rB.get()
                c.op("act", lambda e: e.activation(out=gzs.t[:, :], in_=pb[bz][:, :], func=AF.Silu), reads=[PB(bz)], writes=[gzs])
                on = rmsnorm_gdn(bO, 128)
                pendG = onorm_tail(i, 128, c0, on, gzs, o_gT, "ogT", gng_t, "gng", True, defer=True)
            if pendG is not None:
                pendG()
            if with_sample:
                sample_gdn(T7, k7)

        def cq_cg0(c0):
            return [0 if c0 < 512 else 512]

        not_sample_block = [False]

        def sample_gdn(T7, k7):
            i, c0 = SI, SC0
            gk, _ba = gdn_scalars(i, NS, c0)
            sa = rS.get()
            c.op("act", lambda e: e.activation(out=sa.t[:NS, 0:4], in_=gsc[:NS, i, 0:4], func=AF.Exp), reads=[gk], writes=[sa])
            ad = rF.get()
            c.op("pool", lambda e: e.memset(ad.t[:, 0:NS * 4], 0.0), writes=[ad])
            c.op("dve", lambda e: e.tensor_tensor(out=ad.t[:NS, 0:NS * 4].rearrange("p (t h) -> p t h", h=4),
                                                  in0=sa.t[:NS, 0:4].unsqueeze(1).broadcast_to([NS, NS, 4]),
                                                  in1=cf[:NS, IDF, :NS].unsqueeze(2).broadcast_to([NS, NS, 4]), op=ALU.mult),
                 reads=[sa, "cf", ad], writes=[ad])
            ba = bank()
            c.op("pe", lambda e: e.matmul(pb[ba][:, 0:NS * 4], lhsT=cf[:, ONES, :], rhs=ad.t[:, 0:NS * 4], start=True, stop=True),
                 reads=[ad, "cf"], writes=[PB(ba)])
            c.op("act", lambda e: e.activation(out=s_abc[:].rearrange("p t h -> p (t h)"), in_=pb[ba][:, 0:NS * 4], func=AF.Copy),
                 reads=[PB(ba)], writes=["s_abc"])
            cq_reads = [("cqT", s, SC0) for s in range(12)]
            bT = bank()
            for h in range(4):
                c.op("pe", lambda e, h=h: e.transpose(out=pbb[bT][:NS, hb(h)], in_=cqT[:, 4 + h, c0:c0 + NS], identity=identb),
                     reads=cq_reads + ["cb"], writes=[PB(bT)])
            bT2 = bank()
            for h in range(4):
                c.op("pe", lambda e, h=h: e.transpose(out=pbb[bT2][:NS, hb(h)], in_=cqT[:, 8 + h, c0:c0 + NS], identity=identb),
                     reads=cq_reads + ["cb"], writes=[PB(bT2)])
            c.op("act", lambda e: e.activation(out=s_ks[:NS, :], in_=pbb[bT][:NS, 0:512], func=AF.Copy), reads=[PB(bT)], writes=["s_ks"])
            c.op("act", lambda e: e.activation(out=s_vs[:NS, :], in_=pbb[bT2][:NS, 0:512], func=AF.Copy), reads=[PB(bT2)], writes=["s_vs"])
            build_masked(s_km, "s_km", lambda h: cqT[:, 4 + h, c0:c0 + NS], cq_reads)
            build_masked(s_qm, "s_qm", lambda h: cqT[:, h, c0:c0 + NS], cq_reads)
            for h in range(4):
                S0gs, S0bs = [], []
                for tg in range(4):
                    S0g = rF.get()
                    c.dma("sp", lambda e, S0g=S0g, tg=tg: e.dma_start(
                        out=v4(S0g.t[:, 0:512]), in_=sgdn_in[tg * 4:(tg + 1) * 4, h].rearrange("t d e -> d t e")),
                        writes=[S0g])
                    S0b = rB.get()
                    c.op("act", lambda e, S0b=S0b, S0g=S0g: e.activation(out=S0b.t[:, :], in_=S0g.t[:, 0:512], func=AF.Copy),
                         reads=[S0g], writes=[S0b])
                    S0gs.append(S0g)
                    S0bs.append(S0b)
                bK = bank()
                for t in range(NS):
                    c.op("pe", lambda e, t=t: e.matmul(pb[bK][:NS, 0:128], lhsT=s_km[:, h, t, :], rhs=S0bs[t // 4].t[:, hb(t % 4)],
                                                       start=(t == 0), stop=(t == NS - 1)), reads=["s_km", S0bs[t // 4]], writes=[PB(bK)])
                uu = rF.get()
                c.op("dve", lambda e: e.scalar_tensor_tensor(out=uu.t[:NS, 0:128], in0=pb[bK][:NS, 0:128], scalar=sa.t[:NS, h:h + 1],
                                                             op0=ALU.mult, in1=s_vs[:NS, hb(h)], op1=ALU.subtract),
                     reads=[PB(bK), sa, "s_vs"], writes=[uu])
                c.op("dve", lambda e: e.tensor_scalar(out=uu.t[:NS, 0:128], in0=uu.t[:NS, 0:128], scalar1=gsc[:NS, i, 8 + h:9 + h],
                                                      scalar2=None, op0=ALU.mult), reads=[uu, gk], writes=[uu])
                snbs = []
                for tg in range(4):
                    ub = make_ublk(uu.t[:NS, 0:128], [uu], tg)
                    a_bc = s_abc[:, tg * 4:(tg + 1) * 4, h].unsqueeze(2).broadcast_to([128, 4, 128])
                    snbs.append(sample_state_update(h, tg, S0gs[tg], s_ks[:NS, hb(h)], ub, None, a_bc, nsg_s))
                bQ = bank()
                for t in range(NS):
                    c.op("pe", lambda e, t=t: e.matmul(pb[bQ][:NS, 0:128], lhsT=s_qm[:, h, t, :], rhs=snbs[t // 4].t[:, hb(t % 4)],
                                                       start=(t == 0), stop=(t == NS - 1)), reads=["s_qm", snbs[t // 4]], writes=[PB(bQ)])
                c.op("act", lambda e, h=h: e.activation(out=s_os[:NS, hb(h)], in_=pb[bQ][:NS, 0:128], func=AF.Copy),
                     reads=[PB(bQ), "s_os"], writes=["s_os"])
            bz = bank()
            mm8(bz, NS, 512, lambda k: hT[:, k, c0:c0 + NS], lambda k: T7[:, k, :], [k7, ("hT", i)])
            c.op("act", lambda e: e.activation(out=s_gate[:NS, :], in_=pb[bz][:NS, :], func=AF.Silu), reads=[PB(bz)], writes=["s_gate"])
            on = rmsnorm_gdn(None, NS, s_os[:NS, :], "s_os")
            onorm_tail(i, NS, c0, on, None, o_gT, "ogT", gng_t, "gng", True)

        def phase_M(tiles, cgs, after_tile=None):
            load_gp(1)
            yk = []
            for half in range(2):
                Tgr = wload(w_in, 0, 8, 4104 + half * 512, 512)
                Tgg = wload(w_in, 0, 8, 5128 + half * 512, 512)
                Trb = wload(W["w_ret_branch"], 0, 4, half * 512, 512)
                Tgb = wload(W["w_gdn_branch"], 0, 4, half * 512, 512)
                for cc in range(4):
                    ch = half * 4 + cc
                    for (c0, n, tl) in cgs:
                        hk = [("hT", t) for t in tl]
                        b1, b2, b3, b4 = bank(), bank(), bank(), bank()
                        mm8(b1, 128, n, lambda k: Tgr[0][:, k, cc * 128:(cc + 1) * 128], lambda k: hT[:, k, c0:c0 + n], [Tgr[1]] + hk)
                        mm8(b2, 128, n, lambda k: Tgg[0][:, k, cc * 128:(cc + 1) * 128], lambda k: hT[:, k, c0:c0 + n], [Tgg[1]] + hk)
                        mm8(b3, 128, n, lambda k: Trb[0][:, k, cc * 128:(cc + 1) * 128], lambda k: o_rT[:, k, c0:c0 + n],
                            [Trb[1]] + [("orT", t) for t in tl], nk=4)
                        mm8(b4, 128, n, lambda k: Tgb[0][:, k, cc * 128:(cc + 1) * 128], lambda k: o_gT[:, k, c0:c0 + n],
                            [Tgb[1]] + [("ogT", t) for t in tl], nk=4)
                        s1 = rF.get()
                        c.op("act", lambda e: e.activation(out=s1.t[:, 0:n], in_=pb[b1][:, 0:n], func=AF.Sigmoid), reads=[PB(b1)], writes=[s1])
                        s2 = rF.get()
                        c.op("act", lambda e: e.activation(out=s2.t[:, 0:n], in_=pb[b2][:, 0:n], func=AF.Sigmoid), reads=[PB(b2)], writes=[s2])
                        c.op("dve", lambda e: e.tensor_tensor(out=s1.t[:, 0:n], in0=s1.t[:, 0:n], in1=pb[b3][:, 0:n], op=ALU.mult),
                             reads=[s1, PB(b3)], writes=[s1])
                        c.op("dve", lambda e: e.tensor_tensor(out=s2.t[:, 0:n], in0=s2.t[:, 0:n], in1=pb[b4][:, 0:n], op=ALU.mult),
                             reads=[s2, PB(b4)], writes=[s2])
                        ykey = ("yT", ch, c0)
                        s0_mix_keys.add(ykey)
                        c.op("pool", lambda e: e.tensor_tensor(out=yT[:, ch, c0:c0 + n], in0=s1.t[:, 0:n], in1=s2.t[:, 0:n], op=ALU.add),
                             reads=[s1, s2], writes=[ykey])
            Wo = [wload(W["w_out"], 0, 8, hf * 512, 512) for hf in range(2)]
            pend = [None]
            for (i, nrows, c0, row0) in tiles:
                cgc0 = [cc for (cc, n, tl) in cgs if cc <= c0 < cc + n][0]
                bs = []
                for hf in range(2):
                    b = bank()
                    bs.append(b)
                    mm8(b, nrows, 512, lambda k: yT[:, k, c0:c0 + nrows], lambda k: Wo[hf][0][:, k, :],
                        [Wo[hf][1]] + [("yT", k_, cgc0) for k_ in range(8)])
                if pend[0] is not None:
                    pend[0]()
                    pend[0] = None
                post(i, nrows, bs[0], bs[1], 1.0)
                if after_tile is not None:
                    pend[0] = after_tile(i, nrows, c0)
            if pend[0] is not None:
                pend[0]()

        def mk_tiles(g0, n, sample):
            t = [(i, 128, i * 128, (g0 + i) * 128) for i in range(n)]
            if sample:
                t.append((SI, NS, SC0, SEQ))
            return t

        blocks = [
            dict(g0=0, n=6, sample=True, cgs=[(0, 512, [0, 1, 2, 3]), (512, 256 + NS, [4, 5, SI])],
                 cgs_p=[(0, 512, [0, 1, 2, 3]), (512, 256, [4, 5])]),
            dict(g0=6, n=6, sample=False, cgs=[(0, 512, [0, 1, 2, 3]), (512, 256, [4, 5])],
                 cgs_p=[(0, 512, [0, 1, 2, 3]), (512, 256, [4, 5])]),
            dict(g0=12, n=4, sample=False, cgs=[(0, 512, [0, 1, 2, 3])], cgs_p=[(0, 512, [0, 1, 2, 3])]),
        ]
        if "oneblock" in SK:
            blocks = blocks[:1]
        for bi, blk in enumerate(blocks):
            tiles = mk_tiles(blk["g0"], blk["n"], blk["sample"])
            ptiles = [t for t in tiles if t[1] == 128]
            cgs = blk["cgs"]
            cgs_p = blk["cgs_p"]
            CQC = (0, 512, SC0)
            not_sample_block[0] = not blk["sample"]
            if bi == 0 or stage < 99:
                for (i, nrows, c0, row0) in tiles:
                    c.dma("sp", lambda e, i=i, nrows=nrows, row0=row0: e.dma_start(out=x1[:nrows, i, :],
                                                                              in_=xin[row0:row0 + nrows, :]),
                          writes=[("x1", i)])
            c.dma("sp", lambda e: e.dma_start(out=cs_t[:, 0:blk["n"]], in_=cs_in[:, blk["g0"]:blk["g0"] + blk["n"]]), writes=["cs_t"])
            if blk["sample"]:
                c.dma("sp", lambda e: e.dma_start(out=cs_t[:, SI], in_=cs_in[:, 16]), reads=["cs_t"], writes=["cs_t"])
            OVL = stage == 99
            c.region("ffn1")
            if "noffn1" not in SK:
                if bi == 0 or not OVL:
                    for (i, nrows, c0, row0) in tiles:
                        prenorm(i, nrows, c0, 0)
                if bi == 0:
                    wload_after[0] = [("x1", t_[0]) for t_ in tiles]
                ffn(tiles, cgs, W["ffn1_w_gate"], W["ffn1_w_up"], W["ffn1_w_down"], 0, final=(stage == 1),
                    after_tile=(lambda i, nrows, c0: prenormA(i, nrows, c0, 1)) if OVL else None)
                wload_after[0] = ()
                if bi == 0:
                    c.op("act", lambda e: e.activation(out=nega_t[:], in_=alog_t[:], func=AF.Exp),
                         reads=["alog", ("x1", 0)], writes=["nega"])
                    c.op("dve", lambda e: e.tensor_scalar(out=nega_t[:], in0=nega_t[:], scalar1=-1.0, scalar2=None, op0=ALU.mult),
                         reads=["nega"], writes=["nega"])
            if stage == 1:
                continue
            c.fence(list(s0_mix_keys) + [("orT", t[0]) for t in tiles] + [("ogT", t[0]) for t in tiles]
                    + [("cqT", s, cc_) for s in range(12) for cc_ in CQC], list(s0_ffn_keys))
            if not OVL:
                for (i, nrows, c0, row0) in tiles:
                    prenorm(i, nrows, c0, 1)
            if blk["sample"] and "sample" not in SK and "nsconv" not in SK:
                for r in range(3):
                    stg = [rF.get() for _ in range(3)]
                    for q_ in range(3):
                        c.dma("sp", lambda e, r=r, q_=q_: e.dma_start(out=stg[q_].t[:NS, 0:512], in_=sconv_in[:, r, q_ * 512:(q_ + 1) * 512]),
                              writes=[stg[q_]])
                    bt = bank()
                    for s in range(12):
                        c.op("pe", lambda e, s=s: e.transpose(out=pb[bt][:, s * NS:(s + 1) * NS],
                                                              in_=stg[s // 4].t[:NS, (s % 4) * 128:(s % 4 + 1) * 128],
                                                              identity=cf[:NS, IDF, :NS]), reads=[stg[s // 4], "cf"], writes=[PB(bt)])
                    c.op("act", lambda e, r=r: e.activation(out=scb[:, :, r, :], in_=pb[bt][:, 0:12 * NS].rearrange("p (s t) -> p s t", s=12),
                                                            func=AF.Copy), reads=[PB(bt), "scb"] if r else [PB(bt)], writes=["scb"])
                c.dma("sp", lambda e: e.dma_start(out=nsc_s[:, 0:2, :], in_=sconv_in[:, 1:3, :]), is_output=True)
            smp = blk["sample"] and "sample" not in SK
            c.region("R")
            g1 = phase_G1(cgs_p, smp and "nsg1" not in SK, bi == len(blocks) - 1) if "G1" not in SK else iter(())
            if "R" not in SK:
                phase_R(ptiles, smp and "nsret" not in SK, g1 if OVL else None)
            for _ in g1:
                pass
            c.region("G2")
            if "G2" not in SK:
                phase_G2(ptiles[:1] if "G2one" in SK else ptiles, smp and "nsgdn" not in SK)
            c.fence([("yT", ch, g[0]) for ch in range(8) for g in cgs], [("cqT", s, cc_) for s in range(12) for cc_ in CQC])
            c.region("M")
            if "M" not in SK:
                phase_M(tiles, cgs, after_tile=(lambda i, nrows, c0: prenormA(i, nrows, c0, 2)) if OVL else None)
            if bi == len(blocks) - 1:
                c.dma("sp", lambda e: e.dma_start(out=nsr_p.rearrange("h d e -> d h e"), in_=Sret[:]), reads=["Sret"], is_output=True)
                c.dma("sp", lambda e: e.dma_start(out=nsg_p.rearrange("h d e -> d h e"), in_=Sgdn[:]), reads=["Sgdn"], is_output=True)
            if stage == 2:
                for (i, nrows, c0, row0) in tiles:
                    c.dma("sp", lambda e, i=i, nrows=nrows, row0=row0: e.dma_start(out=y_out[row0:row0 + nrows, :], in_=x1[:nrows, i, :]),
                          reads=[("x1", i)], is_output=True)
                c.fence(list(s0_ffn_keys), list(s0_mix_keys))
                continue
            c.region("ffn2")
            c.fence(list(s0_ffn_keys), list(s0_mix_keys))
            if not OVL:
                for (i, nrows, c0, row0) in tiles:
                    prenorm(i, nrows, c0, 2)
            nxt = blocks[bi + 1] if bi + 1 < len(blocks) else None
            ntiles = {t[0]: t for t in mk_tiles(nxt["g0"], nxt["n"], nxt["sample"])} if nxt else {}

            def next_block_prefetch(i, nrows, c0):
                if i in ntiles:
                    (i2, nrows2, c02, row02) = ntiles[i]
                    c.dma("sp", lambda e: e.dma_start(out=x1[:nrows2, i2, :], in_=xin[row02:row02 + nrows2, :]),
                          writes=[("x1", i2)])
                    return prenormA(i2, nrows2, c02, 0)
                return None

            ffn(tiles, cgs, W["ffn2_w_gate"], W["ffn2_w_up"], W["ffn2_w_down"], 2, final=True,
                after_tile=next_block_prefetch if OVL else None)

        c.finish("sp")
        c.run_block()
    return nc


def host_consts():
    f32 = np.float32
    p = np.arange(128)
    cf = np.zeros((128, NCF, 128), f32)
    cf[:, 0, :] = np.eye(128)
    cf[:, 1, :] = (p[:, None] <= p[None, :])
    cf[:, 2, :] = (p[:, None] > p[None, :])
    cf[:, 3, :] = (p[:, None] < p[None, :])
    cf[:, 4, :] = 1.0
    cf[:, 5:7, :] = np.eye(NS, dtype=f32).reshape(1, 2, 128)
    lg = np.log(np.array(GAMMA, np.float64))
    rt = np.zeros((128, 2, 4, 128), np.float64)
    for h in range(4):
        dmt = np.exp((p[None, :] - p[:, None]) * lg[h]) * (p[:, None] <= p[None, :]) * DK ** -0.5
        rt[:, 0, h, :] = dmt
        rt[:, 1, h, :] = np.exp((p[None, :] + 1) * lg[h])
    kdec = np.exp((127 - p[:, None]) * lg[None, :]) * DK ** -0.5
    inv = (10000.0 ** (-(np.arange(0, 128, 2, dtype=f32)) / f32(128))).astype(f32)
    pos = np.concatenate([np.arange(SEQ, dtype=f32), np.full((128,), 16384.0, f32)])
    ang = (pos[:, None] * inv[None, :]).astype(f32)
    cs = np.stack([np.cos(ang), np.sin(ang), -np.sin(ang)], axis=1).astype(f32)
    cs = cs.reshape(17, 128, 3, 64).transpose(1, 0, 2, 3)
    return dict(constf=cf, rettab=rt.astype(f32), kdec=kdec.astype(f32), cossin=np.ascontiguousarray(cs))


_CACHE = {}


def kernel(**inp):
    f32 = np.float32
    stage = inp.pop("_stage", 99)
    debug = inp.pop("_debug", False)
    ncores = inp.pop("_cores", 8)
    key = (stage, debug, tuple(sorted(DBG_SKIP)))
    if key not in _CACHE:
        _CACHE[key] = build(stage, debug)
    nc = _CACHE[key]
    hc = host_consts()
    g = lambda n: np.asarray(inp[n], f32)[0]
    shared = {}
    for nm in ["ffn1_w_gate", "ffn1_w_up", "ffn1_w_down", "w_in", "w_ret_branch", "w_gdn_branch", "w_out",
               "ffn2_w_gate", "ffn2_w_up", "ffn2_w_down"]:
        shared[nm] = np.ascontiguousarray(g(nm))
    gpre = np.concatenate([g(n).reshape(8, 128).T for n in ("ffn1_pre_g", "mix_pre_g", "ffn2_pre_g")], axis=1)
    shared["gpre"] = np.ascontiguousarray(gpre)
    shared["gpost"] = np.ascontiguousarray(np.stack([g("ffn1_post_g"), g("mix_post_g"), g("ffn2_post_g")]))
    shared["ret_norm_g"] = g("ret_norm_g").reshape(1, 512)
    shared["gdn_norm_g"] = g("gdn_norm_g").reshape(1, 128)
    cw = g("gdn_conv_w")
    shared["convw"] = np.ascontiguousarray(cw.reshape(4, 12, 128).transpose(2, 1, 0))
    shared["a_log"] = g("gdn_a_log").reshape(1, 4)
    shared["dt_bias"] = g("gdn_dt_bias").reshape(1, 4)
    shared.update(hc)
    xp = np.asarray(inp["x_prompt"], f32)
    xs = np.asarray(inp["x_sample"], f32)
    sr = np.asarray(inp["state_ret"], f32)[0]
    sg = np.asarray(inp["state_gdn"], f32)[0]
    sc = np.asarray(inp["state_conv"], f32)[0]
    in_maps = []
    for b in range(ncores):
        m = dict(shared)
        m["xin"] = np.ascontiguousarray(np.concatenate([xp[b], xs[b * NS:(b + 1) * NS, 0, :]], axis=0))
        m["sret"] = np.ascontiguousarray(sr[b * NS:(b + 1) * NS])
        m["sgdn"] = np.ascontiguousarray(sg[b * NS:(b + 1) * NS])
        m["sconv"] = np.ascontiguousarray(sc[b * NS:(b + 1) * NS])
        in_maps.append(m)
    res = run_bass_kernel_spmd(nc, in_maps, core_ids=list(range(ncores)))
    R = list(res.results) + [res.results[0]] * (8 - ncores)
    yp = np.stack([R[b]["y"][:SEQ] for b in range(8)])
    ys = np.concatenate([R[b]["y"][SEQ:] for b in range(8)])[:, None, :]
    nrp = np.stack([R[b]["nsr_p"] for b in range(8)])[None]
    ngp = np.stack([R[b]["nsg_p"] for b in range(8)])[None]
    ncp = np.stack([R[b]["nsc_p"] for b in range(8)])[None]
    nrs = np.concatenate([R[b]["nsr_s"] for b in range(8)])[None]
    ngs = np.concatenate([R[b]["nsg_s"] for b in range(8)])[None]
    ncs = np.concatenate([R[b]["nsc_s"] for b in range(8)])[None]
    out = (yp, ys, nrp, ngp, ncp, nrs, ngs, ncs)
    if debug:
        return out, [R[b]["dbg"] for b in range(8)]
    return tuple(np.ascontiguousarray(o, dtype=f32) for o in out)
```
